# Optimizing a Trainium2 kernel written in Bass

```python
import math
import jax, jax.numpy as jnp
from jax import lax
import numpy as np

D_MODEL = 1024
BATCH = 8
SEQ = 4096
DEPTH = 2

CHUNK = 64
PLE_DIM = 256
D_MIX = D_MODEL
SSM_WIDTH = D_MIX // 2
SSM_GROUP = 16
SSM_GROUPS = SSM_WIDTH // SSM_GROUP
SSM_STATE = 64
ATTN_WIDTH = D_MIX - SSM_WIDTH
ATTN_HEADS = 8
HEAD_DIM = ATTN_WIDTH // ATTN_HEADS
Q_BLOCK = 128
RMS_EPS = 1e-6
DT_MIN = 1e-3
DT_MAX = 1e-1
IN_COLS = 2 * SSM_WIDTH + 4 * ATTN_WIDTH
SPLITS = [SSM_WIDTH, 2 * SSM_WIDTH, 2 * SSM_WIDTH + ATTN_WIDTH,
          2 * SSM_WIDTH + 2 * ATTN_WIDTH, 2 * SSM_WIDTH + 3 * ATTN_WIDTH]

kernel_name = "hymba_s5_stickbreaking_ple_block"


def rmsnorm(x, g):
    xf = x.astype(jnp.float32)
    xf = xf * lax.rsqrt(jnp.mean(xf * xf, axis=-1, keepdims=True) + RMS_EPS)
    return (xf * g.astype(jnp.float32)).astype(x.dtype)


def _complex_scan_combine(left, right):
    a1r, a1i, x1r, x1i = left
    a2r, a2i, x2r, x2i = right
    ar = a2r * a1r - a2i * a1i
    ai = a2r * a1i + a2i * a1r
    xr = a2r * x1r - a2i * x1i + x2r
    xi = a2r * x1i + a2i * x1r + x2i
    return ar, ai, xr, xi


def s5_branch(u, a_re, a_im, log_dt, b_re, b_im, c_re, c_im, d_skip, w_glu, b_glu):
    f32 = jnp.float32
    bsz, seq, _ = u.shape
    uf = u.astype(f32).reshape(bsz, seq, SSM_GROUPS, SSM_GROUP)
    dt = jnp.exp(log_dt.astype(f32))[:, None]
    lr = a_re.astype(f32)
    li = a_im.astype(f32)
    mag = jnp.exp(lr * dt)
    ab_re = mag * jnp.cos(li * dt)
    ab_im = mag * jnp.sin(li * dt)
    num_re = ab_re - 1.0
    num_im = ab_im
    den = lr * lr + li * li
    f_re = (num_re * lr + num_im * li) / den
    f_im = (num_im * lr - num_re * li) / den
    br = b_re.astype(f32)
    bi = b_im.astype(f32)
    bb_re = f_re[..., None] * br - f_im[..., None] * bi
    bb_im = f_re[..., None] * bi + f_im[..., None] * br
    cr = c_re.astype(f32)
    ci = c_im.astype(f32)

    n_chunks = seq // CHUNK
    u_chunks = uf.reshape(bsz, n_chunks, CHUNK, SSM_GROUPS, SSM_GROUP).transpose(1, 2, 0, 3, 4)
    a_re_l = jnp.broadcast_to(ab_re[None, None], (CHUNK, 1, SSM_GROUPS, SSM_STATE))
    a_im_l = jnp.broadcast_to(ab_im[None, None], (CHUNK, 1, SSM_GROUPS, SSM_STATE))

    def chunk_step(carry, u_c):
        h_re, h_im = carry
        bu_re = jnp.einsum('lbgh,gph->lbgp', u_c, bb_re)
        bu_im = jnp.einsum('lbgh,gph->lbgp', u_c, bb_im)
        p_re, p_im, x_re, x_im = lax.associative_scan(
            _complex_scan_combine, (a_re_l, a_im_l, bu_re, bu_im), axis=0)
        x_re = x_re + p_re * h_re - p_im * h_im
        x_im = x_im + p_re * h_im + p_im * h_re
        y_c = jnp.einsum('lbgp,ghp->lbgh', x_re, cr) - jnp.einsum('lbgp,ghp->lbgh', x_im, ci)
        return (x_re[-1], x_im[-1]), y_c

    h0 = jnp.zeros((bsz, SSM_GROUPS, SSM_STATE), f32)
    _, y = lax.scan(chunk_step, (h0, h0), u_chunks)
    y = y.transpose(2, 0, 1, 3, 4).reshape(bsz, seq, SSM_GROUPS, SSM_GROUP)
    y = (y + d_skip.astype(f32) * uf).reshape(bsz, seq, SSM_WIDTH)
    z = jax.nn.gelu(y)
    zz = z @ w_glu.astype(f32) + b_glu.astype(f32)
    val, gate = jnp.split(zz, 2, axis=-1)
    return (val * jax.nn.sigmoid(gate)).astype(u.dtype)


def stick_breaking_branch(q, k, v, q_g, k_g):
    f32 = jnp.float32
    bsz, seq, _ = q.shape

    def heads(t):
        return t.reshape(bsz, seq, ATTN_HEADS, HEAD_DIM).transpose(0, 2, 1, 3)

    qh = rmsnorm(heads(q), q_g).astype(f32)
    kh = rmsnorm(heads(k), k_g).astype(f32)
    vh = heads(v).astype(f32)
    scale = HEAD_DIM ** -0.5
    outs = []
    for blk in range(seq // Q_BLOCK):
        q0 = blk * Q_BLOCK
        kv_len = q0 + Q_BLOCK
        qb = qh[:, :, q0:kv_len]
        kb = kh[:, :, :kv_len]
        vb = vh[:, :, :kv_len]
        z = jnp.einsum('bhqd,bhkd->bhqk', qb, kb) * scale
        t_pos = q0 + jnp.arange(Q_BLOCK)[:, None]
        s_pos = jnp.arange(kv_len)[None, :]
        strict = s_pos < t_pos
        log_stay = jnp.where(strict, jax.nn.log_sigmoid(-z), 0.0)
        later = lax.cumsum(log_stay, axis=3, reverse=True) - log_stay
        weights = jnp.where(strict, jnp.exp(jax.nn.log_sigmoid(z) + later), 0.0)
        outs.append(jnp.einsum('bhqk,bhkd->bhqd', weights, vb))
    o = jnp.concatenate(outs, axis=2)
    return o.transpose(0, 2, 1, 3).reshape(bsz, seq, ATTN_WIDTH).astype(q.dtype)


def setup_inputs(seed: int = 0) -> dict:
    key = jax.random.key(seed)
    ks = jax.random.split(key, 20)
    f32 = jnp.float32
    G, P, H = SSM_GROUPS, SSM_STATE, SSM_GROUP
    n_idx = jnp.arange(P, dtype=f32)
    inputs = {
        "x": jax.random.normal(ks[0], (BATCH, SEQ, D_MODEL), f32),
        "p": jax.random.normal(ks[1], (DEPTH, BATCH, SEQ, PLE_DIM), f32),
        "mix_norm_g": 1.0 + 0.02 * jax.random.normal(ks[2], (DEPTH, D_MODEL), f32),
        "w_in": jax.random.normal(ks[3], (DEPTH, D_MODEL, IN_COLS), f32) * D_MODEL ** -0.5,
        "ssm_a_re": -0.5 + 0.01 * jax.random.normal(ks[4], (DEPTH, G, P), f32),
        "ssm_a_im": math.pi * n_idx + 0.01 * jax.random.normal(ks[5], (DEPTH, G, P), f32),
        "ssm_log_dt": jax.random.uniform(ks[6], (DEPTH, G), f32, math.log(DT_MIN), math.log(DT_MAX)),
        "ssm_b_re": jax.random.normal(ks[7], (DEPTH, G, P, H), f32) * (2.0 * H) ** -0.5,
        "ssm_b_im": jax.random.normal(ks[8], (DEPTH, G, P, H), f32) * (2.0 * H) ** -0.5,
        "ssm_c_re": jax.random.normal(ks[9], (DEPTH, G, H, P), f32) * (2.0 * P) ** -0.5,
        "ssm_c_im": jax.random.normal(ks[10], (DEPTH, G, H, P), f32) * (2.0 * P) ** -0.5,
        "ssm_d": jax.random.normal(ks[11], (DEPTH, G, H), f32),
        "ssm_w_glu": jax.random.normal(ks[12], (DEPTH, SSM_WIDTH, 2 * SSM_WIDTH), f32) * SSM_WIDTH ** -0.5,
        "ssm_b_glu": 0.01 * jax.random.normal(ks[13], (DEPTH, 2 * SSM_WIDTH), f32),
        "q_norm_g": 1.0 + 0.02 * jax.random.normal(ks[14], (DEPTH, HEAD_DIM), f32),
        "k_norm_g": 1.0 + 0.02 * jax.random.normal(ks[15], (DEPTH, HEAD_DIM), f32),
        "w_out": jax.random.normal(ks[16], (DEPTH, D_MIX, D_MODEL), f32) * D_MIX ** -0.5,
        "ple_norm_g": 1.0 + 0.02 * jax.random.normal(ks[17], (DEPTH, D_MODEL), f32),
        "w_ple_gate": jax.random.normal(ks[18], (DEPTH, D_MODEL, D_MODEL), f32) * D_MODEL ** -0.5,
        "w_ple_proj": jax.random.normal(ks[19], (DEPTH, PLE_DIM, D_MODEL), f32) * PLE_DIM ** -0.5,
    }
    return inputs


def reference(x, p, mix_norm_g, w_in, ssm_a_re, ssm_a_im, ssm_log_dt, ssm_b_re, ssm_b_im,
              ssm_c_re, ssm_c_im, ssm_d, ssm_w_glu, ssm_b_glu, q_norm_g, k_norm_g, w_out,
              ple_norm_g, w_ple_gate, w_ple_proj):
    h = x
    for i in range(DEPTH):
        hn = rmsnorm(h, mix_norm_g[i])
        proj = hn @ w_in[i]
        u, g_ssm, q, k, v, g_attn = jnp.split(proj, SPLITS, axis=-1)
        y_ssm = s5_branch(u, ssm_a_re[i], ssm_a_im[i], ssm_log_dt[i], ssm_b_re[i], ssm_b_im[i],
                          ssm_c_re[i], ssm_c_im[i], ssm_d[i], ssm_w_glu[i], ssm_b_glu[i])
        y_ssm = y_ssm * jax.nn.silu(g_ssm)
        y_att = stick_breaking_branch(q, k, v, q_norm_g[i], k_norm_g[i]) * jax.nn.silu(g_attn)
        h = h + jnp.concatenate([y_ssm, y_att], axis=-1) @ w_out[i]
        ple_gate = jax.nn.sigmoid(rmsnorm(h, ple_norm_g[i]) @ w_ple_gate[i])
        h = h + ple_gate * (p[i] @ w_ple_proj[i])
    return h
```

```python
import math
from contextlib import ExitStack
import numpy as np
import concourse.bass as bass
import concourse.mybir as mybir
from concourse.bass_utils import run_bass_kernel_spmd

F32 = mybir.dt.float32
BF16 = mybir.dt.bfloat16
AF = mybir.ActivationFunctionType
ALU = mybir.AluOpType

S = 4096
D = 1024
TT = 512
NT = S // TT
NSUB = TT // 128
EPS = 1e-6
NPIECE = 14
NCORES = 8


class _Eng:
    def __init__(self, name, handle, sem):
        self.name = name
        self.h = handle
        self.sem = sem
        self.count = 0
        self.waited = {}


class KB:
    def __init__(self, nc, stack):
        self.nc = nc
        self.stack = stack
        self.eng = {}
        for name, h in (("pe", nc.tensor), ("act", nc.scalar), ("dve", nc.vector),
                        ("pool", nc.gpsimd), ("sp", nc.sync)):
            sem = stack.enter_context(nc.semaphore("s_" + name))
            self.eng[name] = _Eng(name, h, sem)
        self.state = {}
        self.dsem = {}
        self.nwaits = 0
        self.ninst = 0

    def sb(self, name, shape, dt, stack=None):
        self._uid = getattr(self, "_uid", 0) + 1
        return (stack or self.stack).enter_context(self.nc.sbuf_tensor(f"{name}_{self._uid}", list(shape), dt))

    def ps(self, name, shape, dt=F32):
        return self.stack.enter_context(self.nc.psum_tensor(name, list(shape), dt))

    def dma_sem(self, name):
        if name not in self.dsem:
            sem = self.stack.enter_context(self.nc.semaphore("d_" + name))
            self.dsem[name] = [sem, 0]
        return self.dsem[name]

    def _st(self, k):
        s = self.state.get(k)
        if s is None:
            s = [None, []]
            self.state[k] = s
        return s

    def _wait(self, e, sem, val):
        key = id(sem)
        if e.waited.get(key, 0) >= val:
            return
        e.h.wait_ge(sem, val)
        e.waited[key] = val
        self.nwaits += 1

    def _deps(self, e, reads, writes):
        for k in reads:
            w = self._st(k)[0]
            if w is not None:
                self._wait(e, w[0], w[1])
        pe = e.name == "pe"
        for k in writes:
            s = self._st(k)
            if s[0] is not None and not (pe and s[0][0] is e.sem):
                self._wait(e, s[0][0], s[0][1])
            for (sem, val) in s[1]:
                if not (pe and sem is e.sem):
                    self._wait(e, sem, val)

    def _commit(self, tag, reads, writes):
        for k in reads:
            r = self._st(k)[1]
            for idx, (sem, val) in enumerate(r):
                if sem is tag[0]:
                    r[idx] = tag if tag[1] > val else (sem, val)
                    break
            else:
                r.append(tag)
        for k in writes:
            s = self._st(k)
            s[0] = tag
            s[1] = []

    def op(self, en, fn, reads=(), writes=(), sig=True):
        e = self.eng[en]
        self._deps(e, reads, writes)
        ins = fn(e.h)
        self.ninst += 1
        if sig:
            e.count += 1
            ins.then_inc(e.sem, 1)
            tag = (e.sem, e.count)
        else:
            tag = (e.sem, e.count + 1)
        self._commit(tag, reads, writes)
        return ins

    def dma(self, qn, out, in_, semname, reads=(), writes=(), **kw):
        e = self.eng[qn]
        ds = self.dma_sem(semname)
        if ds[1] > 0:
            self._wait(e, ds[0], ds[1])
        self._deps(e, reads, writes)
        ins = e.h.dma_start(out=out, in_=in_, **kw)
        ds[1] += 16
        ins.then_inc(ds[0], 16)
        self.ninst += 1
        self._commit((ds[0], ds[1]), reads, writes)
        return ins

    def barrier(self):
        snap_e = [(o.sem, o.count) for o in self.eng.values() if o.count]
        snap_d = [(sem, cnt) for (sem, cnt) in self.dsem.values() if cnt]
        for e in self.eng.values():
            for sem, cnt in snap_e:
                if sem is not e.sem:
                    self._wait(e, sem, cnt)
            for sem, cnt in snap_d:
                self._wait(e, sem, cnt)
        self.state = {}

    def finish(self, en="sp"):
        e = self.eng[en]
        for name, (sem, cnt) in self.dsem.items():
            if cnt:
                self._wait(e, sem, cnt)
        for o in self.eng.values():
            if o.count and o is not e:
                self._wait(e, o.sem, o.count)


def build_program(stop_at=None, dbg=False):
    nc = bass.Bass("TRN2", target_bir_lowering=False)

    def din(name, shape):
        return nc.dram_tensor(name, list(shape), F32, kind="ExternalInput").ap()

    x_d = din("x", [S, D])
    p_d = din("p", [2, S, 256])
    w_in_d = din("w_in", [2, D, 3584])
    w_glu_d = din("w_glu", [2, D, 1024])
    w_out_d = din("w_out", [2, D, D])
    w_pg_d = din("w_pg", [2, D, D])
    w_pp_d = din("w_pp", [2, 256, D])
    gmix_d = din("gmix", [2, 128, 8])
    gple_d = din("gple", [2, 128, 8])
    bglu_d = din("bglu", [2, 128, 8])
    dpad_d = din("dpad", [2, 128, 8])
    gq_d = din("gq", [2, 128, 1])
    gk_d = din("gk", [2, 128, 1])
    are_d = din("a_re", [2, 32, 64])
    aim_d = din("a_im", [2, 32, 64])
    ldt_d = din("logdt", [2, 32, 1])
    bre_d = din("b_re", [2, 32, 64, 16])
    bim_d = din("b_im", [2, 32, 64, 16])
    ccat_d = din("ccat", [2, 4, 128, 128])
    ccsw_d = din("ccatsw", [2, 4, 128, 128])
    out_d = nc.dram_tensor("out", [S, D], F32, kind="ExternalOutput").ap()
    wbf_d = nc.dram_tensor("wbf", [2, NPIECE, 128, 4096], BF16, kind="Internal").ap()
    yss_d = nc.dram_tensor("yss", [2, 4, 128, S], BF16, kind="Internal").ap()
    hmid_d = nc.dram_tensor("hmid", [S, D], F32, kind="Internal").ap()
    dbg_d = {}

    def dbg_out(name, shape):
        dbg_d[name] = nc.dram_tensor(name, list(shape), F32, kind="ExternalOutput").ap()
        return dbg_d[name]

    with ExitStack() as st:
        kb = KB(nc, st)
        op = kb.op

        pb = [kb.ps(f"pb{i}", [128, 512], F32) for i in range(8)]
        pbk = [f"pb{i}" for i in range(8)]
        identb = kb.sb("identb", [128, 128], BF16)
        identf = kb.sb("identf", [128, 128], F32)
        tri = kb.sb("tri", [128, 128], BF16)
        ones = kb.sb("ones", [128, 128], BF16)
        masklt = kb.sb("masklt", [128, 128], BF16)
        blockones = kb.sb("blockones", [128, 128], BF16)
        bdmask = kb.sb("bdmask", [128, 128], F32)
        NRING = 3
        ring = [kb.sb(f"ring{i}", [128, 4096], BF16) for i in range(NRING)]
        stage = kb.sb("stage", [128, 4096], F32)

        def mk_affine(t, key, pattern, cm, cmp_):
            op("pool", lambda e: e.memset(t[:], 1.0), writes=[key])
            op("pool", lambda e: e.affine_select(out=t[:], in_=t[:], pattern=pattern, compare_op=cmp_, fill=0.0,
                                                 base=0, channel_multiplier=cm), reads=[key], writes=[key])

        mk_affine(identb, "identb", [[-1, 128]], 1, ALU.is_equal)
        mk_affine(identf, "identf", [[-1, 128]], 1, ALU.is_equal)
        mk_affine(tri, "tri", [[-1, 128]], 1, ALU.is_ge)
        mk_affine(masklt, "masklt", [[1, 128]], -1, ALU.is_gt)
        op("pool", lambda e: e.memset(ones[:], 1.0), writes=["ones"])
        op("pool", lambda e: e.memset(blockones[:], 0.0), writes=["blockones"])
        for hh in range(2):
            op("pool", lambda e, hh=hh: e.memset(blockones[64 * hh:64 * hh + 64, 64 * hh:64 * hh + 64], 1.0),
               writes=["blockones"])
        op("pool", lambda e: e.memset(bdmask[:], 0.0), writes=["bdmask"])
        for gg in range(4):
            op("pool", lambda e, gg=gg: e.memset(bdmask[32 * gg:32 * gg + 32, 32 * gg:32 * gg + 32], 1.0),
               writes=["bdmask"])

        def piece_src(L, pid):
            if pid <= 6:
                return w_in_d[L, :, pid * 512:(pid + 1) * 512].rearrange("(k p) n -> p k n", p=128), 8, 512
            if pid <= 8:
                return w_glu_d[L, :, (pid - 7) * 512:(pid - 6) * 512].rearrange("(k p) n -> p k n", p=128), 8, 512
            if pid <= 10:
                return w_out_d[L, :, (pid - 9) * 512:(pid - 8) * 512].rearrange("(k p) n -> p k n", p=128), 8, 512
            if pid <= 12:
                return w_pg_d[L, :, (pid - 11) * 512:(pid - 10) * 512].rearrange("(k p) n -> p k n", p=128), 8, 512
            return w_pp_d[L].rearrange("(k p) n -> p k n", p=128), 2, 1024

        converted = set()
        ring_ctr = [0]

        class Pref:
            def __init__(self, L, pids):
                self.L = L
                self.pids = pids
                self.loaded = []

            def _load(self, idx):
                L, pid = self.L, self.pids[idx]
                s = ring_ctr[0] % NRING
                ring_ctr[0] += 1
                key = f"ring{s}"
                src, nk, ncol = piece_src(L, pid)
                n = nk * ncol
                v3 = ring[s][:, 0:n].rearrange("p (k n) -> p k n", k=nk)
                if (L, pid) not in converted:
                    sv = stage[:, 0:n].rearrange("p (k n) -> p k n", k=nk)
                    kb.dma("sp", sv, src, "ld_stage", writes=["stage"])
                    op("pool", lambda e: e.tensor_copy(out=ring[s][:, 0:n], in_=stage[:, 0:n]),
                       reads=["stage"], writes=[key])
                    kb.dma("sp", wbf_d[L, pid, :, 0:n], ring[s][:, 0:n], f"st_ring{s}",
                           reads=[key], writes=[f"wbf{L}_{pid}"])
                    converted.add((L, pid))
                else:
                    kb.dma("sp", ring[s][:, 0:n], wbf_d[L, pid, :, 0:n], f"ld_ring{s}",
                           reads=[f"wbf{L}_{pid}"], writes=[key])
                self.loaded.append((v3, key))

            def get(self, idx, ahead=2):
                while len(self.loaded) <= min(idx + ahead, len(self.pids) - 1):
                    self._load(len(self.loaded))
                return self.loaded[idx]

        def norm_phase(stk_bufs, src_rows, gain, tagp):
            hsub, hnt, ss, lnv, rstd, sqj, hnT = stk_bufs
            for i in range(NSUB):
                hs = hsub[i % 2]
                hk = f"hsub{i % 2}"
                kb.dma("sp", hs[:], src_rows(i), f"ld_hsub{i % 2}", writes=[hk])
                op("act", lambda e: e.activation(out=sqj[:], in_=hs[:], func=AF.Square, accum_out=ss[:, i:i + 1]),
                   reads=[hk], writes=["sqj", f"ss{i}"])
                op("act", lambda e: e.activation(out=lnv[:, i:i + 1], in_=ss[:, i:i + 1], func=AF.Ln,
                                                 scale=1.0 / D, bias=EPS), reads=[f"ss{i}"], writes=[f"lnv{i}"])
                op("act", lambda e: e.activation(out=rstd[:, i:i + 1], in_=lnv[:, i:i + 1], func=AF.Exp, scale=-0.5),
                   reads=[f"lnv{i}"], writes=[f"rstd{i}"])
                ht = hnt[i % 2]
                tk = f"hnt{i % 2}"
                op("pool", lambda e: e.tensor_scalar(out=ht[:], in0=hs[:], scalar1=rstd[:, i:i + 1], scalar2=None,
                                                     op0=ALU.mult), reads=[hk, f"rstd{i}"], writes=[tk])
                bk = i % 2
                pv = pb[bk][:].bitcast(BF16)
                for kc in range(8):
                    op("pe", lambda e, kc=kc: e.transpose(out=pv[:, kc * 128:(kc + 1) * 128],
                                                          in_=ht[:, kc * 128:(kc + 1) * 128], identity=identb[:]),
                       reads=[tk, "identb"], writes=[pbk[bk]], sig=(kc == 7))
                op("dve", lambda e: e.tensor_tensor(out=hnT[:, :, i * 128:(i + 1) * 128],
                                                    in0=pv.rearrange("p (k t) -> p k t", k=8),
                                                    in1=gain[:, :].unsqueeze(2).to_broadcast([128, 8, 128]),
                                                    op=ALU.mult),
                   reads=[pbk[bk], tagp], writes=["hnT"])

        def proj_fm(w3, wkey, col0, hnT, bank):
            for kc in range(8):
                op("pe", lambda e, kc=kc: e.matmul(pb[bank][:], lhsT=w3[:, kc, col0:col0 + 128], rhs=hnT[:, kc, :],
                                                   start=(kc == 0), stop=(kc == 7)),
                   reads=[wkey, "hnT"], writes=[pbk[bank]], sig=(kc == 7))

        for L in range(2):
            h_src = x_d if L == 0 else hmid_d
            h_dst = hmid_d if L == 0 else out_d

            with ExitStack() as lst:
                gmix = kb.sb(f"gmix{L}", [128, 8], F32, lst)
                gple = kb.sb(f"gple{L}", [128, 8], F32, lst)
                bglu = kb.sb(f"bglu{L}", [128, 8], F32, lst)
                dpad = kb.sb(f"dpad{L}", [128, 8], F32, lst)
                gq = kb.sb(f"gq{L}", [128, 1], F32, lst)
                gk = kb.sb(f"gk{L}", [128, 1], F32, lst)
                for t, d_, k_ in ((gmix, gmix_d, "gmix"), (gple, gple_d, "gple"), (bglu, bglu_d, "bglu"),
                                  (dpad, dpad_d, "dpad"), (gq, gq_d, "gq"), (gk, gk_d, "gk")):
                    kb.dma("sp", t[:], d_[L], "ld_small_" + k_, writes=[k_])
                op("dve", lambda e: e.tensor_scalar(out=gq[:], in0=gq[:], scalar1=0.125, scalar2=None, op0=ALU.mult),
                   reads=["gq"], writes=["gq"])

                with ExitStack() as s1:
                    T_tab = kb.sb("T_tab", [128, 8, 8, 128], BF16, s1)
                    WS_tab = kb.sb("WS_tab", [128, 8, 8, 128], BF16, s1)
                    WC_tab = kb.sb("WC_tab", [128, 32, 8, 32], BF16, s1)
                    AA1 = kb.sb("AA1", [128, 2, 32], F32, s1)
                    AA2 = kb.sb("AA2", [128, 2, 32], F32, s1)
                    Hh = kb.sb("Hh", [128, 2, 32, 65], F32, s1)
                    with ExitStack() as su:
                        def g32(name):
                            return kb.sb(name, [32, 64], F32, su)
                        are, aim, Lr, Li, mag, cc_, ss_, t1, t2, t3 = [g32(n) for n in
                                                                      ("are", "aim", "Lr", "Li", "mag", "cc_", "ss_", "t1", "t2", "t3")]
                        ldt = kb.sb("ldt", [32, 1], F32, su)
                        dtv = kb.sb("dtv", [32, 1], F32, su)
                        kb.dma("sp", are[:], are_d[L], "ld_are", writes=["are"])
                        kb.dma("sp", aim[:], aim_d[L], "ld_aim", writes=["aim"])
                        kb.dma("sp", ldt[:], ldt_d[L], "ld_ldt", writes=["ldt"])
                        op("act", lambda e: e.activation(out=dtv[:], in_=ldt[:], func=AF.Exp), reads=["ldt"], writes=["dtv"])

                        def dv(fn, reads, writes):
                            op("dve", fn, reads=reads, writes=writes)

                        def tt_(o, a, b, alu, ok, ak, bk_):
                            dv(lambda e: e.tensor_tensor(out=o, in0=a, in1=b, op=alu), [ak, bk_], [ok])

                        dv(lambda e: e.tensor_scalar(out=Lr[:], in0=are[:], scalar1=dtv[:, 0:1], scalar2=None, op0=ALU.mult),
                           ["are", "dtv"], ["Lr"])
                        dv(lambda e: e.tensor_scalar(out=Li[:], in0=aim[:], scalar1=dtv[:, 0:1], scalar2=None, op0=ALU.mult),
                           ["aim", "dtv"], ["Li"])
                        op("act", lambda e: e.activation(out=mag[:], in_=Lr[:], func=AF.Exp), reads=["Lr"], writes=["mag"])
                        op("act", lambda e: e.activation(out=ss_[:], in_=Li[:], func=AF.Sin, scale=1.0 / 32.0),
                           reads=["Li"], writes=["ss_"])
                        op("act", lambda e: e.activation(out=cc_[:], in_=Li[:], func=AF.Sin, scale=1.0 / 32.0,
                                                         bias=math.pi / 2), reads=["Li"], writes=["cc_"])
                        for _ in range(5):
                            tt_(t1[:], cc_[:], cc_[:], ALU.mult, "t1", "cc_", "cc_")
                            tt_(t2[:], ss_[:], ss_[:], ALU.mult, "t2", "ss_", "ss_")
                            tt_(t3[:], cc_[:], ss_[:], ALU.mult, "t3", "cc_", "ss_")
                            tt_(cc_[:], t1[:], t2[:], ALU.subtract, "cc_", "t1", "t2")
                            dv(lambda e: e.tensor_scalar(out=ss_[:], in0=t3[:], scalar1=2.0, scalar2=None, op0=ALU.mult),
                               ["t3"], ["ss_"])
                        Pre = kb.sb("Pre", [32, 9, 64], F32, su)
                        Pim = kb.sb("Pim", [32, 9, 64], F32, su)
                        tt_(Pre[:, 1, :], mag[:], cc_[:], ALU.mult, "P1r", "mag", "cc_")
                        tt_(Pim[:, 1, :], mag[:], ss_[:], ALU.mult, "P1i", "mag", "ss_")
                        tt_(t1[:], are[:], are[:], ALU.mult, "t1", "are", "are")
                        tt_(t2[:], aim[:], aim[:], ALU.mult, "t2", "aim", "aim")
                        tt_(t1[:], t1[:], t2[:], ALU.add, "t1", "t1", "t2")
                        dv(lambda e: e.reciprocal(out=t3[:], in_=t1[:]), ["t1"], ["t3"])
                        dv(lambda e: e.tensor_scalar(out=t1[:], in0=Pre[:, 1, :], scalar1=-1.0, scalar2=None, op0=ALU.add),
                           ["P1r"], ["t1"])
                        tt_(t2[:], t1[:], are[:], ALU.mult, "t2", "t1", "are")
                        tt_(mag[:], Pim[:, 1, :], aim[:], ALU.mult, "mag", "P1i", "aim")
                        tt_(t2[:], t2[:], mag[:], ALU.add, "t2", "t2", "mag")
                        tt_(Pre[:, 0, :], t2[:], t3[:], ALU.mult, "P0r", "t2", "t3")
                        tt_(t2[:], Pim[:, 1, :], are[:], ALU.mult, "t2", "P1i", "are")
                        tt_(mag[:], t1[:], aim[:], ALU.mult, "mag", "t1", "aim")
                        tt_(t2[:], t2[:], mag[:], ALU.subtract, "t2", "t2", "mag")
                        tt_(Pim[:, 0, :], t2[:], t3[:], ALU.mult, "P0i", "t2", "t3")
                        for k in range(1, 8):
                            a, b_ = f"P{k}r", f"P{k}i"
                            tt_(t1[:], Pre[:, k, :], Pre[:, 1, :], ALU.mult, "t1", a, "P1r")
                            tt_(t2[:], Pim[:, k, :], Pim[:, 1, :], ALU.mult, "t2", b_, "P1i")
                            tt_(Pre[:, k + 1, :], t1[:], t2[:], ALU.subtract, f"P{k + 1}r", "t1", "t2")
                            tt_(t1[:], Pre[:, k, :], Pim[:, 1, :], ALU.mult, "t1", a, "P1i")
                            tt_(t2[:], Pim[:, k, :], Pre[:, 1, :], ALU.mult, "t2", b_, "P1r")
                            tt_(Pim[:, k + 1, :], t1[:], t2[:], ALU.add, f"P{k + 1}i", "t1", "t2")
                        A12 = kb.sb("A12", [128, 9, 2, 32], F32, su)
                        cat = kb.sb("cat", [32, 2, 128], F32, su)
                        for k in range(9):
                            rk, ik = f"P{k}r", f"P{k}i"
                            dv(lambda e, k=k: e.tensor_copy(out=cat[:, 0, 0:64], in_=Pre[:, k, :]), [rk], ["cat"])
                            dv(lambda e, k=k: e.tensor_copy(out=cat[:, 0, 64:128], in_=Pre[:, k, :]), [rk], ["cat"])
                            dv(lambda e, k=k: e.tensor_scalar(out=cat[:, 1, 0:64], in0=Pim[:, k, :], scalar1=-1.0,
                                                              scalar2=None, op0=ALU.mult), [ik], ["cat"])
                            dv(lambda e, k=k: e.tensor_copy(out=cat[:, 1, 64:128], in_=Pim[:, k, :]), [ik], ["cat"])
                            for w_ in range(2):
                                op("pe", lambda e, w_=w_: e.transpose(out=pb[0][:, w_ * 32:(w_ + 1) * 32], in_=cat[:, w_, :],
                                                                      identity=identf[0:32, 0:32]),
                                   reads=["cat", "identf"], writes=["pb0"])
                            dv(lambda e, k=k: e.tensor_copy(out=A12[:, k, :, :],
                                                            in_=pb[0][:, 0:64].rearrange("p (a b) -> p a b", a=2)),
                               ["pb0"], [f"A12_{k}"])
                        for w_ in range(2):
                            dv(lambda e, w_=w_: e.tensor_copy(
                                out=AA1[:, w_, :].rearrange("p (a b) -> p a b", a=4),
                                in_=A12[:, 8, 0, :].rearrange("p (b a) -> p a b", a=4)), ["A12_8"], ["AA1"])
                        dv(lambda e: e.tensor_copy(out=AA2[:, 0, :].rearrange("p (a b) -> p a b", a=4),
                                                   in_=A12[:, 8, 1, :].rearrange("p (b a) -> p a b", a=4)), ["A12_8"], ["AA2"])
                        dv(lambda e: e.tensor_scalar(out=AA2[:, 1, :].rearrange("p (a b) -> p a b", a=4),
                                                     in0=A12[:, 8, 1, :].rearrange("p (b a) -> p a b", a=4),
                                                     scalar1=-1.0, scalar2=None, op0=ALU.mult), ["A12_8"], ["AA2"])
                        Braw = kb.sb("Braw", [128, 32, 16], F32, su)
                        Brsw = kb.sb("Brsw", [128, 32, 16], F32, su)
                        Bm = kb.sb("Bm", [128, 32, 16], F32, su)
                        Bsw = kb.sb("Bsw", [128, 32, 16], F32, su)
                        tmpA = kb.sb("tmpA", [128, 32, 16], F32, su)
                        tmpB = kb.sb("tmpB", [128, 32, 16], F32, su)
                        bre_v = bre_d[L].rearrange("g p h -> p g h")
                        bim_v = bim_d[L].rearrange("g p h -> p g h")
                        kb.dma("sp", Braw[0:64], bre_v, "ld_b0", writes=["Braw"])
                        kb.dma("sp", Braw[64:128], bim_v, "ld_b1", writes=["Braw"])
                        kb.dma("sp", Brsw[0:64], bim_v, "ld_b2", writes=["Brsw"])
                        kb.dma("sp", Brsw[64:128], bre_v, "ld_b3", writes=["Brsw"])

                        def bc(k, w_):
                            return A12[:, k, w_, :].unsqueeze(2).to_broadcast([128, 32, 16])

                        tt_(tmpA[:], Braw[:], bc(0, 0), ALU.mult, "tmpA", "Braw", "A12_0")
                        tt_(tmpB[:], Brsw[:], bc(0, 1), ALU.mult, "tmpB", "Brsw", "A12_0")
                        tt_(Bm[:], tmpA[:], tmpB[:], ALU.add, "Bm", "tmpA", "tmpB")
                        tt_(tmpA[:], Brsw[:], bc(0, 0), ALU.mult, "tmpA", "Brsw", "A12_0")
                        tt_(tmpB[:], Braw[:], bc(0, 1), ALU.mult, "tmpB", "Braw", "A12_0")
                        tt_(Bsw[:], tmpA[:], tmpB[:], ALU.subtract, "Bsw", "tmpA", "tmpB")
                        Cm = kb.sb("Cm", [128, 32, 16], F32, su)
                        Cmsw = kb.sb("Cmsw", [128, 32, 16], F32, su)
                        ccs = kb.sb("ccs", [128, 128], F32, su)
                        for (dst, dkey, srcd, neg_lo) in ((Cm, "Cm", ccat_d, False), (Cmsw, "Cmsw", ccsw_d, True)):
                            for c in range(4):
                                kb.dma("sp", ccs[:], srcd[L, c], "ld_ccs", writes=["ccs"])
                                op("pe", lambda e: e.transpose(out=pb[1][:, 0:128], in_=ccs[:], identity=identf[:]),
                                   reads=["ccs", "identf"], writes=["pb1"])
                                lo = pb[1][0:64, 0:128].rearrange("p (a b) -> p a b", a=8)
                                hi = pb[1][64:128, 0:128].rearrange("p (a b) -> p a b", a=8)
                                dlo = dst[0:64, 8 * c:8 * c + 8, :]
                                dhi = dst[64:128, 8 * c:8 * c + 8, :]
                                if neg_lo:
                                    dv(lambda e, lo=lo, dlo=dlo: e.tensor_scalar(out=dlo, in0=lo, scalar1=-1.0, scalar2=None,
                                                                                 op0=ALU.mult), ["pb1"], [dkey])
                                    dv(lambda e, hi=hi, dhi=dhi: e.tensor_copy(out=dhi, in_=hi), ["pb1"], [dkey])
                                else:
                                    dv(lambda e, lo=lo, dlo=dlo: e.tensor_copy(out=dlo, in_=lo), ["pb1"], [dkey])
                                    dv(lambda e, hi=hi, dhi=dhi: e.tensor_scalar(out=dhi, in0=hi, scalar1=-1.0, scalar2=None,
                                                                                 op0=ALU.mult), ["pb1"], [dkey])
                        Cpad = kb.sb("Cpad", [128, 32, 32], BF16, su)
                        ABpad = kb.sb("ABpad", [128, 32, 32], BF16, su)
                        op("pool", lambda e: e.memset(Cpad[:], 0.0), writes=["Cpad"])
                        op("pool", lambda e: e.memset(ABpad[:], 0.0), writes=["ABpad"])
                        op("pool", lambda e: e.memset(WC_tab[:], 0.0), writes=["WC"])
                        dv(lambda e: e.tensor_copy(out=Cpad[:, :, 0:16], in_=Cm[:]), ["Cm"], ["Cpad"])
                        for tau in range(8):
                            if tau == 0:
                                dv(lambda e: e.tensor_copy(out=ABpad[:, :, 0:16], in_=Bm[:]), ["Bm"], ["ABpad"])
                            else:
                                tt_(tmpA[:], Bm[:], bc(tau, 0), ALU.mult, "tmpA", "Bm", f"A12_{tau}")
                                tt_(tmpB[:], Bsw[:], bc(tau, 1), ALU.mult, "tmpB", "Bsw", f"A12_{tau}")
                                tt_(ABpad[:, :, 0:16], tmpA[:], tmpB[:], ALU.add, "ABpad", "tmpA", "tmpB")
                            pv = pb[2][:].bitcast(BF16)
                            for pc in range(8):
                                op("pe", lambda e, pc=pc: e.transpose(
                                    out=pv[:, pc * 128:(pc + 1) * 128],
                                    in_=ABpad[:, 4 * pc:4 * pc + 4, :].rearrange("p a b -> p (a b)"), identity=identb[:]),
                                   reads=["ABpad", "identb"], writes=["pb2"])
                            dv(lambda e, tau=tau: e.tensor_copy(out=WS_tab[:, :, 7 - tau, :],
                                                                in_=pv.rearrange("p (a b) -> p a b", a=8)), ["pb2"], ["WS"])
                            for pc in range(8):
                                bk = 3 + pc // 4
                                op("pe", lambda e, pc=pc, bk=bk: e.matmul(
                                    pb[bk][:, (pc % 4) * 128:(pc % 4 + 1) * 128],
                                    lhsT=ABpad[:, 4 * pc:4 * pc + 4, :].rearrange("p a b -> p (a b)"),
                                    rhs=Cpad[:, 4 * pc:4 * pc + 4, :].rearrange("p a b -> p (a b)"), start=True, stop=True),
                                   reads=["ABpad", "Cpad"], writes=[pbk[bk]])
                            for hf in range(2):
                                dv(lambda e, hf=hf, tau=tau: e.tensor_tensor(
                                    out=T_tab[:, 4 * hf:4 * hf + 4, tau, :],
                                    in0=pb[3 + hf][:].rearrange("p (a b) -> p a b", a=4),
                                    in1=bdmask[:].unsqueeze(1).to_broadcast([128, 4, 128]), op=ALU.mult),
                                   [pbk[3 + hf], "bdmask"], ["T"])
                        for lp in range(8):
                            k = lp + 1
                            tt_(tmpA[:], Cm[:], bc(k, 0), ALU.mult, "tmpA", "Cm", f"A12_{k}")
                            tt_(tmpB[:], Cmsw[:], bc(k, 1), ALU.mult, "tmpB", "Cmsw", f"A12_{k}")
                            tt_(WC_tab[:, :, lp, 0:16], tmpA[:], tmpB[:], ALU.subtract, "WC", "tmpA", "tmpB")
                        op("dve", lambda e: e.memset(Hh[:, :, :, 0:1], 0.0), writes=["Hh"])
                        kb.barrier()
                    hsub = [kb.sb(f"hsub{i}", [128, D], F32, s1) for i in range(2)]
                    hnt = [kb.sb(f"hnt{i}", [128, D], BF16, s1) for i in range(2)]
                    ss = kb.sb("ssq", [128, 4], F32, s1)
                    lnv = kb.sb("lnv", [128, 4], F32, s1)
                    rstd = kb.sb("rstd", [128, 4], F32, s1)
                    sqj = kb.sb("sqj", [128, D], BF16, s1)
                    hnT = kb.sb("hnT", [128, 8, TT], BF16, s1)
                    nb = (hsub, hnt, ss, lnv, rstd, sqj, hnT)
                    uTp = kb.sb("uTp", [128, 8, TT], BF16, s1)
                    sgs = kb.sb("sgs", [128, 4, TT], BF16, s1)
                    zTp = kb.sb("zTp", [128, 8, TT], BF16, s1)
                    ysT = kb.sb("ysT", [128, 4, TT], BF16, s1)
                    SS = kb.sb("SS", [128, 2, 32, 64], F32, s1)
                    Hprev = kb.sb("Hprev", [128, 32, 64], BF16, s1)
                    rt1 = kb.sb("rt1", [128, 2, 32], F32, s1)
                    rt2 = kb.sb("rt2", [128, 2, 32], F32, s1)
                    Ysb = [kb.sb(f"Ysb{i}", [64, 8, 4, 32], BF16, s1) for i in range(2)]
                    ytmp = [kb.sb(f"ytmp{i}", [128, TT], F32, s1) for i in range(2)]
                    sgt = [kb.sb(f"sgt{i}", [128, TT], F32, s1) for i in range(2)]
                    gtt = [kb.sb(f"gtt{i}", [128, TT], F32, s1) for i in range(2)]
                    hh_ap = Hh[:]
                    pstep = list(hh_ap.ap[0])

                    for j in range(NT):
                        pf = Pref(L, [0, 1, 2, 7, 8])
                        norm_phase(nb, lambda i: h_src[j * TT + i * 128:j * TT + (i + 1) * 128, :], gmix, "gmix")
                        for half in range(2):
                            w3, wk = pf.get(half)
                            for q4 in range(4):
                                pc = half * 4 + q4
                                bank = 2 + pc % 2
                                proj_fm(w3, wk, q4 * 128, hnT, bank)
                                op("act", lambda e, pc=pc, bank=bank: e.activation(out=uTp[:, pc, :], in_=pb[bank][:], func=AF.Copy),
                                   reads=[pbk[bank]], writes=["uTp"])
                        w3, wk = pf.get(2)
                        for c in range(4):
                            bank = 2 + c % 2
                            proj_fm(w3, wk, c * 128, hnT, bank)
                            op("act", lambda e, c=c, bank=bank: e.activation(out=sgs[:, c, :], in_=pb[bank][:], func=AF.Silu),
                               reads=[pbk[bank]], writes=["sgs"])
                        for pc in range(8):
                            for l in range(8):
                                for gg in range(4):
                                    op("pe", lambda e, pc=pc, l=l, gg=gg: e.matmul(
                                        pb[4 + gg][:, pc * 64:(pc + 1) * 64],
                                        lhsT=WS_tab[32 * gg:32 * gg + 32, pc, l, :],
                                        rhs=uTp[32 * gg:32 * gg + 32, pc, l:TT:8],
                                        start=(l == 0), stop=(l == 7), tile_position=(32 * gg, 0)),
                                       reads=["WS", "uTp"], writes=[pbk[4 + gg]], sig=(l == 7))
                        for gg in range(4):
                            op("act" if gg % 2 else "dve",
                               (lambda e, gg=gg: e.activation(out=SS[:, 0, 8 * gg:8 * gg + 8, :].rearrange("p a b -> p (a b)"),
                                                              in_=pb[4 + gg][:], func=AF.Copy)) if gg % 2 else
                               (lambda e, gg=gg: e.tensor_copy(out=SS[:, 0, 8 * gg:8 * gg + 8, :].rearrange("p a b -> p (a b)"),
                                                               in_=pb[4 + gg][:])),
                               reads=[pbk[4 + gg]], writes=["SS0"])
                        op("pool", lambda e: e.tensor_copy(out=SS[0:64, 1, :, :], in_=SS[64:128, 0, :, :]), reads=["SS0"], writes=["SS1"])
                        op("pool", lambda e: e.tensor_copy(out=SS[64:128, 1, :, :], in_=SS[0:64, 0, :, :]), reads=["SS0"], writes=["SS1"])
                        for sc in range(64):
                            cur = Hh[:, :, :, sc]
                            swp = bass.AP(hh_ap.tensor, hh_ap.offset + 32 * 65 + sc, [pstep, [-32 * 65, 2], [65, 32]])
                            op("dve", lambda e, cur=cur: e.tensor_tensor(out=rt1[:], in0=cur, in1=AA1[:], op=ALU.mult),
                               reads=["Hh", "AA1"], writes=["rt1"])
                            op("dve", lambda e, swp=swp: e.tensor_tensor(out=rt2[:], in0=swp, in1=AA2[:], op=ALU.mult),
                               reads=["Hh", "AA2"], writes=["rt2"])
                            op("dve", lambda e: e.tensor_tensor(out=rt1[:], in0=rt1[:], in1=rt2[:], op=ALU.add),
                               reads=["rt1", "rt2"], writes=["rt1"])
                            op("dve", lambda e, sc=sc: e.tensor_tensor(out=Hh[:, :, :, sc + 1], in0=rt1[:], in1=SS[:, :, :, sc],
                                                                       op=ALU.add),
                               reads=["rt1", "SS0", "SS1"], writes=["Hh"])
                        op("pool", lambda e: e.tensor_copy(out=Hprev[:], in_=Hh[:, 0, :, 0:64]), reads=["Hh"], writes=["Hprev"])
                        op("dve", lambda e: e.tensor_copy(out=Hh[:, :, :, 0:1], in_=Hh[:, :, :, 64:65]), reads=["Hh"], writes=["Hh"])
                        for pc in range(8):
                            yb = Ysb[pc % 2]
                            ykey = f"Ysb{pc % 2}"
                            for gg in range(4):
                                gi = gg * 8 + pc
                                gn = 4 * pc + gg
                                bank = gg // 2
                                op("pe", lambda e, gi=gi, gn=gn, gg=gg, bank=bank: e.matmul(
                                    pb[bank][0:64, (gg % 2) * 256:(gg % 2 + 1) * 256],
                                    lhsT=Hprev[:, gi, :], rhs=WC_tab[:, gn, :, :].rearrange("p a b -> p (a b)"),
                                    start=True, stop=True), reads=["Hprev", "WC"], writes=[pbk[bank]])
                            for bank in range(2):
                                op("act", lambda e, bank=bank, yb=yb: e.activation(
                                    out=yb[:, :, 2 * bank:2 * bank + 2, :],
                                    in_=pb[bank][0:64, :].rearrange("p (g l h) -> p l g h", g=2, l=8),
                                    func=AF.Copy), reads=[pbk[bank]], writes=[ykey])
                            ybank = 2 + pc % 2
                            for lp in range(8):
                                op("pe", lambda e, lp=lp, yb=yb, ybank=ybank: e.matmul(
                                    pb[ybank][:, lp:TT:8], lhsT=yb[:, lp, :, :].rearrange("p g h -> p (g h)"), rhs=identb[0:64, 0:64],
                                    start=True, stop=False), reads=[ykey, "identb"], writes=[pbk[ybank]], sig=False)
                                for l in range(lp + 1):
                                    op("pe", lambda e, lp=lp, l=l, pc=pc, ybank=ybank: e.matmul(
                                        pb[ybank][:, lp:TT:8], lhsT=T_tab[:, pc, lp - l, :], rhs=uTp[:, pc, l:TT:8],
                                        start=False, stop=(l == lp)), reads=["T", "uTp"], writes=[pbk[ybank]],
                                       sig=(l == lp and lp == 7))
                            yt = ytmp[pc % 2]
                            op("dve", lambda e, pc=pc, yt=yt, ybank=ybank: e.scalar_tensor_tensor(
                                out=yt[:], in0=uTp[:, pc, :], scalar=dpad[:, pc:pc + 1], in1=pb[ybank][:],
                                op0=ALU.mult, op1=ALU.add), reads=["uTp", "dpad", pbk[ybank]], writes=[f"ytmp{pc % 2}"])
                            op("act", lambda e, pc=pc, yt=yt: e.activation(out=zTp[:, pc, :], in_=yt[:], func=AF.Gelu_apprx_tanh),
                               reads=[f"ytmp{pc % 2}"], writes=["zTp"])
                        wv3, wvk = pf.get(3)
                        wg3, wgk = pf.get(4)
                        for c in range(4):
                            bv_, bg_ = 4 + 2 * (c % 2), 5 + 2 * (c % 2)
                            for (w3, wk, bank) in ((wv3, wvk, bv_), (wg3, wgk, bg_)):
                                for kc in range(8):
                                    op("pe", lambda e, kc=kc, w3=w3, bank=bank, c=c: e.matmul(
                                        pb[bank][:], lhsT=w3[:, kc, c * 128:(c + 1) * 128], rhs=zTp[:, kc, :],
                                        start=(kc == 0), stop=(kc == 7)), reads=[wk, "zTp"], writes=[pbk[bank]], sig=(kc == 7))
                            sg_ = sgt[c % 2]
                            gt_ = gtt[c % 2]
                            op("act", lambda e, c=c, sg_=sg_, bg_=bg_: e.activation(out=sg_[:], in_=pb[bg_][:], func=AF.Sigmoid,
                                                                                   bias=bglu[:, 4 + c:5 + c]),
                               reads=[pbk[bg_], "bglu"], writes=[f"sgt{c % 2}"])
                            op("dve", lambda e, c=c, sg_=sg_, gt_=gt_, bv_=bv_: e.scalar_tensor_tensor(
                                out=gt_[:], in0=pb[bv_][:], scalar=bglu[:, c:c + 1], in1=sg_[:], op0=ALU.add, op1=ALU.mult),
                               reads=[pbk[bv_], "bglu", f"sgt{c % 2}"], writes=[f"gtt{c % 2}"])
                            op("pool", lambda e, c=c, gt_=gt_: e.tensor_tensor(out=ysT[:, c, :], in0=gt_[:], in1=sgs[:, c, :],
                                                                               op=ALU.mult),
                               reads=[f"gtt{c % 2}", "sgs"], writes=["ysT"])
                        kb.dma("sp", yss_d[L, :, :, j * TT:(j + 1) * TT].rearrange("c p t -> p c t"), ysT[:], "st_yss",
                               reads=["ysT"])
                    kb.barrier()
                if stop_at == f"s1_{L}":
                    dd = nc.dram_tensor("dbg_yss", [4, 128, S], BF16, kind="ExternalOutput").ap()
                    kb.dma("sp", dd, yss_d[L], "st_dbgyss")
                    break

                with ExitStack() as s2:
                    kT = kb.sb("kT", [128, 4, S], BF16, s2)
                    Vc = kb.sb("Vc", [128, S // 128, 512], BF16, s2)
                    htile = kb.sb("htile", [128, NSUB, D], F32, s2)
                    hsub = [htile[:, 0, :], htile[:, 1, :]]
                    hnt = [kb.sb(f"hnt{i}", [128, D], BF16, s2) for i in range(2)]
                    ss = kb.sb("ssq", [128, 4], F32, s2)
                    lnv = kb.sb("lnv", [128, 4], F32, s2)
                    rstd = kb.sb("rstd", [128, 4], F32, s2)
                    sqj = kb.sb("sqj", [128, D], BF16, s2)
                    hnT = kb.sb("hnT", [128, 8, TT], BF16, s2)
                    qT = kb.sb("qT", [128, 4, TT], BF16, s2)
                    sga = kb.sb("sga", [128, 4, TT], BF16, s2)
                    yT = kb.sb("yT", [128, 8, TT], BF16, s2)
                    sq = [kb.sb(f"sq{i}", [128, TT], BF16, s2) for i in range(2)]
                    sd = [kb.sb(f"sd{i}", [128, TT], F32, s2) for i in range(2)]
                    e_sb = [kb.sb(f"e_sb{i}", [128, TT], BF16, s2) for i in range(2)]
                    sp_sb = [kb.sb(f"sp_sb{i}", [128, TT], BF16, s2) for i in range(2)]
                    x_sb = [kb.sb(f"x_sb{i}", [128, TT], BF16, s2) for i in range(2)]
                    w_sb = [kb.sb(f"w_sb{i}", [128, TT], BF16, s2) for i in range(2)]
                    S_sb = [kb.sb(f"S_sb{i}", [128, TT], BF16, s2) for i in range(2)]
                    psub = [kb.sb(f"psub{i}", [128, 256], F32, s2) for i in range(2)]
                    pbf = [kb.sb(f"pbf{i}", [128, 256], BF16, s2) for i in range(2)]
                    pT = kb.sb("pT", [128, 2, TT], BF16, s2)
                    gsb = [kb.sb(f"gsb{i}", [128, TT], F32, s2) for i in range(2)]
                    gpp = [kb.sb(f"gpp{i}", [128, TT], F32, s2) for i in range(2)]

                    class _Sub:
                        pass

                    tctr = [0]
                    HT = ["htile", "hsub0", "hsub1"]
                    for j in range(NT):
                        pf = Pref(L, [3, 4, 5, 6, 9, 10, 11, 13, 12])
                        nb = (hsub, hnt, ss, lnv, rstd, sqj, hnT)
                        norm_phase(nb, lambda i: h_src[j * TT + i * 128:j * TT + (i + 1) * 128, :], gmix, "gmix")
                        for which in range(2):
                            w3, wk = pf.get(which)
                            gvec = gq if which == 0 else gk
                            gkey = "gq" if which == 0 else "gk"
                            for c in range(4):
                                bank = 2 + c % 2
                                proj_fm(w3, wk, c * 128, hnT, bank)
                                sq_, sd_ = sq[c % 2], sd[c % 2]
                                op("act", lambda e, sq_=sq_, bank=bank: e.activation(out=sq_[:], in_=pb[bank][:], func=AF.Square),
                                   reads=[pbk[bank]], writes=[f"sq{c % 2}"])
                                sbank = 4 + c % 2
                                op("pe", lambda e, sq_=sq_, sbank=sbank: e.matmul(pb[sbank][:], lhsT=blockones[:], rhs=sq_[:],
                                                                                   start=True, stop=True),
                                   reads=[f"sq{c % 2}", "blockones"], writes=[pbk[sbank]])
                                op("act", lambda e, sd_=sd_, sbank=sbank: e.activation(out=sd_[:], in_=pb[sbank][:], func=AF.Ln,
                                                                                      scale=1.0 / 64.0, bias=EPS),
                                   reads=[pbk[sbank]], writes=[f"sd{c % 2}"])
                                op("act", lambda e, sd_=sd_: e.activation(out=sd_[:], in_=sd_[:], func=AF.Exp, scale=-0.5),
                                   reads=[f"sd{c % 2}"], writes=[f"sd{c % 2}"])
                                dst = qT[:, c, :] if which == 0 else kT[:, c, j * TT:(j + 1) * TT]
                                op("dve", lambda e, dst=dst, bank=bank, sd_=sd_, gvec=gvec: e.scalar_tensor_tensor(
                                    out=dst, in0=pb[bank][:], scalar=gvec[:, 0:1], in1=sd_[:], op0=ALU.mult, op1=ALU.mult),
                                   reads=[pbk[bank], gkey, f"sd{c % 2}"], writes=["qT" if which == 0 else "kT"])
                        w3, wk = pf.get(2)
                        for i in range(NSUB):
                            bank = 2 + i % 2
                            for kc in range(8):
                                op("pe", lambda e, kc=kc, i=i, bank=bank: e.matmul(
                                    pb[bank][:], lhsT=hnT[:, kc, i * 128:(i + 1) * 128], rhs=w3[:, kc, :],
                                    start=(kc == 0), stop=(kc == 7)), reads=[wk, "hnT"], writes=[pbk[bank]], sig=(kc == 7))
                            op("dve", lambda e, i=i, bank=bank: e.tensor_copy(out=Vc[:, 4 * j + i, :], in_=pb[bank][:]),
                               reads=[pbk[bank]], writes=["Vc"])
                        w3, wk = pf.get(3)
                        for c in range(4):
                            bank = 2 + c % 2
                            proj_fm(w3, wk, c * 128, hnT, bank)
                            op("act", lambda e, c=c, bank=bank: e.activation(out=sga[:, c, :], in_=pb[bank][:], func=AF.Silu),
                               reads=[pbk[bank]], writes=["sga"])
                        for h in range(8):
                            c, base = h // 2, 64 * (h % 2)
                            obank = 6 + h % 2
                            nblk = 4 * j + 4
                            s_cur = None
                            for kb_ in range(nblk - 1, -1, -1):
                                t = tctr[0]
                                tctr[0] += 1
                                par = t % 2
                                diag = kb_ >= 4 * j
                                qoff = (kb_ - 4 * j) * 128 if diag else 0
                                zb, cb = par, 2 + par
                                e_, sp_, x_, w_ = e_sb[par], sp_sb[par], x_sb[par], w_sb[par]
                                ek, spk, xk, wk_ = f"e_sb{par}", f"sp_sb{par}", f"x_sb{par}", f"w_sb{par}"
                                op("pe", lambda e: e.matmul(pb[zb][:, qoff:TT], lhsT=kT[base:base + 64, c, kb_ * 128:(kb_ + 1) * 128],
                                                            rhs=qT[base:base + 64, c, qoff:TT], start=True, stop=True),
                                   reads=["kT", "qT"], writes=[pbk[zb]])
                                op("act", lambda e: e.activation(out=e_[:, qoff:TT], in_=pb[zb][:, qoff:TT], func=AF.Exp),
                                   reads=[pbk[zb]], writes=[ek])
                                if diag:
                                    op("pool", lambda e: e.tensor_tensor(out=e_[:, qoff:qoff + 128], in0=e_[:, qoff:qoff + 128],
                                                                         in1=masklt[:], op=ALU.mult),
                                       reads=[ek, "masklt"], writes=[ek])
                                op("act", lambda e: e.activation(out=sp_[:, qoff:TT], in_=e_[:, qoff:TT], func=AF.Ln, bias=1.0),
                                   reads=[ek], writes=[spk])
                                if s_cur is None:
                                    s_off = None
                                elif diag:
                                    s_off = qoff + 128
                                else:
                                    s_off = 0
                                has_s = s_off is not None and s_off < TT
                                op("pe", lambda e: e.matmul(pb[cb][:, qoff:TT], lhsT=tri[:], rhs=sp_[:, qoff:TT], start=True,
                                                            stop=not has_s), reads=["tri", spk], writes=[pbk[cb]])
                                if has_s:
                                    sc_t, sc_k = S_sb[s_cur], f"S_sb{s_cur}"
                                    op("pe", lambda e: e.matmul(pb[cb][:, s_off:TT], lhsT=ones[:], rhs=sc_t[:, s_off:TT],
                                                                start=False, stop=True), reads=["ones", sc_k], writes=[pbk[cb]])
                                op("act", lambda e: e.activation(out=x_[:, qoff:TT], in_=pb[cb][:, qoff:TT], func=AF.Exp, scale=-1.0),
                                   reads=[pbk[cb]], writes=[xk])
                                op("dve", lambda e: e.tensor_tensor(out=w_[:, qoff:TT], in0=e_[:, qoff:TT], in1=x_[:, qoff:TT],
                                                                    op=ALU.mult), reads=[ek, xk], writes=[wk_])
                                op("pe", lambda e: e.matmul(pb[obank][base:base + 64, qoff:TT],
                                                            lhsT=Vc[:, kb_, h * 64:(h + 1) * 64], rhs=w_[:, qoff:TT],
                                                            start=(kb_ == nblk - 1), stop=(kb_ == 0),
                                                            skip_group_check=True),
                                   reads=["Vc", wk_], writes=[pbk[obank]])
                                if kb_ > 0:
                                    s_nxt = 0 if s_cur is None else 1 - s_cur
                                    sn_t, sn_k = S_sb[s_nxt], f"S_sb{s_nxt}"
                                    if s_cur is None:
                                        op("dve", lambda e: e.tensor_copy(out=sn_t[:, qoff:TT], in_=sp_[:, qoff:TT]),
                                           reads=[spk], writes=[sn_k])
                                    else:
                                        sc_t, sc_k = S_sb[s_cur], f"S_sb{s_cur}"
                                        if diag:
                                            op("dve", lambda e: e.tensor_copy(out=sn_t[:, qoff:qoff + 128], in_=sp_[:, qoff:qoff + 128]),
                                               reads=[spk], writes=[sn_k])
                                            a0 = qoff + 128
                                        else:
                                            a0 = 0
                                        op("dve", lambda e: e.tensor_tensor(out=sn_t[:, a0:TT], in0=sc_t[:, a0:TT], in1=sp_[:, a0:TT],
                                                                            op=ALU.add), reads=[sc_k, spk], writes=[sn_k])
                                    s_cur = s_nxt
                            op("dve", lambda e: e.tensor_tensor(out=yT[base:base + 64, 4 + c, :], in0=pb[obank][base:base + 64, :],
                                                                in1=sga[base:base + 64, c, :], op=ALU.mult),
                               reads=[pbk[obank], "sga"], writes=["yT"])
                        kb.dma("sp", yT[:, 0:4, :], yss_d[L, :, :, j * TT:(j + 1) * TT].rearrange("c p t -> p c t"), "ld_yss",
                               writes=["yT"])
                        kb.dma("sp", htile[:], h_src[j * TT:(j + 1) * TT, :].rearrange("(i p) d -> p i d", p=128), "ld_htile", writes=HT)
                        for half in range(2):
                            w3, wk = pf.get(4 + half)
                            for i in range(NSUB):
                                bank = 4 + i % 2
                                for kc in range(8):
                                    op("pe", lambda e, kc=kc, i=i, bank=bank, w3=w3: e.matmul(
                                        pb[bank][:], lhsT=yT[:, kc, i * 128:(i + 1) * 128], rhs=w3[:, kc, :],
                                        start=(kc == 0), stop=(kc == 7)), reads=[wk, "yT"], writes=[pbk[bank]], sig=(kc == 7))
                                op("dve", lambda e, i=i, bank=bank, half=half: e.tensor_tensor(
                                    out=htile[:, i, half * 512:(half + 1) * 512], in0=htile[:, i, half * 512:(half + 1) * 512],
                                    in1=pb[bank][:], op=ALU.add), reads=[pbk[bank]] + HT, writes=HT)
                        if dbg and j == 0 and L == 0:
                            kb.dma("sp", dbg_out("dbg_h1", [128, NSUB, D]), htile[:], "st_dbg1", reads=HT)
                        for i in range(NSUB):
                            hs = htile[:, i, :]
                            op("act", lambda e, hs=hs, i=i: e.activation(out=sqj[:], in_=hs, func=AF.Square, accum_out=ss[:, i:i + 1]),
                               reads=HT, writes=["sqj", f"ss{i}"])
                            op("act", lambda e, i=i: e.activation(out=lnv[:, i:i + 1], in_=ss[:, i:i + 1], func=AF.Ln,
                                                                  scale=1.0 / D, bias=EPS), reads=[f"ss{i}"], writes=[f"lnv{i}"])
                            op("act", lambda e, i=i: e.activation(out=rstd[:, i:i + 1], in_=lnv[:, i:i + 1], func=AF.Exp, scale=-0.5),
                               reads=[f"lnv{i}"], writes=[f"rstd{i}"])
                            ht = hnt[i % 2]
                            tk = f"hnt{i % 2}"
                            op("pool", lambda e, hs=hs, ht=ht, i=i: e.tensor_scalar(out=ht[:], in0=hs, scalar1=rstd[:, i:i + 1],
                                                                                    scalar2=None, op0=ALU.mult),
                               reads=HT + [f"rstd{i}"], writes=[tk])
                            bk = i % 2
                            pv = pb[bk][:].bitcast(BF16)
                            for kc in range(8):
                                op("pe", lambda e, kc=kc, ht=ht, pv=pv: e.transpose(out=pv[:, kc * 128:(kc + 1) * 128],
                                                                                   in_=ht[:, kc * 128:(kc + 1) * 128], identity=identb[:]),
                                   reads=[tk, "identb"], writes=[pbk[bk]], sig=(kc == 7))
                            op("dve", lambda e, i=i, pv=pv: e.tensor_tensor(out=hnT[:, :, i * 128:(i + 1) * 128],
                                                                            in0=pv.rearrange("p (k t) -> p k t", k=8),
                                                                            in1=gple[:, :].unsqueeze(2).to_broadcast([128, 8, 128]),
                                                                            op=ALU.mult), reads=[pbk[bk], "gple"], writes=["hnT"])
                            ps_, pb_ = psub[i % 2], pbf[i % 2]
                            kb.dma("sp", ps_[:], p_d[L, j * TT + i * 128:j * TT + (i + 1) * 128, :], f"ld_psub{i % 2}",
                                   writes=[f"psub{i % 2}"])
                            op("pool", lambda e, ps_=ps_, pb_=pb_: e.tensor_copy(out=pb_[:], in_=ps_[:]),
                               reads=[f"psub{i % 2}"], writes=[f"pbf{i % 2}"])
                            pv2 = pb[2 + i % 2][:].bitcast(BF16)
                            for k2 in range(2):
                                op("pe", lambda e, k2=k2, pb_=pb_, pv2=pv2: e.transpose(out=pv2[:, k2 * 128:(k2 + 1) * 128],
                                                                                       in_=pb_[:, k2 * 128:(k2 + 1) * 128],
                                                                                       identity=identb[:]),
                                   reads=[f"pbf{i % 2}", "identb"], writes=[pbk[2 + i % 2]], sig=(k2 == 1))
                            op("dve", lambda e, i=i, pv2=pv2: e.tensor_copy(out=pT[:, :, i * 128:(i + 1) * 128],
                                                                            in_=pv2[:, 0:256].rearrange("p (k t) -> p k t", k=2)),
                               reads=[pbk[2 + i % 2]], writes=["pT"])
                        wpp3, wppk = None, None
                        for half in range(2):
                            w3, wk = pf.get(6 if half == 0 else 8)
                            if wpp3 is None:
                                wpp3, wppk = pf.get(7)
                            for i in range(NSUB):
                                gb, pbk_ = 4 + i % 2, 6 + i % 2
                                for kc in range(8):
                                    op("pe", lambda e, kc=kc, i=i, gb=gb, w3=w3: e.matmul(
                                        pb[gb][:], lhsT=hnT[:, kc, i * 128:(i + 1) * 128], rhs=w3[:, kc, :],
                                        start=(kc == 0), stop=(kc == 7)), reads=[wk, "hnT"], writes=[pbk[gb]], sig=(kc == 7))
                                for k2 in range(2):
                                    op("pe", lambda e, k2=k2, i=i, pbk_=pbk_, half=half: e.matmul(
                                        pb[pbk_][:], lhsT=pT[:, k2, i * 128:(i + 1) * 128],
                                        rhs=wpp3[:, k2, half * 512:(half + 1) * 512], start=(k2 == 0), stop=(k2 == 1)),
                                       reads=[wppk, "pT"], writes=[pbk[pbk_]], sig=(k2 == 1))
                                g_, gp_ = gsb[i % 2], gpp[i % 2]
                                op("act", lambda e, g_=g_, gb=gb: e.activation(out=g_[:], in_=pb[gb][:], func=AF.Sigmoid),
                                   reads=[pbk[gb]], writes=[f"gsb{i % 2}"])
                                op("dve", lambda e, g_=g_, gp_=gp_, pbk_=pbk_: e.tensor_tensor(out=gp_[:], in0=g_[:], in1=pb[pbk_][:],
                                                                                               op=ALU.mult),
                                   reads=[f"gsb{i % 2}", pbk[pbk_]], writes=[f"gpp{i % 2}"])
                                op("pool", lambda e, i=i, half=half, gp_=gp_: e.tensor_tensor(
                                    out=htile[:, i, half * 512:(half + 1) * 512], in0=htile[:, i, half * 512:(half + 1) * 512],
                                    in1=gp_[:], op=ALU.add), reads=[f"gpp{i % 2}"] + HT, writes=HT)
                        kb.dma("sp", h_dst[j * TT:(j + 1) * TT, :].rearrange("(i p) d -> p i d", p=128), htile[:], "st_h", reads=HT)
                    kb.barrier()
                if stop_at == f"s2_{L}":
                    dd = nc.dram_tensor("dbg_h", [S, D], F32, kind="ExternalOutput").ap()
                    kb.dma("sp", dd, h_dst, "st_dbgh")
                    break
        kb.finish("sp")
        build_program.stats = (kb.ninst, kb.nwaits)
    return nc


def _prep_shared(inp):
    f = np.float32
    w_in = np.asarray(inp["w_in"], f)
    u_cols = w_in[:, :, 0:512].reshape(2, D, 32, 16)
    u_pad = np.zeros((2, D, 32, 32), f)
    u_pad[:, :, :, 0:16] = u_cols
    w_in_p = np.concatenate([u_pad.reshape(2, D, 1024), w_in[:, :, 512:]], axis=2)
    w_glu = np.asarray(inp["ssm_w_glu"], f).reshape(2, 32, 16, 1024)
    w_glu_p = np.zeros((2, 32, 32, 1024), f)
    w_glu_p[:, :, 0:16, :] = w_glu
    w_glu_p = w_glu_p.reshape(2, 1024, 1024)

    def colT(v):
        return np.ascontiguousarray(np.asarray(v, f).reshape(2, 8, 128).transpose(0, 2, 1))

    d = np.asarray(inp["ssm_d"], f)
    d_pad = np.zeros((2, 32, 32), f)
    d_pad[:, :, 0:16] = d
    dpad = np.ascontiguousarray(d_pad.reshape(2, 8, 128).transpose(0, 2, 1))
    gq = np.tile(np.asarray(inp["q_norm_g"], f), (1, 2)).reshape(2, 128, 1)
    gk = np.tile(np.asarray(inp["k_norm_g"], f), (1, 2)).reshape(2, 128, 1)
    cre = np.asarray(inp["ssm_c_re"], f).reshape(2, 4, 128, 64)
    cim = np.asarray(inp["ssm_c_im"], f).reshape(2, 4, 128, 64)
    return {
        "w_in": np.ascontiguousarray(w_in_p), "w_glu": w_glu_p,
        "w_out": np.asarray(inp["w_out"], f), "w_pg": np.asarray(inp["w_ple_gate"], f),
        "w_pp": np.asarray(inp["w_ple_proj"], f),
        "gmix": colT(inp["mix_norm_g"]), "gple": colT(inp["ple_norm_g"]), "bglu": colT(inp["ssm_b_glu"]),
        "dpad": dpad, "gq": np.ascontiguousarray(gq), "gk": np.ascontiguousarray(gk),
        "a_re": np.asarray(inp["ssm_a_re"], f), "a_im": np.asarray(inp["ssm_a_im"], f),
        "logdt": np.asarray(inp["ssm_log_dt"], f).reshape(2, 32, 1),
        "b_re": np.asarray(inp["ssm_b_re"], f), "b_im": np.asarray(inp["ssm_b_im"], f),
        "ccat": np.ascontiguousarray(np.concatenate([cre, cim], axis=3)),
        "ccatsw": np.ascontiguousarray(np.concatenate([cim, cre], axis=3)),
    }


def kernel(**inputs):
    shared = _prep_shared(inputs)
    x = np.asarray(inputs["x"], np.float32)
    p = np.asarray(inputs["p"], np.float32)
    nc = build_program()
    in_maps = []
    for b in range(NCORES):
        m = dict(shared)
        m["x"] = np.ascontiguousarray(x[b])
        m["p"] = np.ascontiguousarray(p[:, b])
        in_maps.append(m)
    res = run_bass_kernel_spmd(nc, in_maps, core_ids=list(range(NCORES)))
    return np.stack([r["out"] for r in res.results], axis=0).astype(np.float32)
```

```python
import math
from contextlib import ExitStack
import numpy as np
import concourse.bass as bass
import concourse.mybir as mybir
from concourse.bass_utils import run_bass_kernel_spmd

F32 = mybir.dt.float32
BF16 = mybir.dt.bfloat16
AF = mybir.ActivationFunctionType
ALU = mybir.AluOpType

S = 4096
D = 1024
TT = 512
NT = S // TT
NSUB = TT // 128
EPS = 1e-6
NPIECE = 14
NCORES = 8


class _Eng:
    def __init__(self, name, handle, sem):
        self.name = name
        self.h = handle
        self.sem = sem
        self.count = 0
        self.waited = {}


class KB:
    def __init__(self, nc, stack):
        self.nc = nc
        self.stack = stack
        self.eng = {}
        for name, h in (("pe", nc.tensor), ("act", nc.scalar), ("dve", nc.vector),
                        ("pool", nc.gpsimd), ("sp", nc.sync)):
            sem = stack.enter_context(nc.semaphore("s_" + name))
            self.eng[name] = _Eng(name, h, sem)
        self.state = {}
        self.dsem = {}
        self.nwaits = 0
        self.ninst = 0

    def sb(self, name, shape, dt, stack=None):
        self._uid = getattr(self, "_uid", 0) + 1
        return (stack or self.stack).enter_context(self.nc.sbuf_tensor(f"{name}_{self._uid}", list(shape), dt))

    def ps(self, name, shape, dt=F32):
        return self.stack.enter_context(self.nc.psum_tensor(name, list(shape), dt))

    def dma_sem(self, name):
        if name not in self.dsem:
            sem = self.stack.enter_context(self.nc.semaphore("d_" + name))
            self.dsem[name] = [sem, 0]
        return self.dsem[name]

    def _st(self, k):
        s = self.state.get(k)
        if s is None:
            s = [None, []]
            self.state[k] = s
        return s

    def _wait(self, e, sem, val):
        key = id(sem)
        if e.waited.get(key, 0) >= val:
            return
        e.h.wait_ge(sem, val)
        e.waited[key] = val
        self.nwaits += 1

    def _deps(self, e, reads, writes):
        for k in reads:
            w = self._st(k)[0]
            if w is not None:
                self._wait(e, w[0], w[1])
        pe = e.name == "pe"
        for k in writes:
            s = self._st(k)
            if s[0] is not None and not (pe and s[0][0] is e.sem):
                self._wait(e, s[0][0], s[0][1])
            for (sem, val) in s[1]:
                if not (pe and sem is e.sem):
                    self._wait(e, sem, val)

    def _commit(self, tag, reads, writes):
        for k in reads:
            r = self._st(k)[1]
            for idx, (sem, val) in enumerate(r):
                if sem is tag[0]:
                    r[idx] = tag if tag[1] > val else (sem, val)
                    break
            else:
                r.append(tag)
        for k in writes:
            s = self._st(k)
            s[0] = tag
            s[1] = []

    def op(self, en, fn, reads=(), writes=(), sig=True):
        e = self.eng[en]
        self._deps(e, reads, writes)
        ins = fn(e.h)
        self.ninst += 1
        if sig:
            e.count += 1
            ins.then_inc(e.sem, 1)
            tag = (e.sem, e.count)
        else:
            tag = (e.sem, e.count + 1)
        self._commit(tag, reads, writes)
        return ins

    def dma(self, qn, out, in_, semname, reads=(), writes=(), **kw):
        e = self.eng[qn]
        ds = self.dma_sem(semname)
        if ds[1] > 0:
            self._wait(e, ds[0], ds[1])
        self._deps(e, reads, writes)
        ins = e.h.dma_start(out=out, in_=in_, **kw)
        ds[1] += 16
        ins.then_inc(ds[0], 16)
        self.ninst += 1
        self._commit((ds[0], ds[1]), reads, writes)
        return ins

    def barrier(self):
        snap_e = [(o.sem, o.count) for o in self.eng.values() if o.count]
        snap_d = [(sem, cnt) for (sem, cnt) in self.dsem.values() if cnt]
        for e in self.eng.values():
            for sem, cnt in snap_e:
                if sem is not e.sem:
                    self._wait(e, sem, cnt)
            for sem, cnt in snap_d:
                self._wait(e, sem, cnt)
        self.state = {}

    def finish(self, en="sp"):
        e = self.eng[en]
        for name, (sem, cnt) in self.dsem.items():
            if cnt:
                self._wait(e, sem, cnt)
        for o in self.eng.values():
            if o.count and o is not e:
                self._wait(e, o.sem, o.count)


def build_program(stop_at=None, dbg=False):
    nc = bass.Bass("TRN2", target_bir_lowering=False)

    def din(name, shape):
        return nc.dram_tensor(name, list(shape), F32, kind="ExternalInput").ap()

    x_d = din("x", [S, D])
    p_d = din("p", [2, S, 256])
    w_in_d = din("w_in", [2, D, 3584])
    w_glu_d = din("w_glu", [2, D, 1024])
    w_out_d = din("w_out", [2, D, D])
    w_pg_d = din("w_pg", [2, D, D])
    w_pp_d = din("w_pp", [2, 256, D])
    gmix_d = din("gmix", [2, 128, 8])
    gple_d = din("gple", [2, 128, 8])
    bglu_d = din("bglu", [2, 128, 8])
    dpad_d = din("dpad", [2, 128, 8])
    gq_d = din("gq", [2, 128, 1])
    gk_d = din("gk", [2, 128, 1])
    are_d = din("a_re", [2, 32, 64])
    aim_d = din("a_im", [2, 32, 64])
    ldt_d = din("logdt", [2, 32, 1])
    bre_d = din("b_re", [2, 32, 64, 16])
    bim_d = din("b_im", [2, 32, 64, 16])
    ccat_d = din("ccat", [2, 4, 128, 128])
    ccsw_d = din("ccatsw", [2, 4, 128, 128])
    out_d = nc.dram_tensor("out", [S, D], F32, kind="ExternalOutput").ap()
    wbf_d = nc.dram_tensor("wbf", [2, NPIECE, 128, 4096], BF16, kind="Internal").ap()
    yss_d = nc.dram_tensor("yss", [2, 4, 128, S], BF16, kind="Internal").ap()
    hmid_d = nc.dram_tensor("hmid", [S, D], F32, kind="Internal").ap()
    dbg_d = {}

    def dbg_out(name, shape):
        dbg_d[name] = nc.dram_tensor(name, list(shape), F32, kind="ExternalOutput").ap()
        return dbg_d[name]

    with ExitStack() as st:
        kb = KB(nc, st)
        op = kb.op

        pb = [kb.ps(f"pb{i}", [128, 512], F32) for i in range(8)]
        pbk = [f"pb{i}" for i in range(8)]
        identb = kb.sb("identb", [128, 128], BF16)
        identf = kb.sb("identf", [128, 128], F32)
        tri = kb.sb("tri", [128, 128], BF16)
        ones = kb.sb("ones", [128, 128], BF16)
        masklt = kb.sb("masklt", [128, 128], BF16)
        blockones = kb.sb("blockones", [128, 128], BF16)
        bdmask = kb.sb("bdmask", [128, 128], F32)
        NRING = 3
        ring = [kb.sb(f"ring{i}", [128, 4096], BF16) for i in range(NRING)]
        stage = [kb.sb(f"stage{i}", [128, 1024], F32) for i in range(2)]

        def mk_affine(t, key, pattern, cm, cmp_):
            op("pool", lambda e: e.memset(t[:], 1.0), writes=[key])
            op("pool", lambda e: e.affine_select(out=t[:], in_=t[:], pattern=pattern, compare_op=cmp_, fill=0.0,
                                                 base=0, channel_multiplier=cm), reads=[key], writes=[key])

        mk_affine(identb, "identb", [[-1, 128]], 1, ALU.is_equal)
        mk_affine(identf, "identf", [[-1, 128]], 1, ALU.is_equal)
        mk_affine(tri, "tri", [[-1, 128]], 1, ALU.is_ge)
        mk_affine(masklt, "masklt", [[1, 128]], -1, ALU.is_gt)
        op("pool", lambda e: e.memset(ones[:], 1.0), writes=["ones"])
        op("pool", lambda e: e.memset(blockones[:], 0.0), writes=["blockones"])
        for hh in range(2):
            op("pool", lambda e, hh=hh: e.memset(blockones[64 * hh:64 * hh + 64, 64 * hh:64 * hh + 64], 1.0),
               writes=["blockones"])
        op("pool", lambda e: e.memset(bdmask[:], 0.0), writes=["bdmask"])
        for gg in range(4):
            op("pool", lambda e, gg=gg: e.memset(bdmask[32 * gg:32 * gg + 32, 32 * gg:32 * gg + 32], 1.0),
               writes=["bdmask"])

        def piece_src(L, pid):
            if pid <= 6:
                return w_in_d[L, :, pid * 512:(pid + 1) * 512].rearrange("(k p) n -> p k n", p=128), 8, 512
            if pid <= 8:
                return w_glu_d[L, :, (pid - 7) * 512:(pid - 6) * 512].rearrange("(k p) n -> p k n", p=128), 8, 512
            if pid <= 10:
                return w_out_d[L, :, (pid - 9) * 512:(pid - 8) * 512].rearrange("(k p) n -> p k n", p=128), 8, 512
            if pid <= 12:
                return w_pg_d[L, :, (pid - 11) * 512:(pid - 10) * 512].rearrange("(k p) n -> p k n", p=128), 8, 512
            return w_pp_d[L].rearrange("(k p) n -> p k n", p=128), 2, 1024

        converted = set()
        ring_ctr = [0]

        class Pref:
            def __init__(self, L, pids, ceng="dve"):
                self.L = L
                self.pids = pids
                self.loaded = []
                self.ceng = ceng

            def _load(self, idx):
                L, pid = self.L, self.pids[idx]
                s = ring_ctr[0] % NRING
                ring_ctr[0] += 1
                key = f"ring{s}"
                src, nk, ncol = piece_src(L, pid)
                n = nk * ncol
                v3 = ring[s][:, 0:n].rearrange("p (k n) -> p k n", k=nk)
                if (L, pid) not in converted:
                    ck = max(1, 1024 // ncol)
                    nh = ck * ncol
                    for q_ in range(nk // ck):
                        hf = q_ % 2
                        sv = stage[hf][:, 0:nh].rearrange("p (k n) -> p k n", k=ck)
                        kb.dma("sp", sv, src[:, q_ * ck:(q_ + 1) * ck, :], f"ld_stage{hf}", writes=[f"stage{hf}"])
                        if self.ceng == "act":
                            op("act", lambda e: e.activation(out=ring[s][:, q_ * nh:(q_ + 1) * nh], in_=stage[hf][:, 0:nh],
                                                             func=AF.Copy), reads=[f"stage{hf}"], writes=[key])
                        else:
                            op("dve", lambda e: e.tensor_copy(out=ring[s][:, q_ * nh:(q_ + 1) * nh], in_=stage[hf][:, 0:nh]),
                               reads=[f"stage{hf}"], writes=[key])
                    kb.dma("sp", wbf_d[L, pid, :, 0:n], ring[s][:, 0:n], f"st_ring{s}",
                           reads=[key], writes=[f"wbf{L}_{pid}"])
                    converted.add((L, pid))
                else:
                    kb.dma("sp", ring[s][:, 0:n], wbf_d[L, pid, :, 0:n], f"ld_ring{s}",
                           reads=[f"wbf{L}_{pid}"], writes=[key])
                self.loaded.append((v3, key))

            def get(self, idx, ahead=2):
                while len(self.loaded) <= min(idx + ahead, len(self.pids) - 1):
                    self._load(len(self.loaded))
                return self.loaded[idx]

        def norm_phase(stk_bufs, src_rows, gain, tagp, seng):
            hsub, hnt, ss, lnv, rstd, sqj, hnT = stk_bufs
            for i in range(NSUB):
                hs = hsub[i % 2]
                hk = f"hsub{i % 2}"
                kb.dma("sp", hs[:], src_rows(i), f"ld_hsub{i % 2}", writes=[hk])
                op("act", lambda e: e.activation(out=sqj[:], in_=hs[:], func=AF.Square, accum_out=ss[:, i:i + 1]),
                   reads=[hk], writes=["sqj", f"ss{i}"])
                op("act", lambda e: e.activation(out=lnv[:, i:i + 1], in_=ss[:, i:i + 1], func=AF.Ln,
                                                 scale=1.0 / D, bias=EPS), reads=[f"ss{i}"], writes=[f"lnv{i}"])
                op("act", lambda e: e.activation(out=rstd[:, i:i + 1], in_=lnv[:, i:i + 1], func=AF.Exp, scale=-0.5),
                   reads=[f"lnv{i}"], writes=[f"rstd{i}"])
                ht = hnt[i % 2]
                tk = f"hnt{i % 2}"
                if seng == "act":
                    op("act", lambda e: e.activation(out=ht[:], in_=hs[:], func=AF.Copy, scale=rstd[:, i:i + 1]),
                       reads=[hk, f"rstd{i}"], writes=[tk])
                else:
                    op("dve", lambda e: e.tensor_scalar(out=ht[:], in0=hs[:], scalar1=rstd[:, i:i + 1], scalar2=None,
                                                        op0=ALU.mult), reads=[hk, f"rstd{i}"], writes=[tk])
                bk = i % 2
                pv = pb[bk][:].bitcast(BF16)
                for kc in range(8):
                    op("pe", lambda e, kc=kc: e.transpose(out=pv[:, kc * 128:(kc + 1) * 128],
                                                          in_=ht[:, kc * 128:(kc + 1) * 128], identity=identb[:]),
                       reads=[tk, "identb"], writes=[pbk[bk]], sig=(kc == 7))
                op("dve", lambda e: e.tensor_tensor(out=hnT[:, :, i * 128:(i + 1) * 128],
                                                    in0=pv.rearrange("p (k t) -> p k t", k=8),
                                                    in1=gain[:, :].unsqueeze(2).to_broadcast([128, 8, 128]),
                                                    op=ALU.mult),
                   reads=[pbk[bk], tagp], writes=["hnT"])

        def proj_fm(w3, wkey, col0, hnT, bank):
            for kc in range(8):
                op("pe", lambda e, kc=kc: e.matmul(pb[bank][:], lhsT=w3[:, kc, col0:col0 + 128], rhs=hnT[:, kc, :],
                                                   start=(kc == 0), stop=(kc == 7)),
                   reads=[wkey, "hnT"], writes=[pbk[bank]], sig=(kc == 7))

        for L in range(2):
            h_src = x_d if L == 0 else hmid_d
            h_dst = hmid_d if L == 0 else out_d

            with ExitStack() as lst:
                gmix = kb.sb(f"gmix{L}", [128, 8], F32, lst)
                gple = kb.sb(f"gple{L}", [128, 8], F32, lst)
                bglu = kb.sb(f"bglu{L}", [128, 8], F32, lst)
                dpad = kb.sb(f"dpad{L}", [128, 8], F32, lst)
                gq = kb.sb(f"gq{L}", [128, 1], F32, lst)
                gk = kb.sb(f"gk{L}", [128, 1], F32, lst)
                for t, d_, k_ in ((gmix, gmix_d, "gmix"), (gple, gple_d, "gple"), (bglu, bglu_d, "bglu"),
                                  (dpad, dpad_d, "dpad"), (gq, gq_d, "gq"), (gk, gk_d, "gk")):
                    kb.dma("sp", t[:], d_[L], "ld_small_" + k_, writes=[k_])
                op("dve", lambda e: e.tensor_scalar(out=gq[:], in0=gq[:], scalar1=0.125, scalar2=None, op0=ALU.mult),
                   reads=["gq"], writes=["gq"])

                with ExitStack() as s1:
                    T_tab = kb.sb("T_tab", [128, 8, 9, 128], BF16, s1)
                    WS_tab = kb.sb("WS_tab", [128, 8, 8, 128], BF16, s1)
                    WC_tab = kb.sb("WC_tab", [128, 32, 8, 32], BF16, s1)
                    AA1 = kb.sb("AA1", [128, 2, 32], F32, s1)
                    AA2 = kb.sb("AA2", [128, 2, 32], F32, s1)
                    Hh = kb.sb("Hh", [128, 2, 32, 65], F32, s1)
                    with ExitStack() as su:
                        def g32(name):
                            return kb.sb(name, [32, 64], F32, su)
                        are, aim, Lr, Li, mag, cc_, ss_, t1, t2, t3 = [g32(n) for n in
                                                                      ("are", "aim", "Lr", "Li", "mag", "cc_", "ss_", "t1", "t2", "t3")]
                        ldt = kb.sb("ldt", [32, 1], F32, su)
                        dtv = kb.sb("dtv", [32, 1], F32, su)
                        kb.dma("sp", are[:], are_d[L], "ld_are", writes=["are"])
                        kb.dma("sp", aim[:], aim_d[L], "ld_aim", writes=["aim"])
                        kb.dma("sp", ldt[:], ldt_d[L], "ld_ldt", writes=["ldt"])
                        op("act", lambda e: e.activation(out=dtv[:], in_=ldt[:], func=AF.Exp), reads=["ldt"], writes=["dtv"])

                        def dv(fn, reads, writes):
                            op("dve", fn, reads=reads, writes=writes)

                        def tt_(o, a, b, alu, ok, ak, bk_):
                            dv(lambda e: e.tensor_tensor(out=o, in0=a, in1=b, op=alu), [ak, bk_], [ok])

                        dv(lambda e: e.tensor_scalar(out=Lr[:], in0=are[:], scalar1=dtv[:, 0:1], scalar2=None, op0=ALU.mult),
                           ["are", "dtv"], ["Lr"])
                        dv(lambda e: e.tensor_scalar(out=Li[:], in0=aim[:], scalar1=dtv[:, 0:1], scalar2=None, op0=ALU.mult),
                           ["aim", "dtv"], ["Li"])
                        op("act", lambda e: e.activation(out=mag[:], in_=Lr[:], func=AF.Exp), reads=["Lr"], writes=["mag"])
                        op("act", lambda e: e.activation(out=ss_[:], in_=Li[:], func=AF.Sin, scale=1.0 / 32.0),
                           reads=["Li"], writes=["ss_"])
                        op("act", lambda e: e.activation(out=cc_[:], in_=Li[:], func=AF.Sin, scale=1.0 / 32.0,
                                                         bias=math.pi / 2), reads=["Li"], writes=["cc_"])
                        for _ in range(5):
                            tt_(t1[:], cc_[:], cc_[:], ALU.mult, "t1", "cc_", "cc_")
                            tt_(t2[:], ss_[:], ss_[:], ALU.mult, "t2", "ss_", "ss_")
                            tt_(t3[:], cc_[:], ss_[:], ALU.mult, "t3", "cc_", "ss_")
                            tt_(cc_[:], t1[:], t2[:], ALU.subtract, "cc_", "t1", "t2")
                            dv(lambda e: e.tensor_scalar(out=ss_[:], in0=t3[:], scalar1=2.0, scalar2=None, op0=ALU.mult),
                               ["t3"], ["ss_"])
                        Pre = kb.sb("Pre", [32, 9, 64], F32, su)
                        Pim = kb.sb("Pim", [32, 9, 64], F32, su)
                        tt_(Pre[:, 1, :], mag[:], cc_[:], ALU.mult, "P1r", "mag", "cc_")
                        tt_(Pim[:, 1, :], mag[:], ss_[:], ALU.mult, "P1i", "mag", "ss_")
                        tt_(t1[:], are[:], are[:], ALU.mult, "t1", "are", "are")
                        tt_(t2[:], aim[:], aim[:], ALU.mult, "t2", "aim", "aim")
                        tt_(t1[:], t1[:], t2[:], ALU.add, "t1", "t1", "t2")
                        dv(lambda e: e.reciprocal(out=t3[:], in_=t1[:]), ["t1"], ["t3"])
                        dv(lambda e: e.tensor_scalar(out=t1[:], in0=Pre[:, 1, :], scalar1=-1.0, scalar2=None, op0=ALU.add),
                           ["P1r"], ["t1"])
                        tt_(t2[:], t1[:], are[:], ALU.mult, "t2", "t1", "are")
                        tt_(mag[:], Pim[:, 1, :], aim[:], ALU.mult, "mag", "P1i", "aim")
                        tt_(t2[:], t2[:], mag[:], ALU.add, "t2", "t2", "mag")
                        tt_(Pre[:, 0, :], t2[:], t3[:], ALU.mult, "P0r", "t2", "t3")
                        tt_(t2[:], Pim[:, 1, :], are[:], ALU.mult, "t2", "P1i", "are")
                        tt_(mag[:], t1[:], aim[:], ALU.mult, "mag", "t1", "aim")
                        tt_(t2[:], t2[:], mag[:], ALU.subtract, "t2", "t2", "mag")
                        tt_(Pim[:, 0, :], t2[:], t3[:], ALU.mult, "P0i", "t2", "t3")
                        for k in range(1, 8):
                            a, b_ = f"P{k}r", f"P{k}i"
                            tt_(t1[:], Pre[:, k, :], Pre[:, 1, :], ALU.mult, "t1", a, "P1r")
                            tt_(t2[:], Pim[:, k, :], Pim[:, 1, :], ALU.mult, "t2", b_, "P1i")
                            tt_(Pre[:, k + 1, :], t1[:], t2[:], ALU.subtract, f"P{k + 1}r", "t1", "t2")
                            tt_(t1[:], Pre[:, k, :], Pim[:, 1, :], ALU.mult, "t1", a, "P1i")
                            tt_(t2[:], Pim[:, k, :], Pre[:, 1, :], ALU.mult, "t2", b_, "P1r")
                            tt_(Pim[:, k + 1, :], t1[:], t2[:], ALU.add, f"P{k + 1}i", "t1", "t2")
                        A12 = kb.sb("A12", [128, 9, 2, 32], F32, su)
                        cat = kb.sb("cat", [32, 2, 128], F32, su)
                        for k in range(9):
                            rk, ik = f"P{k}r", f"P{k}i"
                            dv(lambda e, k=k: e.tensor_copy(out=cat[:, 0, 0:64], in_=Pre[:, k, :]), [rk], ["cat"])
                            dv(lambda e, k=k: e.tensor_copy(out=cat[:, 0, 64:128], in_=Pre[:, k, :]), [rk], ["cat"])
                            dv(lambda e, k=k: e.tensor_scalar(out=cat[:, 1, 0:64], in0=Pim[:, k, :], scalar1=-1.0,
                                                              scalar2=None, op0=ALU.mult), [ik], ["cat"])
                            dv(lambda e, k=k: e.tensor_copy(out=cat[:, 1, 64:128], in_=Pim[:, k, :]), [ik], ["cat"])
                            for w_ in range(2):
                                op("pe", lambda e, w_=w_: e.transpose(out=pb[0][:, w_ * 32:(w_ + 1) * 32], in_=cat[:, w_, :],
                                                                      identity=identf[0:32, 0:32]),
                                   reads=["cat", "identf"], writes=["pb0"])
                            dv(lambda e, k=k: e.tensor_copy(out=A12[:, k, :, :],
                                                            in_=pb[0][:, 0:64].rearrange("p (a b) -> p a b", a=2)),
                               ["pb0"], [f"A12_{k}"])
                        for w_ in range(2):
                            dv(lambda e, w_=w_: e.tensor_copy(
                                out=AA1[:, w_, :].rearrange("p (a b) -> p a b", a=4),
                                in_=A12[:, 8, 0, :].rearrange("p (b a) -> p a b", a=4)), ["A12_8"], ["AA1"])
                        dv(lambda e: e.tensor_copy(out=AA2[:, 0, :].rearrange("p (a b) -> p a b", a=4),
                                                   in_=A12[:, 8, 1, :].rearrange("p (b a) -> p a b", a=4)), ["A12_8"], ["AA2"])
                        dv(lambda e: e.tensor_scalar(out=AA2[:, 1, :].rearrange("p (a b) -> p a b", a=4),
                                                     in0=A12[:, 8, 1, :].rearrange("p (b a) -> p a b", a=4),
                                                     scalar1=-1.0, scalar2=None, op0=ALU.mult), ["A12_8"], ["AA2"])
                        Braw = kb.sb("Braw", [128, 32, 16], F32, su)
                        Brsw = kb.sb("Brsw", [128, 32, 16], F32, su)
                        Bm = kb.sb("Bm", [128, 32, 16], F32, su)
                        Bsw = kb.sb("Bsw", [128, 32, 16], F32, su)
                        tmpA = kb.sb("tmpA", [128, 32, 16], F32, su)
                        tmpB = kb.sb("tmpB", [128, 32, 16], F32, su)
                        bre_v = bre_d[L].rearrange("g p h -> p g h")
                        bim_v = bim_d[L].rearrange("g p h -> p g h")
                        kb.dma("sp", Braw[0:64], bre_v, "ld_b0", writes=["Braw"])
                        kb.dma("sp", Braw[64:128], bim_v, "ld_b1", writes=["Braw"])
                        kb.dma("sp", Brsw[0:64], bim_v, "ld_b2", writes=["Brsw"])
                        kb.dma("sp", Brsw[64:128], bre_v, "ld_b3", writes=["Brsw"])

                        def bc(k, w_):
                            return A12[:, k, w_, :].unsqueeze(2).to_broadcast([128, 32, 16])

                        tt_(tmpA[:], Braw[:], bc(0, 0), ALU.mult, "tmpA", "Braw", "A12_0")
                        tt_(tmpB[:], Brsw[:], bc(0, 1), ALU.mult, "tmpB", "Brsw", "A12_0")
                        tt_(Bm[:], tmpA[:], tmpB[:], ALU.add, "Bm", "tmpA", "tmpB")
                        tt_(tmpA[:], Brsw[:], bc(0, 0), ALU.mult, "tmpA", "Brsw", "A12_0")
                        tt_(tmpB[:], Braw[:], bc(0, 1), ALU.mult, "tmpB", "Braw", "A12_0")
                        tt_(Bsw[:], tmpA[:], tmpB[:], ALU.subtract, "Bsw", "tmpA", "tmpB")
                        Cm = kb.sb("Cm", [128, 32, 16], F32, su)
                        Cmsw = kb.sb("Cmsw", [128, 32, 16], F32, su)
                        ccs = kb.sb("ccs", [128, 128], F32, su)
                        for (dst, dkey, srcd, neg_lo) in ((Cm, "Cm", ccat_d, False), (Cmsw, "Cmsw", ccsw_d, True)):
                            for c in range(4):
                                kb.dma("sp", ccs[:], srcd[L, c], "ld_ccs", writes=["ccs"])
                                op("pe", lambda e: e.transpose(out=pb[1][:, 0:128], in_=ccs[:], identity=identf[:]),
                                   reads=["ccs", "identf"], writes=["pb1"])
                                lo = pb[1][0:64, 0:128].rearrange("p (a b) -> p a b", a=8)
                                hi = pb[1][64:128, 0:128].rearrange("p (a b) -> p a b", a=8)
                                dlo = dst[0:64, 8 * c:8 * c + 8, :]
                                dhi = dst[64:128, 8 * c:8 * c + 8, :]
                                if neg_lo:
                                    dv(lambda e, lo=lo, dlo=dlo: e.tensor_scalar(out=dlo, in0=lo, scalar1=-1.0, scalar2=None,
                                                                                 op0=ALU.mult), ["pb1"], [dkey])
                                    dv(lambda e, hi=hi, dhi=dhi: e.tensor_copy(out=dhi, in_=hi), ["pb1"], [dkey])
                                else:
                                    dv(lambda e, lo=lo, dlo=dlo: e.tensor_copy(out=dlo, in_=lo), ["pb1"], [dkey])
                                    dv(lambda e, hi=hi, dhi=dhi: e.tensor_scalar(out=dhi, in0=hi, scalar1=-1.0, scalar2=None,
                                                                                 op0=ALU.mult), ["pb1"], [dkey])
                        T0f = kb.sb("T0f", [128, 8, 128], F32, su)
                        Cpad = kb.sb("Cpad", [128, 32, 32], BF16, su)
                        ABpad = kb.sb("ABpad", [128, 32, 32], BF16, su)
                        op("dve", lambda e: e.memset(Cpad[:], 0.0), writes=["Cpad"])
                        op("dve", lambda e: e.memset(ABpad[:], 0.0), writes=["ABpad"])
                        op("dve", lambda e: e.memset(WC_tab[:], 0.0), writes=["WC"])
                        dv(lambda e: e.tensor_copy(out=Cpad[:, :, 0:16], in_=Cm[:]), ["Cm"], ["Cpad"])
                        for tau in range(8):
                            if tau == 0:
                                dv(lambda e: e.tensor_copy(out=ABpad[:, :, 0:16], in_=Bm[:]), ["Bm"], ["ABpad"])
                            else:
                                tt_(tmpA[:], Bm[:], bc(tau, 0), ALU.mult, "tmpA", "Bm", f"A12_{tau}")
                                tt_(tmpB[:], Bsw[:], bc(tau, 1), ALU.mult, "tmpB", "Bsw", f"A12_{tau}")
                                tt_(ABpad[:, :, 0:16], tmpA[:], tmpB[:], ALU.add, "ABpad", "tmpA", "tmpB")
                            pv = pb[2][:].bitcast(BF16)
                            for pc in range(8):
                                op("pe", lambda e, pc=pc: e.transpose(
                                    out=pv[:, pc * 128:(pc + 1) * 128],
                                    in_=ABpad[:, 4 * pc:4 * pc + 4, :].rearrange("p a b -> p (a b)"), identity=identb[:]),
                                   reads=["ABpad", "identb"], writes=["pb2"])
                            dv(lambda e, tau=tau: e.tensor_copy(out=WS_tab[:, :, 7 - tau, :],
                                                                in_=pv.rearrange("p (a b) -> p a b", a=8)), ["pb2"], ["WS"])
                            for pc in range(8):
                                bk = 3 + pc // 4
                                op("pe", lambda e, pc=pc, bk=bk: e.matmul(
                                    pb[bk][:, (pc % 4) * 128:(pc % 4 + 1) * 128],
                                    lhsT=ABpad[:, 4 * pc:4 * pc + 4, :].rearrange("p a b -> p (a b)"),
                                    rhs=Cpad[:, 4 * pc:4 * pc + 4, :].rearrange("p a b -> p (a b)"), start=True, stop=True),
                                   reads=["ABpad", "Cpad"], writes=[pbk[bk]])
                            for hf in range(2):
                                if tau > 0:
                                    dv(lambda e, hf=hf, tau=tau: e.tensor_tensor(
                                        out=T_tab[:, 4 * hf:4 * hf + 4, tau, :],
                                        in0=pb[3 + hf][:].rearrange("p (a b) -> p a b", a=4),
                                        in1=bdmask[:].unsqueeze(1).to_broadcast([128, 4, 128]), op=ALU.mult),
                                       [pbk[3 + hf], "bdmask"], ["T"])
                                else:
                                    dv(lambda e, hf=hf: e.tensor_tensor(
                                        out=T0f[:, 4 * hf:4 * hf + 4, :],
                                        in0=pb[3 + hf][:].rearrange("p (a b) -> p a b", a=4),
                                        in1=bdmask[:].unsqueeze(1).to_broadcast([128, 4, 128]), op=ALU.mult),
                                       [pbk[3 + hf], "bdmask"], ["T0f"])
                            if tau == 0:
                                for pc in range(8):
                                    dv(lambda e, pc=pc: e.scalar_tensor_tensor(
                                        out=T0f[:, pc, :], in0=identf[:], scalar=dpad[:, pc:pc + 1], in1=T0f[:, pc, :],
                                        op0=ALU.mult, op1=ALU.add), ["identf", "dpad", "T0f"], ["T0f"])
                                dv(lambda e: e.tensor_copy(out=T_tab[:, :, 0, :], in_=T0f[:]), ["T0f"], ["T"])
                                dv(lambda e: e.tensor_tensor(out=T0f[:], in0=T0f[:], in1=T_tab[:, :, 0, :], op=ALU.subtract),
                                   ["T0f", "T"], ["T0f"])
                                dv(lambda e: e.tensor_copy(out=T_tab[:, :, 8, :], in_=T0f[:]), ["T0f"], ["T"])
                        for lp in range(8):
                            k = lp + 1
                            tt_(tmpA[:], Cm[:], bc(k, 0), ALU.mult, "tmpA", "Cm", f"A12_{k}")
                            tt_(tmpB[:], Cmsw[:], bc(k, 1), ALU.mult, "tmpB", "Cmsw", f"A12_{k}")
                            tt_(WC_tab[:, :, lp, 0:16], tmpA[:], tmpB[:], ALU.subtract, "WC", "tmpA", "tmpB")
                        op("dve", lambda e: e.memset(Hh[:, :, :, 0:1], 0.0), writes=["Hh"])
                        kb.barrier()
                    hsub = [kb.sb(f"hsub{i}", [128, D], F32, s1) for i in range(2)]
                    hnt = [kb.sb(f"hnt{i}", [128, D], BF16, s1) for i in range(2)]
                    ss = kb.sb("ssq", [128, 4], F32, s1)
                    lnv = kb.sb("lnv", [128, 4], F32, s1)
                    rstd = kb.sb("rstd", [128, 4], F32, s1)
                    sqj = kb.sb("sqj", [128, D], BF16, s1)
                    hnT = kb.sb("hnT", [128, 8, TT], BF16, s1)
                    nb = (hsub, hnt, ss, lnv, rstd, sqj, hnT)
                    uTp2 = [kb.sb(f"uTp{i}", [128, 8, TT], BF16, s1) for i in range(2)]
                    sgs2 = [kb.sb(f"sgs{i}", [128, 4, TT], BF16, s1) for i in range(2)]
                    Hprev2 = [kb.sb(f"Hprev{i}", [128, 32, 64], BF16, s1) for i in range(2)]
                    zTp = kb.sb("zTp", [128, 8, TT], BF16, s1)
                    ysT = kb.sb("ysT", [128, 4, TT], BF16, s1)
                    SS = kb.sb("SS", [128, 2, 32, 64], F32, s1)
                    rt1 = kb.sb("rt1", [128, 2, 32], F32, s1)
                    rt2 = kb.sb("rt2", [128, 2, 32], F32, s1)
                    Ysb = [kb.sb(f"Ysb{i}", [64, 8, 4, 32], BF16, s1) for i in range(2)]
                    sgt = [kb.sb(f"sgt{i}", [128, TT], F32, s1) for i in range(2)]
                    vat = [kb.sb(f"vat{i}", [128, TT], F32, s1) for i in range(2)]
                    hh_ap = Hh[:]
                    pstep = list(hh_ap.ap[0])

                    def stage_X(j, pf):
                        uTp, sgs, Hprev = uTp2[j % 2], sgs2[j % 2], Hprev2[j % 2]
                        uk, sk, hk_ = f"uTp{j % 2}", f"sgs{j % 2}", f"Hprev{j % 2}"
                        norm_phase(nb, lambda i: h_src[j * TT + i * 128:j * TT + (i + 1) * 128, :], gmix, "gmix", "act")
                        for half in range(2):
                            w3, wk = pf.get(half)
                            for q4 in range(4):
                                pc = half * 4 + q4
                                bank = 2 + pc % 2
                                proj_fm(w3, wk, q4 * 128, hnT, bank)
                                op("act", lambda e: e.activation(out=uTp[:, pc, :].rearrange("p (l s) -> p l s", l=8),
                                                                 in_=pb[bank][:].rearrange("p (s l) -> p l s", l=8), func=AF.Copy),
                                   reads=[pbk[bank]], writes=[uk])
                        w3, wk = pf.get(2)
                        for c in range(4):
                            bank = 2 + c % 2
                            proj_fm(w3, wk, c * 128, hnT, bank)
                            op("act", lambda e: e.activation(out=sgs[:, c, :].rearrange("p (l s) -> p l s", l=8),
                                                             in_=pb[bank][:].rearrange("p (s l) -> p l s", l=8), func=AF.Silu),
                               reads=[pbk[bank]], writes=[sk])
                        for pc in range(8):
                            for l in range(8):
                                for gg in range(4):
                                    op("pe", lambda e: e.matmul(
                                        pb[4 + gg][:, pc * 64:(pc + 1) * 64],
                                        lhsT=WS_tab[32 * gg:32 * gg + 32, pc, l, :],
                                        rhs=uTp[32 * gg:32 * gg + 32, pc, l * 64:(l + 1) * 64],
                                        start=(l == 0), stop=(l == 7), tile_position=(32 * gg, 0)),
                                       reads=["WS", uk], writes=[pbk[4 + gg]], sig=(l == 7))
                        for gg in range(4):
                            op("dve", lambda e: e.tensor_copy(out=SS[:, 0, 8 * gg:8 * gg + 8, :].rearrange("p a b -> p (a b)"),
                                                              in_=pb[4 + gg][:]), reads=[pbk[4 + gg]], writes=["SS0"])
                        op("act", lambda e: e.activation(out=SS[0:64, 1, :, :], in_=SS[64:128, 0, :, :], func=AF.Copy),
                           reads=["SS0"], writes=["SS1"])
                        op("act", lambda e: e.activation(out=SS[64:128, 1, :, :], in_=SS[0:64, 0, :, :], func=AF.Copy),
                           reads=["SS0"], writes=["SS1"])
                        for sc in range(64):
                            cur = Hh[:, :, :, sc]
                            swp = bass.AP(hh_ap.tensor, hh_ap.offset + 32 * 65 + sc, [pstep, [-32 * 65, 2], [65, 32]])
                            op("dve", lambda e: e.tensor_tensor(out=rt1[:], in0=cur, in1=AA1[:], op=ALU.mult),
                               reads=["Hh", "AA1"], writes=["rt1"])
                            op("dve", lambda e: e.tensor_tensor(out=rt2[:], in0=swp, in1=AA2[:], op=ALU.mult),
                               reads=["Hh", "AA2"], writes=["rt2"])
                            op("dve", lambda e: e.tensor_tensor(out=rt1[:], in0=rt1[:], in1=rt2[:], op=ALU.add),
                               reads=["rt1", "rt2"], writes=["rt1"])
                            op("dve", lambda e: e.tensor_tensor(out=Hh[:, :, :, sc + 1], in0=rt1[:], in1=SS[:, :, :, sc],
                                                                op=ALU.add), reads=["rt1", "SS0", "SS1"], writes=["Hh"])
                        op("dve", lambda e: e.tensor_copy(out=Hprev[:], in_=Hh[:, 0, :, 0:64]), reads=["Hh"], writes=[hk_])
                        op("dve", lambda e: e.tensor_copy(out=Hh[:, :, :, 0:1], in_=Hh[:, :, :, 64:65]), reads=["Hh"], writes=["Hh"])

                    def stage_Y(j, pf, i0):
                        uTp, sgs, Hprev = uTp2[j % 2], sgs2[j % 2], Hprev2[j % 2]
                        uk, sk, hk_ = f"uTp{j % 2}", f"sgs{j % 2}", f"Hprev{j % 2}"

                        def yint(pc):
                            yb, ykey = Ysb[pc % 2], f"Ysb{pc % 2}"
                            for gg in range(4):
                                gi, gn, bank = gg * 8 + pc, 4 * pc + gg, gg // 2
                                op("pe", lambda e: e.matmul(
                                    pb[bank][0:64, (gg % 2) * 256:(gg % 2 + 1) * 256],
                                    lhsT=Hprev[:, gi, :], rhs=WC_tab[:, gn, :, :].rearrange("p a b -> p (a b)"),
                                    start=True, stop=True), reads=[hk_, "WC"], writes=[pbk[bank]])
                            for bank in range(2):
                                op("act", lambda e: e.activation(
                                    out=yb[:, :, 2 * bank:2 * bank + 2, :],
                                    in_=pb[bank][0:64, :].rearrange("p (g l h) -> p l g h", g=2, l=8),
                                    func=AF.Copy), reads=[pbk[bank]], writes=[ykey])

                        yint(0)
                        for pc in range(8):
                            if pc + 1 < 8:
                                yint(pc + 1)
                            yb, ykey = Ysb[pc % 2], f"Ysb{pc % 2}"
                            ybank = 2 + pc % 2
                            op("pe", lambda e: e.matmul(pb[ybank][:], lhsT=T_tab[:, pc, 0, :], rhs=uTp[:, pc, :], start=True, stop=False),
                               reads=["T", uk], writes=[pbk[ybank]], sig=False)
                            op("pe", lambda e: e.matmul(pb[ybank][:], lhsT=T_tab[:, pc, 8, :], rhs=uTp[:, pc, :], start=False, stop=False),
                               reads=["T", uk], writes=[pbk[ybank]], sig=False)
                            for tau in range(1, 8):
                                op("pe", lambda e: e.matmul(pb[ybank][:, tau * 64:TT], lhsT=T_tab[:, pc, tau, :],
                                                            rhs=uTp[:, pc, 0:(8 - tau) * 64], start=False, stop=False),
                                   reads=["T", uk], writes=[pbk[ybank]], sig=False)
                            for lp in range(8):
                                op("pe", lambda e: e.matmul(
                                    pb[ybank][:, lp * 64:(lp + 1) * 64], lhsT=yb[:, lp, :, :].rearrange("p g h -> p (g h)"),
                                    rhs=identb[0:64, 0:64], start=False, stop=(lp == 7)),
                                   reads=[ykey, "identb"], writes=[pbk[ybank]], sig=(lp == 7))
                            op("act", lambda e: e.activation(out=zTp[:, pc, :], in_=pb[ybank][:], func=AF.Gelu_apprx_tanh),
                               reads=[pbk[ybank]], writes=["zTp"])
                        wv3, wvk = pf.get(i0)
                        wg3, wgk = pf.get(i0 + 1)
                        for c in range(4):
                            bv_, bg_ = 4 + 2 * (c % 2), 5 + 2 * (c % 2)
                            for (w3, wk, bank) in ((wv3, wvk, bv_), (wg3, wgk, bg_)):
                                for kc in range(8):
                                    op("pe", lambda e: e.matmul(
                                        pb[bank][:], lhsT=w3[:, kc, c * 128:(c + 1) * 128], rhs=zTp[:, kc, :],
                                        start=(kc == 0), stop=(kc == 7)), reads=[wk, "zTp"], writes=[pbk[bank]], sig=(kc == 7))
                            sg_, va_ = sgt[c % 2], vat[c % 2]
                            op("act", lambda e: e.activation(out=sg_[:], in_=pb[bg_][:], func=AF.Sigmoid, bias=bglu[:, 4 + c:5 + c]),
                               reads=[pbk[bg_], "bglu"], writes=[f"sgt{c % 2}"])
                            op("act", lambda e: e.activation(out=va_[:], in_=pb[bv_][:], func=AF.Identity, bias=bglu[:, c:c + 1]),
                               reads=[pbk[bv_], "bglu"], writes=[f"vat{c % 2}"])
                            op("pool", lambda e: e.tensor_tensor(out=va_[:], in0=va_[:], in1=sg_[:], op=ALU.mult),
                               reads=[f"vat{c % 2}", f"sgt{c % 2}"], writes=[f"vat{c % 2}"])
                            op("pool", lambda e: e.tensor_tensor(out=ysT[:, c, :].rearrange("p (s l) -> p l s", l=8),
                                                                 in0=va_[:].rearrange("p (l s) -> p l s", l=8),
                                                                 in1=sgs[:, c, :].rearrange("p (l s) -> p l s", l=8), op=ALU.mult),
                               reads=[f"vat{c % 2}", sk], writes=["ysT"])
                        kb.dma("sp", yss_d[L, :, :, j * TT:(j + 1) * TT].rearrange("c p t -> p c t"), ysT[:], "st_yss",
                               reads=["ysT"])

                    for j in range(NT + 1):
                        pids = ([0, 1, 2] if j < NT else []) + ([7, 8] if j >= 1 else [])
                        pf = Pref(L, pids, "act")
                        if j < NT:
                            stage_X(j, pf)
                        if j >= 1:
                            stage_Y(j - 1, pf, 3 if j < NT else 0)
                    kb.barrier()
                if stop_at == f"s1_{L}":
                    dd = nc.dram_tensor("dbg_yss", [4, 128, S], BF16, kind="ExternalOutput").ap()
                    kb.dma("sp", dd, yss_d[L], "st_dbgyss")
                    break

                with ExitStack() as s2:
                    kT = kb.sb("kT", [128, 4, S], BF16, s2)
                    Vc = kb.sb("Vc", [128, S // 128, 512], BF16, s2)
                    htile = kb.sb("htile", [128, NSUB, D], F32, s2)
                    hsub = [htile[:, 0, :], htile[:, 1, :]]
                    hnt = [kb.sb(f"hnt{i}", [128, D], BF16, s2) for i in range(2)]
                    ss = kb.sb("ssq", [128, 4], F32, s2)
                    lnv = kb.sb("lnv", [128, 4], F32, s2)
                    rstd = kb.sb("rstd", [128, 4], F32, s2)
                    sqj = kb.sb("sqj", [128, D], BF16, s2)
                    hnT = kb.sb("hnT", [128, 8, TT], BF16, s2)
                    qT = kb.sb("qT", [128, 4, TT], BF16, s2)
                    sga = kb.sb("sga", [128, 4, TT], BF16, s2)
                    yT = kb.sb("yT", [128, 8, TT], BF16, s2)
                    sq = [kb.sb(f"sq{i}", [128, TT], BF16, s2) for i in range(2)]
                    sd = [kb.sb(f"sd{i}", [128, TT], F32, s2) for i in range(2)]
                    e_sb = [kb.sb(f"e_sb{i}", [128, TT], BF16, s2) for i in range(4)]
                    sp_sb = [kb.sb(f"sp_sb{i}", [128, TT], BF16, s2) for i in range(3)]
                    x_sb = [kb.sb(f"x_sb{i}", [128, TT], BF16, s2) for i in range(2)]
                    w_sb = [kb.sb(f"w_sb{i}", [128, TT], BF16, s2) for i in range(2)]
                    S_sb = [kb.sb(f"S_sb{i}", [128, TT], BF16, s2) for i in range(3)]
                    psub = [kb.sb(f"psub{i}", [128, 256], F32, s2) for i in range(2)]
                    pbf = [kb.sb(f"pbf{i}", [128, 256], BF16, s2) for i in range(2)]
                    pT = kb.sb("pT", [128, 2, TT], BF16, s2)
                    gsb = [kb.sb(f"gsb{i}", [128, TT], F32, s2) for i in range(2)]
                    gpp = [kb.sb(f"gpp{i}", [128, TT], F32, s2) for i in range(2)]

                    class _Sub:
                        pass

                    tctr = [0]
                    HT = ["htile", "hsub0", "hsub1"]
                    for j in range(NT):
                        pf = Pref(L, [3, 4, 5, 6, 9, 10, 11, 13, 12], "dve")
                        nb = (hsub, hnt, ss, lnv, rstd, sqj, hnT)
                        norm_phase(nb, lambda i: h_src[j * TT + i * 128:j * TT + (i + 1) * 128, :], gmix, "gmix", "dve")
                        for which in range(2):
                            w3, wk = pf.get(which)
                            gvec = gq if which == 0 else gk
                            gkey = "gq" if which == 0 else "gk"
                            for c in range(4):
                                bank = 2 + c % 2
                                proj_fm(w3, wk, c * 128, hnT, bank)
                                sq_, sd_ = sq[c % 2], sd[c % 2]
                                op("act", lambda e, sq_=sq_, bank=bank: e.activation(out=sq_[:], in_=pb[bank][:], func=AF.Square),
                                   reads=[pbk[bank]], writes=[f"sq{c % 2}"])
                                sbank = 4 + c % 2
                                op("pe", lambda e, sq_=sq_, sbank=sbank: e.matmul(pb[sbank][:], lhsT=blockones[:], rhs=sq_[:],
                                                                                   start=True, stop=True),
                                   reads=[f"sq{c % 2}", "blockones"], writes=[pbk[sbank]])
                                op("act", lambda e, sd_=sd_, sbank=sbank: e.activation(out=sd_[:], in_=pb[sbank][:], func=AF.Ln,
                                                                                      scale=1.0 / 64.0, bias=EPS),
                                   reads=[pbk[sbank]], writes=[f"sd{c % 2}"])
                                op("act", lambda e, sd_=sd_: e.activation(out=sd_[:], in_=sd_[:], func=AF.Exp, scale=-0.5),
                                   reads=[f"sd{c % 2}"], writes=[f"sd{c % 2}"])
                                dst = qT[:, c, :] if which == 0 else kT[:, c, j * TT:(j + 1) * TT]
                                op("dve", lambda e, dst=dst, bank=bank, sd_=sd_, gvec=gvec: e.scalar_tensor_tensor(
                                    out=dst, in0=pb[bank][:], scalar=gvec[:, 0:1], in1=sd_[:], op0=ALU.mult, op1=ALU.mult),
                                   reads=[pbk[bank], gkey, f"sd{c % 2}"], writes=["qT" if which == 0 else "kT"])
                        w3, wk = pf.get(2)
                        for i in range(NSUB):
                            bank = 2 + i % 2
                            for kc in range(8):
                                op("pe", lambda e, kc=kc, i=i, bank=bank: e.matmul(
                                    pb[bank][:], lhsT=hnT[:, kc, i * 128:(i + 1) * 128], rhs=w3[:, kc, :],
                                    start=(kc == 0), stop=(kc == 7)), reads=[wk, "hnT"], writes=[pbk[bank]], sig=(kc == 7))
                            op("dve", lambda e, i=i, bank=bank: e.tensor_copy(out=Vc[:, 4 * j + i, :], in_=pb[bank][:]),
                               reads=[pbk[bank]], writes=["Vc"])
                        w3, wk = pf.get(3)
                        for c in range(4):
                            bank = 2 + c % 2
                            proj_fm(w3, wk, c * 128, hnT, bank)
                            op("act", lambda e, c=c, bank=bank: e.activation(out=sga[:, c, :], in_=pb[bank][:], func=AF.Silu),
                               reads=[pbk[bank]], writes=["sga"])
                        tiles = []
                        for h in range(8):
                            nblk = 4 * j + 4
                            for kb_ in range(nblk - 1, -1, -1):
                                diag = kb_ >= 4 * j
                                tiles.append(dict(h=h, c=h // 2, base=64 * (h % 2), obank=6 + h % 2, kb=kb_, diag=diag,
                                                  qoff=(kb_ - 4 * j) * 128 if diag else 0,
                                                  first=(kb_ == nblk - 1), last=(kb_ == 0)))
                        s_idx = 0
                        for t_ in tiles:
                            if t_["first"]:
                                t_["s_in"], t_["s_off"] = None, None
                            else:
                                t_["s_in"] = s_idx
                                t_["s_off"] = (t_["qoff"] + 128) if t_["diag"] else 0
                            if not t_["last"]:
                                s_idx = (s_idx + 1) % 3
                                t_["s_out"] = s_idx
                            else:
                                t_["s_out"] = None
                        NTL = len(tiles)

                        def st_QK(i):
                            t_ = tiles[i]
                            g_ = tctr[0] + i
                            zb = g_ % 3
                            qo, c, base, kb_ = t_["qoff"], t_["c"], t_["base"], t_["kb"]
                            op("pe", lambda e: e.matmul(pb[zb][:, qo:TT], lhsT=kT[base:base + 64, c, kb_ * 128:(kb_ + 1) * 128],
                                                        rhs=qT[base:base + 64, c, qo:TT], start=True, stop=True),
                               reads=["kT", "qT"], writes=[pbk[zb]])

                        def st_A1(i):
                            t_ = tiles[i]
                            g_ = tctr[0] + i
                            zb, e_, ek = g_ % 3, e_sb[g_ % 4], f"e_sb{g_ % 4}"
                            qo = t_["qoff"]
                            op("act", lambda e: e.activation(out=e_[:, qo:TT], in_=pb[zb][:, qo:TT], func=AF.Exp),
                               reads=[pbk[zb]], writes=[ek])
                            if t_["diag"]:
                                op("pool", lambda e: e.tensor_tensor(out=e_[:, qo:qo + 128], in0=e_[:, qo:qo + 128],
                                                                     in1=masklt[:], op=ALU.mult), reads=[ek, "masklt"], writes=[ek])

                        def st_A2(i):
                            t_ = tiles[i]
                            g_ = tctr[0] + i
                            e_, ek = e_sb[g_ % 4], f"e_sb{g_ % 4}"
                            sp_, spk = sp_sb[g_ % 3], f"sp_sb{g_ % 3}"
                            qo = t_["qoff"]
                            op("act", lambda e: e.activation(out=sp_[:, qo:TT], in_=e_[:, qo:TT], func=AF.Ln, bias=1.0),
                               reads=[ek], writes=[spk])
                            if t_["s_out"] is not None:
                                sn_t, sn_k = S_sb[t_["s_out"]], f"S_sb{t_['s_out']}"
                                if t_["s_in"] is None:
                                    op("dve", lambda e: e.tensor_copy(out=sn_t[:, qo:TT], in_=sp_[:, qo:TT]), reads=[spk], writes=[sn_k])
                                else:
                                    sc_t, sc_k = S_sb[t_["s_in"]], f"S_sb{t_['s_in']}"
                                    a0 = 0
                                    if t_["diag"]:
                                        op("dve", lambda e: e.tensor_copy(out=sn_t[:, qo:qo + 128], in_=sp_[:, qo:qo + 128]),
                                           reads=[spk], writes=[sn_k])
                                        a0 = qo + 128
                                    op("dve", lambda e: e.tensor_tensor(out=sn_t[:, a0:TT], in0=sc_t[:, a0:TT], in1=sp_[:, a0:TT],
                                                                        op=ALU.add), reads=[sc_k, spk], writes=[sn_k])

                        def st_B(i):
                            t_ = tiles[i]
                            g_ = tctr[0] + i
                            cb = 3 + g_ % 2
                            sp_, spk = sp_sb[g_ % 3], f"sp_sb{g_ % 3}"
                            qo = t_["qoff"]
                            has_s = t_["s_in"] is not None and t_["s_off"] < TT
                            op("pe", lambda e: e.matmul(pb[cb][:, qo:TT], lhsT=tri[:], rhs=sp_[:, qo:TT], start=True, stop=not has_s),
                               reads=["tri", spk], writes=[pbk[cb]])
                            if has_s:
                                sc_t, sc_k, so = S_sb[t_["s_in"]], f"S_sb{t_['s_in']}", t_["s_off"]
                                op("pe", lambda e: e.matmul(pb[cb][:, so:TT], lhsT=ones[:], rhs=sc_t[:, so:TT], start=False, stop=True),
                                   reads=["ones", sc_k], writes=[pbk[cb]])

                        def st_C(i):
                            t_ = tiles[i]
                            g_ = tctr[0] + i
                            cb = 3 + g_ % 2
                            e_, ek = e_sb[g_ % 4], f"e_sb{g_ % 4}"
                            x_, xk = x_sb[g_ % 2], f"x_sb{g_ % 2}"
                            w_, wk_ = w_sb[g_ % 2], f"w_sb{g_ % 2}"
                            qo, h, c, base, ob = t_["qoff"], t_["h"], t_["c"], t_["base"], t_["obank"]
                            op("act", lambda e: e.activation(out=x_[:, qo:TT], in_=pb[cb][:, qo:TT], func=AF.Exp, scale=-1.0),
                               reads=[pbk[cb]], writes=[xk])
                            op("dve", lambda e: e.tensor_tensor(out=w_[:, qo:TT], in0=e_[:, qo:TT], in1=x_[:, qo:TT], op=ALU.mult),
                               reads=[ek, xk], writes=[wk_])
                            op("pe", lambda e: e.matmul(pb[ob][base:base + 64, qo:TT], lhsT=Vc[:, t_["kb"], h * 64:(h + 1) * 64],
                                                        rhs=w_[:, qo:TT], start=t_["first"], stop=t_["last"], skip_group_check=True),
                               reads=["Vc", wk_], writes=[pbk[ob]])
                            if t_["last"]:
                                op("dve", lambda e: e.tensor_tensor(out=yT[base:base + 64, 4 + c, :], in0=pb[ob][base:base + 64, :],
                                                                    in1=sga[base:base + 64, c, :], op=ALU.mult),
                                   reads=[pbk[ob], "sga"], writes=["yT"])

                        for i in range(-1, NTL + 3):
                            if 0 <= i + 1 < NTL:
                                st_QK(i + 1)
                            if 0 <= i < NTL:
                                st_A1(i)
                            if 0 <= i - 1 < NTL:
                                st_A2(i - 1)
                            if 0 <= i - 2 < NTL:
                                st_B(i - 2)
                            if 0 <= i - 3 < NTL:
                                st_C(i - 3)
                        tctr[0] += NTL
                        kb.dma("sp", yT[:, 0:4, :], yss_d[L, :, :, j * TT:(j + 1) * TT].rearrange("c p t -> p c t"), "ld_yss",
                               writes=["yT"])
                        kb.dma("sp", htile[:], h_src[j * TT:(j + 1) * TT, :].rearrange("(i p) d -> p i d", p=128), "ld_htile", writes=HT)
                        for half in range(2):
                            w3, wk = pf.get(4 + half)
                            for i in range(NSUB):
                                bank = 4 + i % 2
                                for kc in range(8):
                                    op("pe", lambda e, kc=kc, i=i, bank=bank, w3=w3: e.matmul(
                                        pb[bank][:], lhsT=yT[:, kc, i * 128:(i + 1) * 128], rhs=w3[:, kc, :],
                                        start=(kc == 0), stop=(kc == 7)), reads=[wk, "yT"], writes=[pbk[bank]], sig=(kc == 7))
                                op("dve", lambda e, i=i, bank=bank, half=half: e.tensor_tensor(
                                    out=htile[:, i, half * 512:(half + 1) * 512], in0=htile[:, i, half * 512:(half + 1) * 512],
                                    in1=pb[bank][:], op=ALU.add), reads=[pbk[bank]] + HT, writes=HT)
                        if dbg and j == 0 and L == 0:
                            kb.dma("sp", dbg_out("dbg_h1", [128, NSUB, D]), htile[:], "st_dbg1", reads=HT)
                        for i in range(NSUB):
                            hs = htile[:, i, :]
                            op("act", lambda e, hs=hs, i=i: e.activation(out=sqj[:], in_=hs, func=AF.Square, accum_out=ss[:, i:i + 1]),
                               reads=HT, writes=["sqj", f"ss{i}"])
                            op("act", lambda e, i=i: e.activation(out=lnv[:, i:i + 1], in_=ss[:, i:i + 1], func=AF.Ln,
                                                                  scale=1.0 / D, bias=EPS), reads=[f"ss{i}"], writes=[f"lnv{i}"])
                            op("act", lambda e, i=i: e.activation(out=rstd[:, i:i + 1], in_=lnv[:, i:i + 1], func=AF.Exp, scale=-0.5),
                               reads=[f"lnv{i}"], writes=[f"rstd{i}"])
                            ht = hnt[i % 2]
                            tk = f"hnt{i % 2}"
                            op("dve", lambda e, hs=hs, ht=ht, i=i: e.tensor_scalar(out=ht[:], in0=hs, scalar1=rstd[:, i:i + 1],
                                                                                    scalar2=None, op0=ALU.mult),
                               reads=HT + [f"rstd{i}"], writes=[tk])
                            bk = i % 2
                            pv = pb[bk][:].bitcast(BF16)
                            for kc in range(8):
                                op("pe", lambda e, kc=kc, ht=ht, pv=pv: e.transpose(out=pv[:, kc * 128:(kc + 1) * 128],
                                                                                   in_=ht[:, kc * 128:(kc + 1) * 128], identity=identb[:]),
                                   reads=[tk, "identb"], writes=[pbk[bk]], sig=(kc == 7))
                            op("dve", lambda e, i=i, pv=pv: e.tensor_tensor(out=hnT[:, :, i * 128:(i + 1) * 128],
                                                                            in0=pv.rearrange("p (k t) -> p k t", k=8),
                                                                            in1=gple[:, :].unsqueeze(2).to_broadcast([128, 8, 128]),
                                                                            op=ALU.mult), reads=[pbk[bk], "gple"], writes=["hnT"])
                            ps_, pb_ = psub[i % 2], pbf[i % 2]
                            kb.dma("sp", ps_[:], p_d[L, j * TT + i * 128:j * TT + (i + 1) * 128, :], f"ld_psub{i % 2}",
                                   writes=[f"psub{i % 2}"])
                            op("dve", lambda e, ps_=ps_, pb_=pb_: e.tensor_copy(out=pb_[:], in_=ps_[:]),
                               reads=[f"psub{i % 2}"], writes=[f"pbf{i % 2}"])
                            pv2 = pb[2 + i % 2][:].bitcast(BF16)
                            for k2 in range(2):
                                op("pe", lambda e, k2=k2, pb_=pb_, pv2=pv2: e.transpose(out=pv2[:, k2 * 128:(k2 + 1) * 128],
                                                                                       in_=pb_[:, k2 * 128:(k2 + 1) * 128],
                                                                                       identity=identb[:]),
                                   reads=[f"pbf{i % 2}", "identb"], writes=[pbk[2 + i % 2]], sig=(k2 == 1))
                            op("dve", lambda e, i=i, pv2=pv2: e.tensor_copy(out=pT[:, :, i * 128:(i + 1) * 128],
                                                                            in_=pv2[:, 0:256].rearrange("p (k t) -> p k t", k=2)),
                               reads=[pbk[2 + i % 2]], writes=["pT"])
                        wpp3, wppk = None, None
                        for half in range(2):
                            w3, wk = pf.get(6 if half == 0 else 8)
                            if wpp3 is None:
                                wpp3, wppk = pf.get(7)
                            for i in range(NSUB):
                                gb, pbk_ = 4 + i % 2, 6 + i % 2
                                for kc in range(8):
                                    op("pe", lambda e, kc=kc, i=i, gb=gb, w3=w3: e.matmul(
                                        pb[gb][:], lhsT=hnT[:, kc, i * 128:(i + 1) * 128], rhs=w3[:, kc, :],
                                        start=(kc == 0), stop=(kc == 7)), reads=[wk, "hnT"], writes=[pbk[gb]], sig=(kc == 7))
                                for k2 in range(2):
                                    op("pe", lambda e, k2=k2, i=i, pbk_=pbk_, half=half: e.matmul(
                                        pb[pbk_][:], lhsT=pT[:, k2, i * 128:(i + 1) * 128],
                                        rhs=wpp3[:, k2, half * 512:(half + 1) * 512], start=(k2 == 0), stop=(k2 == 1)),
                                       reads=[wppk, "pT"], writes=[pbk[pbk_]], sig=(k2 == 1))
                                g_, gp_ = gsb[i % 2], gpp[i % 2]
                                op("act", lambda e, g_=g_, gb=gb: e.activation(out=g_[:], in_=pb[gb][:], func=AF.Sigmoid),
                                   reads=[pbk[gb]], writes=[f"gsb{i % 2}"])
                                op("dve", lambda e, g_=g_, gp_=gp_, pbk_=pbk_: e.tensor_tensor(out=gp_[:], in0=g_[:], in1=pb[pbk_][:],
                                                                                               op=ALU.mult),
                                   reads=[f"gsb{i % 2}", pbk[pbk_]], writes=[f"gpp{i % 2}"])
                                op("dve", lambda e, i=i, half=half, gp_=gp_: e.tensor_tensor(
                                    out=htile[:, i, half * 512:(half + 1) * 512], in0=htile[:, i, half * 512:(half + 1) * 512],
                                    in1=gp_[:], op=ALU.add), reads=[f"gpp{i % 2}"] + HT, writes=HT)
                        kb.dma("sp", h_dst[j * TT:(j + 1) * TT, :].rearrange("(i p) d -> p i d", p=128), htile[:], "st_h", reads=HT)
                    kb.barrier()
                if stop_at == f"s2_{L}":
                    dd = nc.dram_tensor("dbg_h", [S, D], F32, kind="ExternalOutput").ap()
                    kb.dma("sp", dd, h_dst, "st_dbgh")
                    break
        kb.finish("sp")
        build_program.stats = (kb.ninst, kb.nwaits)
    return nc


def _prep_shared(inp):
    f = np.float32
    w_in = np.asarray(inp["w_in"], f)
    u_cols = w_in[:, :, 0:512].reshape(2, D, 32, 16)
    u_pad = np.zeros((2, D, 32, 32), f)
    u_pad[:, :, :, 0:16] = u_cols
    w_in_p = np.concatenate([u_pad.reshape(2, D, 1024), w_in[:, :, 512:]], axis=2)
    w_glu = np.asarray(inp["ssm_w_glu"], f).reshape(2, 32, 16, 1024)
    w_glu_p = np.zeros((2, 32, 32, 1024), f)
    w_glu_p[:, :, 0:16, :] = w_glu
    w_glu_p = w_glu_p.reshape(2, 1024, 1024)

    def colT(v):
        return np.ascontiguousarray(np.asarray(v, f).reshape(2, 8, 128).transpose(0, 2, 1))

    d = np.asarray(inp["ssm_d"], f)
    d_pad = np.zeros((2, 32, 32), f)
    d_pad[:, :, 0:16] = d
    dpad = np.ascontiguousarray(d_pad.reshape(2, 8, 128).transpose(0, 2, 1))
    gq = np.tile(np.asarray(inp["q_norm_g"], f), (1, 2)).reshape(2, 128, 1)
    gk = np.tile(np.asarray(inp["k_norm_g"], f), (1, 2)).reshape(2, 128, 1)
    cre = np.asarray(inp["ssm_c_re"], f).reshape(2, 4, 128, 64)
    cim = np.asarray(inp["ssm_c_im"], f).reshape(2, 4, 128, 64)
    return {
        "w_in": np.ascontiguousarray(w_in_p), "w_glu": w_glu_p,
        "w_out": np.asarray(inp["w_out"], f), "w_pg": np.asarray(inp["w_ple_gate"], f),
        "w_pp": np.asarray(inp["w_ple_proj"], f),
        "gmix": colT(inp["mix_norm_g"]), "gple": colT(inp["ple_norm_g"]), "bglu": colT(inp["ssm_b_glu"]),
        "dpad": dpad, "gq": np.ascontiguousarray(gq), "gk": np.ascontiguousarray(gk),
        "a_re": np.asarray(inp["ssm_a_re"], f), "a_im": np.asarray(inp["ssm_a_im"], f),
        "logdt": np.asarray(inp["ssm_log_dt"], f).reshape(2, 32, 1),
        "b_re": np.asarray(inp["ssm_b_re"], f), "b_im": np.asarray(inp["ssm_b_im"], f),
        "ccat": np.ascontiguousarray(np.concatenate([cre, cim], axis=3)),
        "ccatsw": np.ascontiguousarray(np.concatenate([cim, cre], axis=3)),
    }


def kernel(**inputs):
    shared = _prep_shared(inputs)
    x = np.asarray(inputs["x"], np.float32)
    p = np.asarray(inputs["p"], np.float32)
    nc = build_program()
    in_maps = []
    for b in range(NCORES):
        m = dict(shared)
        m["x"] = np.ascontiguousarray(x[b])
        m["p"] = np.ascontiguousarray(p[:, b])
        in_maps.append(m)
    res = run_bass_kernel_spmd(nc, in_maps, core_ids=list(range(NCORES)))
    return np.stack([r["out"] for r in res.results], axis=0).astype(np.float32)
```

```python
import math
from contextlib import ExitStack
import numpy as np
import concourse.bass as bass
import concourse.mybir as mybir
from concourse.bass_utils import run_bass_kernel_spmd

F32 = mybir.dt.float32
BF16 = mybir.dt.bfloat16
AF = mybir.ActivationFunctionType
ALU = mybir.AluOpType

S = 4096
D = 1024
TT = 512
NT = S // TT
NSUB = TT // 128
EPS = 1e-6
NPIECE = 14
NCORES = 8


class _Eng:
    def __init__(self, name, handle, sem):
        self.name = name
        self.h = handle
        self.sem = sem
        self.count = 0
        self.waited = {}


class KB:
    def __init__(self, nc, stack):
        self.nc = nc
        self.stack = stack
        self.eng = {}
        for name, h in (("pe", nc.tensor), ("act", nc.scalar), ("dve", nc.vector),
                        ("pool", nc.gpsimd), ("sp", nc.sync)):
            sem = stack.enter_context(nc.semaphore("s_" + name))
            self.eng[name] = _Eng(name, h, sem)
        self.state = {}
        self.dsem = {}
        self.nwaits = 0
        self.ninst = 0

    def sb(self, name, shape, dt, stack=None):
        self._uid = getattr(self, "_uid", 0) + 1
        return (stack or self.stack).enter_context(self.nc.sbuf_tensor(f"{name}_{self._uid}", list(shape), dt))

    def ps(self, name, shape, dt=F32):
        return self.stack.enter_context(self.nc.psum_tensor(name, list(shape), dt))

    def dma_sem(self, name):
        if name not in self.dsem:
            sem = self.stack.enter_context(self.nc.semaphore("d_" + name))
            self.dsem[name] = [sem, 0]
        return self.dsem[name]

    def _st(self, k):
        s = self.state.get(k)
        if s is None:
            s = [None, []]
            self.state[k] = s
        return s

    def _wait(self, e, sem, val):
        key = id(sem)
        if e.waited.get(key, 0) >= val:
            return
        e.h.wait_ge(sem, val)
        e.waited[key] = val
        self.nwaits += 1

    def _deps(self, e, reads, writes):
        for k in reads:
            w = self._st(k)[0]
            if w is not None:
                self._wait(e, w[0], w[1])
        pe = e.name == "pe"
        for k in writes:
            s = self._st(k)
            if s[0] is not None and not (pe and s[0][0] is e.sem):
                self._wait(e, s[0][0], s[0][1])
            for (sem, val) in s[1]:
                if not (pe and sem is e.sem):
                    self._wait(e, sem, val)

    def _commit(self, tag, reads, writes):
        for k in reads:
            r = self._st(k)[1]
            for idx, (sem, val) in enumerate(r):
                if sem is tag[0]:
                    r[idx] = tag if tag[1] > val else (sem, val)
                    break
            else:
                r.append(tag)
        for k in writes:
            s = self._st(k)
            s[0] = tag
            s[1] = []

    def op(self, en, fn, reads=(), writes=(), sig=True):
        e = self.eng[en]
        self._deps(e, reads, writes)
        ins = fn(e.h)
        self.ninst += 1
        if sig:
            e.count += 1
            ins.then_inc(e.sem, 1)
            tag = (e.sem, e.count)
        else:
            tag = (e.sem, e.count + 1)
        self._commit(tag, reads, writes)
        return ins

    def dma(self, qn, out, in_, semname, reads=(), writes=(), **kw):
        e = self.eng[qn]
        ds = self.dma_sem(semname)
        if ds[1] > 0:
            self._wait(e, ds[0], ds[1])
        self._deps(e, reads, writes)
        ins = e.h.dma_start(out=out, in_=in_, **kw)
        ds[1] += 16
        ins.then_inc(ds[0], 16)
        self.ninst += 1
        self._commit((ds[0], ds[1]), reads, writes)
        return ins

    def barrier(self):
        snap_e = [(o.sem, o.count) for o in self.eng.values() if o.count]
        snap_d = [(sem, cnt) for (sem, cnt) in self.dsem.values() if cnt]
        for e in self.eng.values():
            for sem, cnt in snap_e:
                if sem is not e.sem:
                    self._wait(e, sem, cnt)
            for sem, cnt in snap_d:
                self._wait(e, sem, cnt)
        self.state = {}

    def finish(self, en="sp"):
        e = self.eng[en]
        for name, (sem, cnt) in self.dsem.items():
            if cnt:
                self._wait(e, sem, cnt)
        for o in self.eng.values():
            if o.count and o is not e:
                self._wait(e, o.sem, o.count)


def build_program(stop_at=None, dbg=False):
    nc = bass.Bass("TRN2", target_bir_lowering=False)

    def din(name, shape):
        return nc.dram_tensor(name, list(shape), F32, kind="ExternalInput").ap()

    x_d = din("x", [S, D])
    p_d = din("p", [2, S, 256])
    w_in_d = din("w_in", [2, D, 3584])
    w_glu_d = din("w_glu", [2, D, 1024])
    w_out_d = din("w_out", [2, D, D])
    w_pg_d = din("w_pg", [2, D, D])
    w_pp_d = din("w_pp", [2, 256, D])
    gmix_d = din("gmix", [2, 128, 8])
    gple_d = din("gple", [2, 128, 8])
    bglu_d = din("bglu", [2, 128, 8])
    dpad_d = din("dpad", [2, 128, 8])
    gq_d = din("gq", [2, 128, 1])
    gk_d = din("gk", [2, 128, 1])
    are_d = din("a_re", [2, 32, 64])
    aim_d = din("a_im", [2, 32, 64])
    ldt_d = din("logdt", [2, 32, 1])
    bre_d = din("b_re", [2, 32, 64, 16])
    bim_d = din("b_im", [2, 32, 64, 16])
    ccat_d = din("ccat", [2, 4, 128, 128])
    ccsw_d = din("ccatsw", [2, 4, 128, 128])
    out_d = nc.dram_tensor("out", [S, D], F32, kind="ExternalOutput").ap()
    wbf_d = nc.dram_tensor("wbf", [2, NPIECE, 128, 4096], BF16, kind="Internal").ap()
    yss_d = nc.dram_tensor("yss", [2, 4, 128, S], BF16, kind="Internal").ap()
    hmid_d = nc.dram_tensor("hmid", [S, D], F32, kind="Internal").ap()
    dbg_d = {}

    def dbg_out(name, shape):
        dbg_d[name] = nc.dram_tensor(name, list(shape), F32, kind="ExternalOutput").ap()
        return dbg_d[name]

    with ExitStack() as st:
        kb = KB(nc, st)
        op = kb.op

        pb = [kb.ps(f"pb{i}", [128, 512], F32) for i in range(8)]
        pbk = [f"pb{i}" for i in range(8)]
        identb = kb.sb("identb", [128, 128], BF16)
        identf = kb.sb("identf", [128, 128], F32)
        tri = kb.sb("tri", [128, 128], BF16)
        ones = kb.sb("ones", [128, 128], BF16)
        masklt = kb.sb("masklt", [128, 128], BF16)
        blockones = kb.sb("blockones", [128, 128], BF16)
        bdmask = kb.sb("bdmask", [128, 128], F32)
        NRING = 3
        ring = [kb.sb(f"ring{i}", [128, 4096], BF16) for i in range(NRING)]
        stage = [kb.sb(f"stage{i}", [128, 1024], F32) for i in range(2)]

        def mk_affine(t, key, pattern, cm, cmp_):
            op("pool", lambda e: e.memset(t[:], 1.0), writes=[key])
            op("pool", lambda e: e.affine_select(out=t[:], in_=t[:], pattern=pattern, compare_op=cmp_, fill=0.0,
                                                 base=0, channel_multiplier=cm), reads=[key], writes=[key])

        mk_affine(identb, "identb", [[-1, 128]], 1, ALU.is_equal)
        mk_affine(identf, "identf", [[-1, 128]], 1, ALU.is_equal)
        mk_affine(tri, "tri", [[-1, 128]], 1, ALU.is_ge)
        mk_affine(masklt, "masklt", [[1, 128]], -1, ALU.is_gt)
        op("pool", lambda e: e.memset(ones[:], 1.0), writes=["ones"])
        op("pool", lambda e: e.memset(blockones[:], 0.0), writes=["blockones"])
        for hh in range(2):
            op("pool", lambda e, hh=hh: e.memset(blockones[64 * hh:64 * hh + 64, 64 * hh:64 * hh + 64], 1.0),
               writes=["blockones"])
        op("pool", lambda e: e.memset(bdmask[:], 0.0), writes=["bdmask"])
        for gg in range(4):
            op("pool", lambda e, gg=gg: e.memset(bdmask[32 * gg:32 * gg + 32, 32 * gg:32 * gg + 32], 1.0),
               writes=["bdmask"])

        def piece_src(L, pid):
            if pid <= 6:
                return w_in_d[L, :, pid * 512:(pid + 1) * 512].rearrange("(k p) n -> p k n", p=128), 8, 512
            if pid <= 8:
                return w_glu_d[L, :, (pid - 7) * 512:(pid - 6) * 512].rearrange("(k p) n -> p k n", p=128), 8, 512
            if pid <= 10:
                return w_out_d[L, :, (pid - 9) * 512:(pid - 8) * 512].rearrange("(k p) n -> p k n", p=128), 8, 512
            if pid <= 12:
                return w_pg_d[L, :, (pid - 11) * 512:(pid - 10) * 512].rearrange("(k p) n -> p k n", p=128), 8, 512
            return w_pp_d[L].rearrange("(k p) n -> p k n", p=128), 2, 1024

        converted = set()
        ring_ctr = [0]

        class Pref:
            def __init__(self, L, pids, ceng="dve"):
                self.L = L
                self.pids = pids
                self.loaded = []
                self.ceng = ceng

            def _load(self, idx):
                L, pid = self.L, self.pids[idx]
                s = ring_ctr[0] % NRING
                ring_ctr[0] += 1
                key = f"ring{s}"
                src, nk, ncol = piece_src(L, pid)
                n = nk * ncol
                v3 = ring[s][:, 0:n].rearrange("p (k n) -> p k n", k=nk)
                if (L, pid) not in converted:
                    ck = max(1, 1024 // ncol)
                    nh = ck * ncol
                    for q_ in range(nk // ck):
                        hf = q_ % 2
                        sv = stage[hf][:, 0:nh].rearrange("p (k n) -> p k n", k=ck)
                        kb.dma("sp", sv, src[:, q_ * ck:(q_ + 1) * ck, :], f"ld_stage{hf}", writes=[f"stage{hf}"])
                        if self.ceng == "act":
                            op("act", lambda e: e.activation(out=ring[s][:, q_ * nh:(q_ + 1) * nh], in_=stage[hf][:, 0:nh],
                                                             func=AF.Copy), reads=[f"stage{hf}"], writes=[key])
                        else:
                            op("dve", lambda e: e.tensor_copy(out=ring[s][:, q_ * nh:(q_ + 1) * nh], in_=stage[hf][:, 0:nh]),
                               reads=[f"stage{hf}"], writes=[key])
                    kb.dma("sp", wbf_d[L, pid, :, 0:n], ring[s][:, 0:n], f"st_ring{s}",
                           reads=[key], writes=[f"wbf{L}_{pid}"])
                    converted.add((L, pid))
                else:
                    kb.dma("sp", ring[s][:, 0:n], wbf_d[L, pid, :, 0:n], f"ld_ring{s}",
                           reads=[f"wbf{L}_{pid}"], writes=[key])
                self.loaded.append((v3, key))

            def get(self, idx, ahead=2):
                while len(self.loaded) <= min(idx + ahead, len(self.pids) - 1):
                    self._load(len(self.loaded))
                return self.loaded[idx]

        def norm_phase(stk_bufs, src_rows, gain, tagp, seng):
            hsub, hnt, ss, lnv, rstd, sqj, hnT = stk_bufs
            for i in range(NSUB):
                hs = hsub[i % 2]
                hk = f"hsub{i % 2}"
                kb.dma("sp", hs[:], src_rows(i), f"ld_hsub{i % 2}", writes=[hk])
                op("act", lambda e: e.activation(out=sqj[:], in_=hs[:], func=AF.Square, accum_out=ss[:, i:i + 1]),
                   reads=[hk], writes=["sqj", f"ss{i}"])
                op("act", lambda e: e.activation(out=lnv[:, i:i + 1], in_=ss[:, i:i + 1], func=AF.Ln,
                                                 scale=1.0 / D, bias=EPS), reads=[f"ss{i}"], writes=[f"lnv{i}"])
                op("act", lambda e: e.activation(out=rstd[:, i:i + 1], in_=lnv[:, i:i + 1], func=AF.Exp, scale=-0.5),
                   reads=[f"lnv{i}"], writes=[f"rstd{i}"])
                ht = hnt[i % 2]
                tk = f"hnt{i % 2}"
                if seng == "act":
                    op("act", lambda e: e.activation(out=ht[:], in_=hs[:], func=AF.Copy, scale=rstd[:, i:i + 1]),
                       reads=[hk, f"rstd{i}"], writes=[tk])
                else:
                    op("dve", lambda e: e.tensor_scalar(out=ht[:], in0=hs[:], scalar1=rstd[:, i:i + 1], scalar2=None,
                                                        op0=ALU.mult), reads=[hk, f"rstd{i}"], writes=[tk])
                bk = i % 2
                pv = pb[bk][:].bitcast(BF16)
                for kc in range(8):
                    op("pe", lambda e, kc=kc: e.transpose(out=pv[:, kc * 128:(kc + 1) * 128],
                                                          in_=ht[:, kc * 128:(kc + 1) * 128], identity=identb[:]),
                       reads=[tk, "identb"], writes=[pbk[bk]], sig=(kc == 7))
                op("dve", lambda e: e.tensor_tensor(out=hnT[:, :, i * 128:(i + 1) * 128],
                                                    in0=pv.rearrange("p (k t) -> p k t", k=8),
                                                    in1=gain[:, :].unsqueeze(2).to_broadcast([128, 8, 128]),
                                                    op=ALU.mult),
                   reads=[pbk[bk], tagp], writes=["hnT"])

        def proj_fm(w3, wkey, col0, hnT, bank):
            for kc in range(8):
                op("pe", lambda e, kc=kc: e.matmul(pb[bank][:], lhsT=w3[:, kc, col0:col0 + 128], rhs=hnT[:, kc, :],
                                                   start=(kc == 0), stop=(kc == 7)),
                   reads=[wkey, "hnT"], writes=[pbk[bank]], sig=(kc == 7))

        for L in range(2):
            h_src = x_d if L == 0 else hmid_d
            h_dst = hmid_d if L == 0 else out_d

            with ExitStack() as lst:
                gmix = kb.sb(f"gmix{L}", [128, 8], F32, lst)
                gple = kb.sb(f"gple{L}", [128, 8], F32, lst)
                bglu = kb.sb(f"bglu{L}", [128, 8], F32, lst)
                dpad = kb.sb(f"dpad{L}", [128, 8], F32, lst)
                gq = kb.sb(f"gq{L}", [128, 1], F32, lst)
                gk = kb.sb(f"gk{L}", [128, 1], F32, lst)
                for t, d_, k_ in ((gmix, gmix_d, "gmix"), (gple, gple_d, "gple"), (bglu, bglu_d, "bglu"),
                                  (dpad, dpad_d, "dpad"), (gq, gq_d, "gq"), (gk, gk_d, "gk")):
                    kb.dma("sp", t[:], d_[L], "ld_small_" + k_, writes=[k_])
                op("dve", lambda e: e.tensor_scalar(out=gq[:], in0=gq[:], scalar1=0.125, scalar2=None, op0=ALU.mult),
                   reads=["gq"], writes=["gq"])

                with ExitStack() as s1:
                    T_tab = kb.sb("T_tab", [128, 8, 9, 128], BF16, s1)
                    WS_tab = kb.sb("WS_tab", [128, 8, 8, 128], BF16, s1)
                    WC_tab = kb.sb("WC_tab", [128, 32, 8, 32], BF16, s1)
                    AA1 = kb.sb("AA1", [128, 2, 32], F32, s1)
                    AA2 = kb.sb("AA2", [128, 2, 32], F32, s1)
                    Hh = kb.sb("Hh", [128, 2, 32, 65], F32, s1)
                    with ExitStack() as su:
                        def g32(name):
                            return kb.sb(name, [32, 64], F32, su)
                        are, aim, Lr, Li, mag, cc_, ss_, t1, t2, t3 = [g32(n) for n in
                                                                      ("are", "aim", "Lr", "Li", "mag", "cc_", "ss_", "t1", "t2", "t3")]
                        ldt = kb.sb("ldt", [32, 1], F32, su)
                        dtv = kb.sb("dtv", [32, 1], F32, su)
                        kb.dma("sp", are[:], are_d[L], "ld_are", writes=["are"])
                        kb.dma("sp", aim[:], aim_d[L], "ld_aim", writes=["aim"])
                        kb.dma("sp", ldt[:], ldt_d[L], "ld_ldt", writes=["ldt"])
                        op("act", lambda e: e.activation(out=dtv[:], in_=ldt[:], func=AF.Exp), reads=["ldt"], writes=["dtv"])

                        def dv(fn, reads, writes):
                            op("dve", fn, reads=reads, writes=writes)

                        def tt_(o, a, b, alu, ok, ak, bk_):
                            dv(lambda e: e.tensor_tensor(out=o, in0=a, in1=b, op=alu), [ak, bk_], [ok])

                        dv(lambda e: e.tensor_scalar(out=Lr[:], in0=are[:], scalar1=dtv[:, 0:1], scalar2=None, op0=ALU.mult),
                           ["are", "dtv"], ["Lr"])
                        dv(lambda e: e.tensor_scalar(out=Li[:], in0=aim[:], scalar1=dtv[:, 0:1], scalar2=None, op0=ALU.mult),
                           ["aim", "dtv"], ["Li"])
                        op("act", lambda e: e.activation(out=mag[:], in_=Lr[:], func=AF.Exp), reads=["Lr"], writes=["mag"])
                        op("act", lambda e: e.activation(out=ss_[:], in_=Li[:], func=AF.Sin, scale=1.0 / 32.0),
                           reads=["Li"], writes=["ss_"])
                        op("act", lambda e: e.activation(out=cc_[:], in_=Li[:], func=AF.Sin, scale=1.0 / 32.0,
                                                         bias=math.pi / 2), reads=["Li"], writes=["cc_"])
                        for _ in range(5):
                            tt_(t1[:], cc_[:], cc_[:], ALU.mult, "t1", "cc_", "cc_")
                            tt_(t2[:], ss_[:], ss_[:], ALU.mult, "t2", "ss_", "ss_")
                            tt_(t3[:], cc_[:], ss_[:], ALU.mult, "t3", "cc_", "ss_")
                            tt_(cc_[:], t1[:], t2[:], ALU.subtract, "cc_", "t1", "t2")
                            dv(lambda e: e.tensor_scalar(out=ss_[:], in0=t3[:], scalar1=2.0, scalar2=None, op0=ALU.mult),
                               ["t3"], ["ss_"])
                        Pre = kb.sb("Pre", [32, 9, 64], F32, su)
                        Pim = kb.sb("Pim", [32, 9, 64], F32, su)
                        tt_(Pre[:, 1, :], mag[:], cc_[:], ALU.mult, "P1r", "mag", "cc_")
                        tt_(Pim[:, 1, :], mag[:], ss_[:], ALU.mult, "P1i", "mag", "ss_")
                        tt_(t1[:], are[:], are[:], ALU.mult, "t1", "are", "are")
                        tt_(t2[:], aim[:], aim[:], ALU.mult, "t2", "aim", "aim")
                        tt_(t1[:], t1[:], t2[:], ALU.add, "t1", "t1", "t2")
                        dv(lambda e: e.reciprocal(out=t3[:], in_=t1[:]), ["t1"], ["t3"])
                        dv(lambda e: e.tensor_scalar(out=t1[:], in0=Pre[:, 1, :], scalar1=-1.0, scalar2=None, op0=ALU.add),
                           ["P1r"], ["t1"])
                        tt_(t2[:], t1[:], are[:], ALU.mult, "t2", "t1", "are")
                        tt_(mag[:], Pim[:, 1, :], aim[:], ALU.mult, "mag", "P1i", "aim")
                        tt_(t2[:], t2[:], mag[:], ALU.add, "t2", "t2", "mag")
                        tt_(Pre[:, 0, :], t2[:], t3[:], ALU.mult, "P0r", "t2", "t3")
                        tt_(t2[:], Pim[:, 1, :], are[:], ALU.mult, "t2", "P1i", "are")
                        tt_(mag[:], t1[:], aim[:], ALU.mult, "mag", "t1", "aim")
                        tt_(t2[:], t2[:], mag[:], ALU.subtract, "t2", "t2", "mag")
                        tt_(Pim[:, 0, :], t2[:], t3[:], ALU.mult, "P0i", "t2", "t3")
                        for k in range(1, 8):
                            a, b_ = f"P{k}r", f"P{k}i"
                            tt_(t1[:], Pre[:, k, :], Pre[:, 1, :], ALU.mult, "t1", a, "P1r")
                            tt_(t2[:], Pim[:, k, :], Pim[:, 1, :], ALU.mult, "t2", b_, "P1i")
                            tt_(Pre[:, k + 1, :], t1[:], t2[:], ALU.subtract, f"P{k + 1}r", "t1", "t2")
                            tt_(t1[:], Pre[:, k, :], Pim[:, 1, :], ALU.mult, "t1", a, "P1i")
                            tt_(t2[:], Pim[:, k, :], Pre[:, 1, :], ALU.mult, "t2", b_, "P1r")
                            tt_(Pim[:, k + 1, :], t1[:], t2[:], ALU.add, f"P{k + 1}i", "t1", "t2")
                        A12 = kb.sb("A12", [128, 9, 2, 32], F32, su)
                        cat = kb.sb("cat", [32, 2, 128], F32, su)
                        for k in range(9):
                            rk, ik = f"P{k}r", f"P{k}i"
                            dv(lambda e, k=k: e.tensor_copy(out=cat[:, 0, 0:64], in_=Pre[:, k, :]), [rk], ["cat"])
                            dv(lambda e, k=k: e.tensor_copy(out=cat[:, 0, 64:128], in_=Pre[:, k, :]), [rk], ["cat"])
                            dv(lambda e, k=k: e.tensor_scalar(out=cat[:, 1, 0:64], in0=Pim[:, k, :], scalar1=-1.0,
                                                              scalar2=None, op0=ALU.mult), [ik], ["cat"])
                            dv(lambda e, k=k: e.tensor_copy(out=cat[:, 1, 64:128], in_=Pim[:, k, :]), [ik], ["cat"])
                            for w_ in range(2):
                                op("pe", lambda e, w_=w_: e.transpose(out=pb[0][:, w_ * 32:(w_ + 1) * 32], in_=cat[:, w_, :],
                                                                      identity=identf[0:32, 0:32]),
                                   reads=["cat", "identf"], writes=["pb0"])
                            dv(lambda e, k=k: e.tensor_copy(out=A12[:, k, :, :],
                                                            in_=pb[0][:, 0:64].rearrange("p (a b) -> p a b", a=2)),
                               ["pb0"], [f"A12_{k}"])
                        for w_ in range(2):
                            dv(lambda e, w_=w_: e.tensor_copy(
                                out=AA1[:, w_, :].rearrange("p (a b) -> p a b", a=4),
                                in_=A12[:, 8, 0, :].rearrange("p (b a) -> p a b", a=4)), ["A12_8"], ["AA1"])
                        dv(lambda e: e.tensor_copy(out=AA2[:, 0, :].rearrange("p (a b) -> p a b", a=4),
                                                   in_=A12[:, 8, 1, :].rearrange("p (b a) -> p a b", a=4)), ["A12_8"], ["AA2"])
                        dv(lambda e: e.tensor_scalar(out=AA2[:, 1, :].rearrange("p (a b) -> p a b", a=4),
                                                     in0=A12[:, 8, 1, :].rearrange("p (b a) -> p a b", a=4),
                                                     scalar1=-1.0, scalar2=None, op0=ALU.mult), ["A12_8"], ["AA2"])
                        Braw = kb.sb("Braw", [128, 32, 16], F32, su)
                        Brsw = kb.sb("Brsw", [128, 32, 16], F32, su)
                        Bm = kb.sb("Bm", [128, 32, 16], F32, su)
                        Bsw = kb.sb("Bsw", [128, 32, 16], F32, su)
                        tmpA = kb.sb("tmpA", [128, 32, 16], F32, su)
                        tmpB = kb.sb("tmpB", [128, 32, 16], F32, su)
                        bre_v = bre_d[L].rearrange("g p h -> p g h")
                        bim_v = bim_d[L].rearrange("g p h -> p g h")
                        kb.dma("sp", Braw[0:64], bre_v, "ld_b0", writes=["Braw"])
                        kb.dma("sp", Braw[64:128], bim_v, "ld_b1", writes=["Braw"])
                        kb.dma("sp", Brsw[0:64], bim_v, "ld_b2", writes=["Brsw"])
                        kb.dma("sp", Brsw[64:128], bre_v, "ld_b3", writes=["Brsw"])

                        def bc(k, w_):
                            return A12[:, k, w_, :].unsqueeze(2).to_broadcast([128, 32, 16])

                        tt_(tmpA[:], Braw[:], bc(0, 0), ALU.mult, "tmpA", "Braw", "A12_0")
                        tt_(tmpB[:], Brsw[:], bc(0, 1), ALU.mult, "tmpB", "Brsw", "A12_0")
                        tt_(Bm[:], tmpA[:], tmpB[:], ALU.add, "Bm", "tmpA", "tmpB")
                        tt_(tmpA[:], Brsw[:], bc(0, 0), ALU.mult, "tmpA", "Brsw", "A12_0")
                        tt_(tmpB[:], Braw[:], bc(0, 1), ALU.mult, "tmpB", "Braw", "A12_0")
                        tt_(Bsw[:], tmpA[:], tmpB[:], ALU.subtract, "Bsw", "tmpA", "tmpB")
                        Cm = kb.sb("Cm", [128, 32, 16], F32, su)
                        Cmsw = kb.sb("Cmsw", [128, 32, 16], F32, su)
                        ccs = kb.sb("ccs", [128, 128], F32, su)
                        for (dst, dkey, srcd, neg_lo) in ((Cm, "Cm", ccat_d, False), (Cmsw, "Cmsw", ccsw_d, True)):
                            for c in range(4):
                                kb.dma("sp", ccs[:], srcd[L, c], "ld_ccs", writes=["ccs"])
                                op("pe", lambda e: e.transpose(out=pb[1][:, 0:128], in_=ccs[:], identity=identf[:]),
                                   reads=["ccs", "identf"], writes=["pb1"])
                                lo = pb[1][0:64, 0:128].rearrange("p (a b) -> p a b", a=8)
                                hi = pb[1][64:128, 0:128].rearrange("p (a b) -> p a b", a=8)
                                dlo = dst[0:64, 8 * c:8 * c + 8, :]
                                dhi = dst[64:128, 8 * c:8 * c + 8, :]
                                if neg_lo:
                                    dv(lambda e, lo=lo, dlo=dlo: e.tensor_scalar(out=dlo, in0=lo, scalar1=-1.0, scalar2=None,
                                                                                 op0=ALU.mult), ["pb1"], [dkey])
                                    dv(lambda e, hi=hi, dhi=dhi: e.tensor_copy(out=dhi, in_=hi), ["pb1"], [dkey])
                                else:
                                    dv(lambda e, lo=lo, dlo=dlo: e.tensor_copy(out=dlo, in_=lo), ["pb1"], [dkey])
                                    dv(lambda e, hi=hi, dhi=dhi: e.tensor_scalar(out=dhi, in0=hi, scalar1=-1.0, scalar2=None,
                                                                                 op0=ALU.mult), ["pb1"], [dkey])
                        T0f = kb.sb("T0f", [128, 8, 128], F32, su)
                        Cpad = kb.sb("Cpad", [128, 32, 32], BF16, su)
                        ABpad = kb.sb("ABpad", [128, 32, 32], BF16, su)
                        op("dve", lambda e: e.memset(Cpad[:], 0.0), writes=["Cpad"])
                        op("dve", lambda e: e.memset(ABpad[:], 0.0), writes=["ABpad"])
                        op("dve", lambda e: e.memset(WC_tab[:], 0.0), writes=["WC"])
                        dv(lambda e: e.tensor_copy(out=Cpad[:, :, 0:16], in_=Cm[:]), ["Cm"], ["Cpad"])
                        for tau in range(8):
                            if tau == 0:
                                dv(lambda e: e.tensor_copy(out=ABpad[:, :, 0:16], in_=Bm[:]), ["Bm"], ["ABpad"])
                            else:
                                tt_(tmpA[:], Bm[:], bc(tau, 0), ALU.mult, "tmpA", "Bm", f"A12_{tau}")
                                tt_(tmpB[:], Bsw[:], bc(tau, 1), ALU.mult, "tmpB", "Bsw", f"A12_{tau}")
                                tt_(ABpad[:, :, 0:16], tmpA[:], tmpB[:], ALU.add, "ABpad", "tmpA", "tmpB")
                            pv = pb[2][:].bitcast(BF16)
                            for pc in range(8):
                                op("pe", lambda e, pc=pc: e.transpose(
                                    out=pv[:, pc * 128:(pc + 1) * 128],
                                    in_=ABpad[:, 4 * pc:4 * pc + 4, :].rearrange("p a b -> p (a b)"), identity=identb[:]),
                                   reads=["ABpad", "identb"], writes=["pb2"])
                            dv(lambda e, tau=tau: e.tensor_copy(out=WS_tab[:, :, 7 - tau, :],
                                                                in_=pv.rearrange("p (a b) -> p a b", a=8)), ["pb2"], ["WS"])
                            for pc in range(8):
                                bk = 3 + pc // 4
                                op("pe", lambda e, pc=pc, bk=bk: e.matmul(
                                    pb[bk][:, (pc % 4) * 128:(pc % 4 + 1) * 128],
                                    lhsT=ABpad[:, 4 * pc:4 * pc + 4, :].rearrange("p a b -> p (a b)"),
                                    rhs=Cpad[:, 4 * pc:4 * pc + 4, :].rearrange("p a b -> p (a b)"), start=True, stop=True),
                                   reads=["ABpad", "Cpad"], writes=[pbk[bk]])
                            for hf in range(2):
                                if tau > 0:
                                    dv(lambda e, hf=hf, tau=tau: e.tensor_tensor(
                                        out=T_tab[:, 4 * hf:4 * hf + 4, tau, :],
                                        in0=pb[3 + hf][:].rearrange("p (a b) -> p a b", a=4),
                                        in1=bdmask[:].unsqueeze(1).to_broadcast([128, 4, 128]), op=ALU.mult),
                                       [pbk[3 + hf], "bdmask"], ["T"])
                                else:
                                    dv(lambda e, hf=hf: e.tensor_tensor(
                                        out=T0f[:, 4 * hf:4 * hf + 4, :],
                                        in0=pb[3 + hf][:].rearrange("p (a b) -> p a b", a=4),
                                        in1=bdmask[:].unsqueeze(1).to_broadcast([128, 4, 128]), op=ALU.mult),
                                       [pbk[3 + hf], "bdmask"], ["T0f"])
                            if tau == 0:
                                for pc in range(8):
                                    dv(lambda e, pc=pc: e.scalar_tensor_tensor(
                                        out=T0f[:, pc, :], in0=identf[:], scalar=dpad[:, pc:pc + 1], in1=T0f[:, pc, :],
                                        op0=ALU.mult, op1=ALU.add), ["identf", "dpad", "T0f"], ["T0f"])
                                dv(lambda e: e.tensor_copy(out=T_tab[:, :, 0, :], in_=T0f[:]), ["T0f"], ["T"])
                                dv(lambda e: e.tensor_tensor(out=T0f[:], in0=T0f[:], in1=T_tab[:, :, 0, :], op=ALU.subtract),
                                   ["T0f", "T"], ["T0f"])
                                dv(lambda e: e.tensor_copy(out=T_tab[:, :, 8, :], in_=T0f[:]), ["T0f"], ["T"])
                        for lp in range(8):
                            k = lp + 1
                            tt_(tmpA[:], Cm[:], bc(k, 0), ALU.mult, "tmpA", "Cm", f"A12_{k}")
                            tt_(tmpB[:], Cmsw[:], bc(k, 1), ALU.mult, "tmpB", "Cmsw", f"A12_{k}")
                            tt_(WC_tab[:, :, lp, 0:16], tmpA[:], tmpB[:], ALU.subtract, "WC", "tmpA", "tmpB")
                        op("dve", lambda e: e.memset(Hh[:, :, :, 0:1], 0.0), writes=["Hh"])
                        kb.barrier()
                    hsub = [kb.sb(f"hsub{i}", [128, D], F32, s1) for i in range(2)]
                    hnt = [kb.sb(f"hnt{i}", [128, D], BF16, s1) for i in range(2)]
                    ss = kb.sb("ssq", [128, 4], F32, s1)
                    lnv = kb.sb("lnv", [128, 4], F32, s1)
                    rstd = kb.sb("rstd", [128, 4], F32, s1)
                    sqj = kb.sb("sqj", [128, D], BF16, s1)
                    hnT = kb.sb("hnT", [128, 8, TT], BF16, s1)
                    nb = (hsub, hnt, ss, lnv, rstd, sqj, hnT)
                    uTp2 = [kb.sb(f"uTp{i}", [128, 8, TT], BF16, s1) for i in range(2)]
                    sgs2 = [kb.sb(f"sgs{i}", [128, 4, TT], BF16, s1) for i in range(2)]
                    Hprev2 = [kb.sb(f"Hprev{i}", [128, 32, 64], BF16, s1) for i in range(2)]
                    zTp = kb.sb("zTp", [128, 8, TT], BF16, s1)
                    ysT = kb.sb("ysT", [128, 4, TT], BF16, s1)
                    SS = kb.sb("SS", [128, 2, 32, 64], F32, s1)
                    rt1 = kb.sb("rt1", [128, 2, 32], F32, s1)
                    rt2 = kb.sb("rt2", [128, 2, 32], F32, s1)
                    Ysb = [kb.sb(f"Ysb{i}", [64, 8, 4, 32], BF16, s1) for i in range(2)]
                    sgt = [kb.sb(f"sgt{i}", [128, TT], F32, s1) for i in range(2)]
                    vat = [kb.sb(f"vat{i}", [128, TT], F32, s1) for i in range(2)]
                    hh_ap = Hh[:]
                    pstep = list(hh_ap.ap[0])

                    def stage_X(j, pf):
                        uTp, sgs, Hprev = uTp2[j % 2], sgs2[j % 2], Hprev2[j % 2]
                        uk, sk, hk_ = f"uTp{j % 2}", f"sgs{j % 2}", f"Hprev{j % 2}"
                        norm_phase(nb, lambda i: h_src[j * TT + i * 128:j * TT + (i + 1) * 128, :], gmix, "gmix", "act")
                        for half in range(2):
                            w3, wk = pf.get(half)
                            for q4 in range(4):
                                pc = half * 4 + q4
                                bank = 2 + pc % 2
                                proj_fm(w3, wk, q4 * 128, hnT, bank)
                                op("act", lambda e: e.activation(out=uTp[:, pc, :].rearrange("p (l s) -> p l s", l=8),
                                                                 in_=pb[bank][:].rearrange("p (s l) -> p l s", l=8), func=AF.Copy),
                                   reads=[pbk[bank]], writes=[uk])
                        w3, wk = pf.get(2)
                        for c in range(4):
                            bank = 2 + c % 2
                            proj_fm(w3, wk, c * 128, hnT, bank)
                            op("act", lambda e: e.activation(out=sgs[:, c, :].rearrange("p (l s) -> p l s", l=8),
                                                             in_=pb[bank][:].rearrange("p (s l) -> p l s", l=8), func=AF.Silu),
                               reads=[pbk[bank]], writes=[sk])
                        for pc in range(8):
                            for l in range(8):
                                for gg in range(4):
                                    op("pe", lambda e: e.matmul(
                                        pb[4 + gg][:, pc * 64:(pc + 1) * 64],
                                        lhsT=WS_tab[32 * gg:32 * gg + 32, pc, l, :],
                                        rhs=uTp[32 * gg:32 * gg + 32, pc, l * 64:(l + 1) * 64],
                                        start=(l == 0), stop=(l == 7), tile_position=(32 * gg, 0)),
                                       reads=["WS", uk], writes=[pbk[4 + gg]], sig=(l == 7))
                        for gg in range(4):
                            op("dve", lambda e: e.tensor_copy(out=SS[:, 0, 8 * gg:8 * gg + 8, :].rearrange("p a b -> p (a b)"),
                                                              in_=pb[4 + gg][:]), reads=[pbk[4 + gg]], writes=["SS0"])
                        op("act", lambda e: e.activation(out=SS[0:64, 1, :, :], in_=SS[64:128, 0, :, :], func=AF.Copy),
                           reads=["SS0"], writes=["SS1"])
                        op("act", lambda e: e.activation(out=SS[64:128, 1, :, :], in_=SS[0:64, 0, :, :], func=AF.Copy),
                           reads=["SS0"], writes=["SS1"])
                        for sc in range(64):
                            cur = Hh[:, :, :, sc]
                            swp = bass.AP(hh_ap.tensor, hh_ap.offset + 32 * 65 + sc, [pstep, [-32 * 65, 2], [65, 32]])
                            op("dve", lambda e: e.tensor_tensor(out=rt1[:], in0=cur, in1=AA1[:], op=ALU.mult),
                               reads=["Hh", "AA1"], writes=["rt1"])
                            op("dve", lambda e: e.tensor_tensor(out=rt2[:], in0=swp, in1=AA2[:], op=ALU.mult),
                               reads=["Hh", "AA2"], writes=["rt2"])
                            op("dve", lambda e: e.tensor_tensor(out=rt1[:], in0=rt1[:], in1=rt2[:], op=ALU.add),
                               reads=["rt1", "rt2"], writes=["rt1"])
                            op("dve", lambda e: e.tensor_tensor(out=Hh[:, :, :, sc + 1], in0=rt1[:], in1=SS[:, :, :, sc],
                                                                op=ALU.add), reads=["rt1", "SS0", "SS1"], writes=["Hh"])
                        op("dve", lambda e: e.tensor_copy(out=Hprev[:], in_=Hh[:, 0, :, 0:64]), reads=["Hh"], writes=[hk_])
                        op("dve", lambda e: e.tensor_copy(out=Hh[:, :, :, 0:1], in_=Hh[:, :, :, 64:65]), reads=["Hh"], writes=["Hh"])

                    def stage_Y(j, pf, i0):
                        uTp, sgs, Hprev = uTp2[j % 2], sgs2[j % 2], Hprev2[j % 2]
                        uk, sk, hk_ = f"uTp{j % 2}", f"sgs{j % 2}", f"Hprev{j % 2}"

                        def yint(pc):
                            yb, ykey = Ysb[pc % 2], f"Ysb{pc % 2}"
                            for gg in range(4):
                                gi, gn, bank = gg * 8 + pc, 4 * pc + gg, gg // 2
                                op("pe", lambda e: e.matmul(
                                    pb[bank][0:64, (gg % 2) * 256:(gg % 2 + 1) * 256],
                                    lhsT=Hprev[:, gi, :], rhs=WC_tab[:, gn, :, :].rearrange("p a b -> p (a b)"),
                                    start=True, stop=True), reads=[hk_, "WC"], writes=[pbk[bank]])
                            for bank in range(2):
                                op("act", lambda e: e.activation(
                                    out=yb[:, :, 2 * bank:2 * bank + 2, :],
                                    in_=pb[bank][0:64, :].rearrange("p (g l h) -> p l g h", g=2, l=8),
                                    func=AF.Copy), reads=[pbk[bank]], writes=[ykey])

                        yint(0)
                        for pc in range(8):
                            if pc + 1 < 8:
                                yint(pc + 1)
                            yb, ykey = Ysb[pc % 2], f"Ysb{pc % 2}"
                            ybank = 2 + pc % 2
                            op("pe", lambda e: e.matmul(pb[ybank][:], lhsT=T_tab[:, pc, 0, :], rhs=uTp[:, pc, :], start=True, stop=False),
                               reads=["T", uk], writes=[pbk[ybank]], sig=False)
                            op("pe", lambda e: e.matmul(pb[ybank][:], lhsT=T_tab[:, pc, 8, :], rhs=uTp[:, pc, :], start=False, stop=False),
                               reads=["T", uk], writes=[pbk[ybank]], sig=False)
                            for tau in range(1, 8):
                                op("pe", lambda e: e.matmul(pb[ybank][:, tau * 64:TT], lhsT=T_tab[:, pc, tau, :],
                                                            rhs=uTp[:, pc, 0:(8 - tau) * 64], start=False, stop=False),
                                   reads=["T", uk], writes=[pbk[ybank]], sig=False)
                            for lp in range(8):
                                op("pe", lambda e: e.matmul(
                                    pb[ybank][:, lp * 64:(lp + 1) * 64], lhsT=yb[:, lp, :, :].rearrange("p g h -> p (g h)"),
                                    rhs=identb[0:64, 0:64], start=False, stop=(lp == 7)),
                                   reads=[ykey, "identb"], writes=[pbk[ybank]], sig=(lp == 7))
                            op("act", lambda e: e.activation(out=zTp[:, pc, :], in_=pb[ybank][:], func=AF.Gelu_apprx_tanh),
                               reads=[pbk[ybank]], writes=["zTp"])
                        wv3, wvk = pf.get(i0)
                        wg3, wgk = pf.get(i0 + 1)
                        for c in range(4):
                            bv_, bg_ = 4 + 2 * (c % 2), 5 + 2 * (c % 2)
                            for (w3, wk, bank) in ((wv3, wvk, bv_), (wg3, wgk, bg_)):
                                for kc in range(8):
                                    op("pe", lambda e: e.matmul(
                                        pb[bank][:], lhsT=w3[:, kc, c * 128:(c + 1) * 128], rhs=zTp[:, kc, :],
                                        start=(kc == 0), stop=(kc == 7)), reads=[wk, "zTp"], writes=[pbk[bank]], sig=(kc == 7))
                            sg_, va_ = sgt[c % 2], vat[c % 2]
                            op("act", lambda e: e.activation(out=sg_[:], in_=pb[bg_][:], func=AF.Sigmoid, bias=bglu[:, 4 + c:5 + c]),
                               reads=[pbk[bg_], "bglu"], writes=[f"sgt{c % 2}"])
                            op("act", lambda e: e.activation(out=va_[:], in_=pb[bv_][:], func=AF.Identity, bias=bglu[:, c:c + 1]),
                               reads=[pbk[bv_], "bglu"], writes=[f"vat{c % 2}"])
                            op("pool", lambda e: e.tensor_tensor(out=va_[:], in0=va_[:], in1=sg_[:], op=ALU.mult),
                               reads=[f"vat{c % 2}", f"sgt{c % 2}"], writes=[f"vat{c % 2}"])
                            op("pool", lambda e: e.tensor_tensor(out=ysT[:, c, :].rearrange("p (s l) -> p l s", l=8),
                                                                 in0=va_[:].rearrange("p (l s) -> p l s", l=8),
                                                                 in1=sgs[:, c, :].rearrange("p (l s) -> p l s", l=8), op=ALU.mult),
                               reads=[f"vat{c % 2}", sk], writes=["ysT"])
                        kb.dma("sp", yss_d[L, :, :, j * TT:(j + 1) * TT].rearrange("c p t -> p c t"), ysT[:], "st_yss",
                               reads=["ysT"])

                    for j in range(NT + 1):
                        pids = ([0, 1, 2] if j < NT else []) + ([7, 8] if j >= 1 else [])
                        pf = Pref(L, pids, "act")
                        if j < NT:
                            stage_X(j, pf)
                        if j >= 1:
                            stage_Y(j - 1, pf, 3 if j < NT else 0)
                    kb.barrier()
                if stop_at == f"s1_{L}":
                    dd = nc.dram_tensor("dbg_yss", [4, 128, S], BF16, kind="ExternalOutput").ap()
                    kb.dma("sp", dd, yss_d[L], "st_dbgyss")
                    break

                with ExitStack() as s2:
                    kT = kb.sb("kT", [128, 4, S], BF16, s2)
                    Vc = kb.sb("Vc", [128, S // 128, 512], BF16, s2)
                    htile = kb.sb("htile", [128, NSUB, D], F32, s2)
                    hsub = [htile[:, 0, :], htile[:, 1, :]]
                    hnt = [kb.sb(f"hnt{i}", [128, D], BF16, s2) for i in range(2)]
                    ss = kb.sb("ssq", [128, 4], F32, s2)
                    lnv = kb.sb("lnv", [128, 4], F32, s2)
                    rstd = kb.sb("rstd", [128, 4], F32, s2)
                    sqj = kb.sb("sqj", [128, D], BF16, s2)
                    hnT = kb.sb("hnT", [128, 8, TT], BF16, s2)
                    qT = kb.sb("qT", [128, 4, TT], BF16, s2)
                    sga = kb.sb("sga", [128, 4, TT], BF16, s2)
                    yT = kb.sb("yT", [128, 8, TT], BF16, s2)
                    sq = [kb.sb(f"sq{i}", [128, TT], BF16, s2) for i in range(2)]
                    sd = [kb.sb(f"sd{i}", [128, TT], F32, s2) for i in range(2)]
                    e_sb = [kb.sb(f"e_sb{i}", [128, TT], BF16, s2) for i in range(6)]
                    sp_sb = [kb.sb(f"sp_sb{i}", [128, TT], BF16, s2) for i in range(3)]
                    x_sb = [kb.sb(f"x_sb{i}", [128, TT], BF16, s2) for i in range(2)]
                    w_sb = [kb.sb(f"w_sb{i}", [128, TT], BF16, s2) for i in range(3)]
                    S_sb = [kb.sb(f"S_sb{i}", [128, TT], BF16, s2) for i in range(3)]
                    psub = [kb.sb(f"psub{i}", [128, 256], F32, s2) for i in range(2)]
                    pbf = [kb.sb(f"pbf{i}", [128, 256], BF16, s2) for i in range(2)]
                    pT = kb.sb("pT", [128, 2, TT], BF16, s2)
                    gsb = [kb.sb(f"gsb{i}", [128, TT], F32, s2) for i in range(2)]
                    gpp = [kb.sb(f"gpp{i}", [128, TT], F32, s2) for i in range(2)]

                    class _Sub:
                        pass

                    tctr = [0]
                    HT = ["htile", "hsub0", "hsub1"]
                    for j in range(NT):
                        pf = Pref(L, [3, 4, 5, 6, 9, 10, 11, 13, 12], "dve")
                        nb = (hsub, hnt, ss, lnv, rstd, sqj, hnT)
                        norm_phase(nb, lambda i: h_src[j * TT + i * 128:j * TT + (i + 1) * 128, :], gmix, "gmix", "dve")
                        for which in range(2):
                            w3, wk = pf.get(which)
                            gvec = gq if which == 0 else gk
                            gkey = "gq" if which == 0 else "gk"
                            for c in range(4):
                                bank = 2 + c % 2
                                proj_fm(w3, wk, c * 128, hnT, bank)
                                sq_, sd_ = sq[c % 2], sd[c % 2]
                                op("act", lambda e, sq_=sq_, bank=bank: e.activation(out=sq_[:], in_=pb[bank][:], func=AF.Square),
                                   reads=[pbk[bank]], writes=[f"sq{c % 2}"])
                                sbank = 4 + c % 2
                                op("pe", lambda e, sq_=sq_, sbank=sbank: e.matmul(pb[sbank][:], lhsT=blockones[:], rhs=sq_[:],
                                                                                   start=True, stop=True),
                                   reads=[f"sq{c % 2}", "blockones"], writes=[pbk[sbank]])
                                op("act", lambda e, sd_=sd_, sbank=sbank: e.activation(out=sd_[:], in_=pb[sbank][:], func=AF.Ln,
                                                                                      scale=1.0 / 64.0, bias=EPS),
                                   reads=[pbk[sbank]], writes=[f"sd{c % 2}"])
                                op("act", lambda e, sd_=sd_: e.activation(out=sd_[:], in_=sd_[:], func=AF.Exp, scale=-0.5),
                                   reads=[f"sd{c % 2}"], writes=[f"sd{c % 2}"])
                                dst = qT[:, c, :] if which == 0 else kT[:, c, j * TT:(j + 1) * TT]
                                op("dve", lambda e, dst=dst, bank=bank, sd_=sd_, gvec=gvec: e.scalar_tensor_tensor(
                                    out=dst, in0=pb[bank][:], scalar=gvec[:, 0:1], in1=sd_[:], op0=ALU.mult, op1=ALU.mult),
                                   reads=[pbk[bank], gkey, f"sd{c % 2}"], writes=["qT" if which == 0 else "kT"])
                        w3, wk = pf.get(2)
                        for i in range(NSUB):
                            bank = 2 + i % 2
                            for kc in range(8):
                                op("pe", lambda e, kc=kc, i=i, bank=bank: e.matmul(
                                    pb[bank][:], lhsT=hnT[:, kc, i * 128:(i + 1) * 128], rhs=w3[:, kc, :],
                                    start=(kc == 0), stop=(kc == 7)), reads=[wk, "hnT"], writes=[pbk[bank]], sig=(kc == 7))
                            op("dve", lambda e, i=i, bank=bank: e.tensor_copy(out=Vc[:, 4 * j + i, :], in_=pb[bank][:]),
                               reads=[pbk[bank]], writes=["Vc"])
                        w3, wk = pf.get(3)
                        for c in range(4):
                            bank = 2 + c % 2
                            proj_fm(w3, wk, c * 128, hnT, bank)
                            op("act", lambda e, c=c, bank=bank: e.activation(out=sga[:, c, :], in_=pb[bank][:], func=AF.Silu),
                               reads=[pbk[bank]], writes=["sga"])
                        tiles = []
                        for h in range(8):
                            nblk = 4 * j + 4
                            for kb_ in range(nblk - 1, -1, -1):
                                diag = kb_ >= 4 * j
                                tiles.append(dict(h=h, c=h // 2, base=64 * (h % 2), obank=6 + h % 2, kb=kb_, diag=diag,
                                                  qoff=(kb_ - 4 * j) * 128 if diag else 0,
                                                  first=(kb_ == nblk - 1), last=(kb_ == 0)))
                        s_idx = 0
                        for t_ in tiles:
                            if t_["first"]:
                                t_["s_in"], t_["s_off"] = None, None
                            else:
                                t_["s_in"] = s_idx
                                t_["s_off"] = (t_["qoff"] + 128) if t_["diag"] else 0
                            if not t_["last"]:
                                s_idx = (s_idx + 1) % 3
                                t_["s_out"] = s_idx
                            else:
                                t_["s_out"] = None
                        NTL = len(tiles)

                        def st_QK(i):
                            t_ = tiles[i]
                            g_ = tctr[0] + i
                            zb = g_ % 3
                            qo, c, base, kb_ = t_["qoff"], t_["c"], t_["base"], t_["kb"]
                            op("pe", lambda e: e.matmul(pb[zb][:, qo:TT], lhsT=kT[base:base + 64, c, kb_ * 128:(kb_ + 1) * 128],
                                                        rhs=qT[base:base + 64, c, qo:TT], start=True, stop=True),
                               reads=["kT", "qT"], writes=[pbk[zb]])

                        def st_A1(i):
                            t_ = tiles[i]
                            g_ = tctr[0] + i
                            zb, e_, ek = g_ % 3, e_sb[g_ % 6], f"e_sb{g_ % 6}"
                            qo = t_["qoff"]
                            op("act", lambda e: e.activation(out=e_[:, qo:TT], in_=pb[zb][:, qo:TT], func=AF.Exp),
                               reads=[pbk[zb]], writes=[ek])
                            if t_["diag"]:
                                op("pool", lambda e: e.tensor_tensor(out=e_[:, qo:qo + 128], in0=e_[:, qo:qo + 128],
                                                                     in1=masklt[:], op=ALU.mult), reads=[ek, "masklt"], writes=[ek])

                        def st_A2(i):
                            t_ = tiles[i]
                            g_ = tctr[0] + i
                            e_, ek = e_sb[g_ % 6], f"e_sb{g_ % 6}"
                            sp_, spk = sp_sb[g_ % 3], f"sp_sb{g_ % 3}"
                            qo = t_["qoff"]
                            op("act", lambda e: e.activation(out=sp_[:, qo:TT], in_=e_[:, qo:TT], func=AF.Ln, bias=1.0),
                               reads=[ek], writes=[spk])
                            if t_["s_out"] is not None:
                                sn_t, sn_k = S_sb[t_["s_out"]], f"S_sb{t_['s_out']}"
                                if t_["s_in"] is None:
                                    op("dve", lambda e: e.tensor_copy(out=sn_t[:, qo:TT], in_=sp_[:, qo:TT]), reads=[spk], writes=[sn_k])
                                else:
                                    sc_t, sc_k = S_sb[t_["s_in"]], f"S_sb{t_['s_in']}"
                                    a0 = 0
                                    if t_["diag"]:
                                        op("dve", lambda e: e.tensor_copy(out=sn_t[:, qo:qo + 128], in_=sp_[:, qo:qo + 128]),
                                           reads=[spk], writes=[sn_k])
                                        a0 = qo + 128
                                    op("dve", lambda e: e.tensor_tensor(out=sn_t[:, a0:TT], in0=sc_t[:, a0:TT], in1=sp_[:, a0:TT],
                                                                        op=ALU.add), reads=[sc_k, spk], writes=[sn_k])

                        def st_B(i):
                            t_ = tiles[i]
                            g_ = tctr[0] + i
                            cb = 3 + g_ % 2
                            sp_, spk = sp_sb[g_ % 3], f"sp_sb{g_ % 3}"
                            qo = t_["qoff"]
                            has_s = t_["s_in"] is not None and t_["s_off"] < TT
                            op("pe", lambda e: e.matmul(pb[cb][:, qo:TT], lhsT=tri[:], rhs=sp_[:, qo:TT], start=True, stop=not has_s),
                               reads=["tri", spk], writes=[pbk[cb]])
                            if has_s:
                                sc_t, sc_k, so = S_sb[t_["s_in"]], f"S_sb{t_['s_in']}", t_["s_off"]
                                op("pe", lambda e: e.matmul(pb[cb][:, so:TT], lhsT=ones[:], rhs=sc_t[:, so:TT], start=False, stop=True),
                                   reads=["ones", sc_k], writes=[pbk[cb]])

                        def st_C1(i):
                            t_ = tiles[i]
                            g_ = tctr[0] + i
                            cb = 3 + g_ % 2
                            e_, ek = e_sb[g_ % 6], f"e_sb{g_ % 6}"
                            x_, xk = x_sb[g_ % 2], f"x_sb{g_ % 2}"
                            w_, wk_ = w_sb[g_ % 3], f"w_sb{g_ % 3}"
                            qo = t_["qoff"]
                            op("act", lambda e: e.activation(out=x_[:, qo:TT], in_=pb[cb][:, qo:TT], func=AF.Exp, scale=-1.0),
                               reads=[pbk[cb]], writes=[xk])
                            op("dve", lambda e: e.tensor_tensor(out=w_[:, qo:TT], in0=e_[:, qo:TT], in1=x_[:, qo:TT], op=ALU.mult),
                               reads=[ek, xk], writes=[wk_])

                        def st_C2(i):
                            t_ = tiles[i]
                            g_ = tctr[0] + i
                            w_, wk_ = w_sb[g_ % 3], f"w_sb{g_ % 3}"
                            qo, h, c, base, ob = t_["qoff"], t_["h"], t_["c"], t_["base"], t_["obank"]
                            op("pe", lambda e: e.matmul(pb[ob][base:base + 64, qo:TT], lhsT=Vc[:, t_["kb"], h * 64:(h + 1) * 64],
                                                        rhs=w_[:, qo:TT], start=t_["first"], stop=t_["last"], skip_group_check=True),
                               reads=["Vc", wk_], writes=[pbk[ob]])
                            if t_["last"]:
                                op("dve", lambda e: e.tensor_tensor(out=yT[base:base + 64, 4 + c, :], in0=pb[ob][base:base + 64, :],
                                                                    in1=sga[base:base + 64, c, :], op=ALU.mult),
                                   reads=[pbk[ob], "sga"], writes=["yT"])

                        for i in range(-1, NTL + 4):
                            if 0 <= i + 1 < NTL:
                                st_QK(i + 1)
                            if 0 <= i < NTL:
                                st_A1(i)
                            if 0 <= i - 1 < NTL:
                                st_A2(i - 1)
                            if 0 <= i - 2 < NTL:
                                st_B(i - 2)
                            if 0 <= i - 4 < NTL:
                                st_C2(i - 4)
                            if 0 <= i - 3 < NTL:
                                st_C1(i - 3)
                        tctr[0] += NTL
                        kb.dma("sp", yT[:, 0:4, :], yss_d[L, :, :, j * TT:(j + 1) * TT].rearrange("c p t -> p c t"), "ld_yss",
                               writes=["yT"])
                        kb.dma("sp", htile[:], h_src[j * TT:(j + 1) * TT, :].rearrange("(i p) d -> p i d", p=128), "ld_htile", writes=HT)
                        for half in range(2):
                            w3, wk = pf.get(4 + half)
                            for i in range(NSUB):
                                bank = 4 + i % 2
                                for kc in range(8):
                                    op("pe", lambda e, kc=kc, i=i, bank=bank, w3=w3: e.matmul(
                                        pb[bank][:], lhsT=yT[:, kc, i * 128:(i + 1) * 128], rhs=w3[:, kc, :],
                                        start=(kc == 0), stop=(kc == 7)), reads=[wk, "yT"], writes=[pbk[bank]], sig=(kc == 7))
                                op("dve", lambda e, i=i, bank=bank, half=half: e.tensor_tensor(
                                    out=htile[:, i, half * 512:(half + 1) * 512], in0=htile[:, i, half * 512:(half + 1) * 512],
                                    in1=pb[bank][:], op=ALU.add), reads=[pbk[bank]] + HT, writes=HT)
                        if dbg and j == 0 and L == 0:
                            kb.dma("sp", dbg_out("dbg_h1", [128, NSUB, D]), htile[:], "st_dbg1", reads=HT)
                        for i in range(NSUB):
                            hs = htile[:, i, :]
                            op("act", lambda e, hs=hs, i=i: e.activation(out=sqj[:], in_=hs, func=AF.Square, accum_out=ss[:, i:i + 1]),
                               reads=HT, writes=["sqj", f"ss{i}"])
                            op("act", lambda e, i=i: e.activation(out=lnv[:, i:i + 1], in_=ss[:, i:i + 1], func=AF.Ln,
                                                                  scale=1.0 / D, bias=EPS), reads=[f"ss{i}"], writes=[f"lnv{i}"])
                            op("act", lambda e, i=i: e.activation(out=rstd[:, i:i + 1], in_=lnv[:, i:i + 1], func=AF.Exp, scale=-0.5),
                               reads=[f"lnv{i}"], writes=[f"rstd{i}"])
                            ht = hnt[i % 2]
                            tk = f"hnt{i % 2}"
                            op("dve", lambda e, hs=hs, ht=ht, i=i: e.tensor_scalar(out=ht[:], in0=hs, scalar1=rstd[:, i:i + 1],
                                                                                    scalar2=None, op0=ALU.mult),
                               reads=HT + [f"rstd{i}"], writes=[tk])
                            bk = i % 2
                            pv = pb[bk][:].bitcast(BF16)
                            for kc in range(8):
                                op("pe", lambda e, kc=kc, ht=ht, pv=pv: e.transpose(out=pv[:, kc * 128:(kc + 1) * 128],
                                                                                   in_=ht[:, kc * 128:(kc + 1) * 128], identity=identb[:]),
                                   reads=[tk, "identb"], writes=[pbk[bk]], sig=(kc == 7))
                            op("dve", lambda e, i=i, pv=pv: e.tensor_tensor(out=hnT[:, :, i * 128:(i + 1) * 128],
                                                                            in0=pv.rearrange("p (k t) -> p k t", k=8),
                                                                            in1=gple[:, :].unsqueeze(2).to_broadcast([128, 8, 128]),
                                                                            op=ALU.mult), reads=[pbk[bk], "gple"], writes=["hnT"])
                            ps_, pb_ = psub[i % 2], pbf[i % 2]
                            kb.dma("sp", ps_[:], p_d[L, j * TT + i * 128:j * TT + (i + 1) * 128, :], f"ld_psub{i % 2}",
                                   writes=[f"psub{i % 2}"])
                            op("dve", lambda e, ps_=ps_, pb_=pb_: e.tensor_copy(out=pb_[:], in_=ps_[:]),
                               reads=[f"psub{i % 2}"], writes=[f"pbf{i % 2}"])
                            pv2 = pb[2 + i % 2][:].bitcast(BF16)
                            for k2 in range(2):
                                op("pe", lambda e, k2=k2, pb_=pb_, pv2=pv2: e.transpose(out=pv2[:, k2 * 128:(k2 + 1) * 128],
                                                                                       in_=pb_[:, k2 * 128:(k2 + 1) * 128],
                                                                                       identity=identb[:]),
                                   reads=[f"pbf{i % 2}", "identb"], writes=[pbk[2 + i % 2]], sig=(k2 == 1))
                            op("dve", lambda e, i=i, pv2=pv2: e.tensor_copy(out=pT[:, :, i * 128:(i + 1) * 128],
                                                                            in_=pv2[:, 0:256].rearrange("p (k t) -> p k t", k=2)),
                               reads=[pbk[2 + i % 2]], writes=["pT"])
                        wpp3, wppk = None, None
                        for half in range(2):
                            w3, wk = pf.get(6 if half == 0 else 8)
                            if wpp3 is None:
                                wpp3, wppk = pf.get(7)
                            for i in range(NSUB):
                                gb, pbk_ = 4 + i % 2, 6 + i % 2
                                for kc in range(8):
                                    op("pe", lambda e, kc=kc, i=i, gb=gb, w3=w3: e.matmul(
                                        pb[gb][:], lhsT=hnT[:, kc, i * 128:(i + 1) * 128], rhs=w3[:, kc, :],
                                        start=(kc == 0), stop=(kc == 7)), reads=[wk, "hnT"], writes=[pbk[gb]], sig=(kc == 7))
                                for k2 in range(2):
                                    op("pe", lambda e, k2=k2, i=i, pbk_=pbk_, half=half: e.matmul(
                                        pb[pbk_][:], lhsT=pT[:, k2, i * 128:(i + 1) * 128],
                                        rhs=wpp3[:, k2, half * 512:(half + 1) * 512], start=(k2 == 0), stop=(k2 == 1)),
                                       reads=[wppk, "pT"], writes=[pbk[pbk_]], sig=(k2 == 1))
                                g_, gp_ = gsb[i % 2], gpp[i % 2]
                                op("act", lambda e, g_=g_, gb=gb: e.activation(out=g_[:], in_=pb[gb][:], func=AF.Sigmoid),
                                   reads=[pbk[gb]], writes=[f"gsb{i % 2}"])
                                op("dve", lambda e, g_=g_, gp_=gp_, pbk_=pbk_: e.tensor_tensor(out=gp_[:], in0=g_[:], in1=pb[pbk_][:],
                                                                                               op=ALU.mult),
                                   reads=[f"gsb{i % 2}", pbk[pbk_]], writes=[f"gpp{i % 2}"])
                                op("dve", lambda e, i=i, half=half, gp_=gp_: e.tensor_tensor(
                                    out=htile[:, i, half * 512:(half + 1) * 512], in0=htile[:, i, half * 512:(half + 1) * 512],
                                    in1=gp_[:], op=ALU.add), reads=[f"gpp{i % 2}"] + HT, writes=HT)
                        kb.dma("sp", h_dst[j * TT:(j + 1) * TT, :].rearrange("(i p) d -> p i d", p=128), htile[:], "st_h", reads=HT)
                    kb.barrier()
                if stop_at == f"s2_{L}":
                    dd = nc.dram_tensor("dbg_h", [S, D], F32, kind="ExternalOutput").ap()
                    kb.dma("sp", dd, h_dst, "st_dbgh")
                    break
        kb.finish("sp")
        build_program.stats = (kb.ninst, kb.nwaits)
    return nc


def _prep_shared(inp):
    f = np.float32
    w_in = np.asarray(inp["w_in"], f)
    u_cols = w_in[:, :, 0:512].reshape(2, D, 32, 16)
    u_pad = np.zeros((2, D, 32, 32), f)
    u_pad[:, :, :, 0:16] = u_cols
    w_in_p = np.concatenate([u_pad.reshape(2, D, 1024), w_in[:, :, 512:]], axis=2)
    w_glu = np.asarray(inp["ssm_w_glu"], f).reshape(2, 32, 16, 1024)
    w_glu_p = np.zeros((2, 32, 32, 1024), f)
    w_glu_p[:, :, 0:16, :] = w_glu
    w_glu_p = w_glu_p.reshape(2, 1024, 1024)

    def colT(v):
        return np.ascontiguousarray(np.asarray(v, f).reshape(2, 8, 128).transpose(0, 2, 1))

    d = np.asarray(inp["ssm_d"], f)
    d_pad = np.zeros((2, 32, 32), f)
    d_pad[:, :, 0:16] = d
    dpad = np.ascontiguousarray(d_pad.reshape(2, 8, 128).transpose(0, 2, 1))
    gq = np.tile(np.asarray(inp["q_norm_g"], f), (1, 2)).reshape(2, 128, 1)
    gk = np.tile(np.asarray(inp["k_norm_g"], f), (1, 2)).reshape(2, 128, 1)
    cre = np.asarray(inp["ssm_c_re"], f).reshape(2, 4, 128, 64)
    cim = np.asarray(inp["ssm_c_im"], f).reshape(2, 4, 128, 64)
    return {
        "w_in": np.ascontiguousarray(w_in_p), "w_glu": w_glu_p,
        "w_out": np.asarray(inp["w_out"], f), "w_pg": np.asarray(inp["w_ple_gate"], f),
        "w_pp": np.asarray(inp["w_ple_proj"], f),
        "gmix": colT(inp["mix_norm_g"]), "gple": colT(inp["ple_norm_g"]), "bglu": colT(inp["ssm_b_glu"]),
        "dpad": dpad, "gq": np.ascontiguousarray(gq), "gk": np.ascontiguousarray(gk),
        "a_re": np.asarray(inp["ssm_a_re"], f), "a_im": np.asarray(inp["ssm_a_im"], f),
        "logdt": np.asarray(inp["ssm_log_dt"], f).reshape(2, 32, 1),
        "b_re": np.asarray(inp["ssm_b_re"], f), "b_im": np.asarray(inp["ssm_b_im"], f),
        "ccat": np.ascontiguousarray(np.concatenate([cre, cim], axis=3)),
        "ccatsw": np.ascontiguousarray(np.concatenate([cim, cre], axis=3)),
    }


def kernel(**inputs):
    shared = _prep_shared(inputs)
    x = np.asarray(inputs["x"], np.float32)
    p = np.asarray(inputs["p"], np.float32)
    nc = build_program()
    in_maps = []
    for b in range(NCORES):
        m = dict(shared)
        m["x"] = np.ascontiguousarray(x[b])
        m["p"] = np.ascontiguousarray(p[:, b])
        in_maps.append(m)
    res = run_bass_kernel_spmd(nc, in_maps, core_ids=list(range(NCORES)))
    return np.stack([r["out"] for r in res.results], axis=0).astype(np.float32)
```

```python
import math
from contextlib import ExitStack
import numpy as np
import concourse.bass as bass
import concourse.mybir as mybir
from concourse.bass_utils import run_bass_kernel_spmd

F32 = mybir.dt.float32
BF16 = mybir.dt.bfloat16
AF = mybir.ActivationFunctionType
ALU = mybir.AluOpType

S = 4096
D = 1024
TT = 512
NT = S // TT
NSUB = TT // 128
EPS = 1e-6
NPIECE = 14
NCORES = 8


class _Eng:
    def __init__(self, name, handle, sem):
        self.name = name
        self.h = handle
        self.sem = sem
        self.count = 0
        self.waited = {}


class KB:
    def __init__(self, nc, stack):
        self.nc = nc
        self.stack = stack
        self.eng = {}
        for name, h in (("pe", nc.tensor), ("act", nc.scalar), ("dve", nc.vector),
                        ("pool", nc.gpsimd), ("sp", nc.sync)):
            sem = stack.enter_context(nc.semaphore("s_" + name))
            self.eng[name] = _Eng(name, h, sem)
        self.state = {}
        self.dsem = {}
        self.nwaits = 0
        self.ninst = 0

    def sb(self, name, shape, dt, stack=None):
        self._uid = getattr(self, "_uid", 0) + 1
        return (stack or self.stack).enter_context(self.nc.sbuf_tensor(f"{name}_{self._uid}", list(shape), dt))

    def ps(self, name, shape, dt=F32):
        return self.stack.enter_context(self.nc.psum_tensor(name, list(shape), dt))

    def dma_sem(self, name):
        if name not in self.dsem:
            sem = self.stack.enter_context(self.nc.semaphore("d_" + name))
            self.dsem[name] = [sem, 0]
        return self.dsem[name]

    def _st(self, k):
        s = self.state.get(k)
        if s is None:
            s = [None, []]
            self.state[k] = s
        return s

    def _wait(self, e, sem, val):
        key = id(sem)
        if e.waited.get(key, 0) >= val:
            return
        e.h.wait_ge(sem, val)
        e.waited[key] = val
        self.nwaits += 1

    def _deps(self, e, reads, writes):
        for k in reads:
            w = self._st(k)[0]
            if w is not None:
                self._wait(e, w[0], w[1])
        pe = e.name == "pe"
        for k in writes:
            s = self._st(k)
            if s[0] is not None and not (pe and s[0][0] is e.sem):
                self._wait(e, s[0][0], s[0][1])
            for (sem, val) in s[1]:
                if not (pe and sem is e.sem):
                    self._wait(e, sem, val)

    def _commit(self, tag, reads, writes):
        for k in reads:
            r = self._st(k)[1]
            for idx, (sem, val) in enumerate(r):
                if sem is tag[0]:
                    r[idx] = tag if tag[1] > val else (sem, val)
                    break
            else:
                r.append(tag)
        for k in writes:
            s = self._st(k)
            s[0] = tag
            s[1] = []

    def op(self, en, fn, reads=(), writes=(), sig=True):
        e = self.eng[en]
        self._deps(e, reads, writes)
        ins = fn(e.h)
        self.ninst += 1
        if sig:
            e.count += 1
            ins.then_inc(e.sem, 1)
            tag = (e.sem, e.count)
        else:
            tag = (e.sem, e.count + 1)
        self._commit(tag, reads, writes)
        return ins

    def dma(self, qn, out, in_, semname, reads=(), writes=(), **kw):
        e = self.eng[qn]
        ds = self.dma_sem(semname)
        if ds[1] > 0:
            self._wait(e, ds[0], ds[1])
        self._deps(e, reads, writes)
        ins = e.h.dma_start(out=out, in_=in_, **kw)
        ds[1] += 16
        ins.then_inc(ds[0], 16)
        self.ninst += 1
        self._commit((ds[0], ds[1]), reads, writes)
        return ins

    def barrier(self):
        snap_e = [(o.sem, o.count) for o in self.eng.values() if o.count]
        snap_d = [(sem, cnt) for (sem, cnt) in self.dsem.values() if cnt]
        for e in self.eng.values():
            for sem, cnt in snap_e:
                if sem is not e.sem:
                    self._wait(e, sem, cnt)
            for sem, cnt in snap_d:
                self._wait(e, sem, cnt)
        self.state = {}

    def finish(self, en="sp"):
        e = self.eng[en]
        for name, (sem, cnt) in self.dsem.items():
            if cnt:
                self._wait(e, sem, cnt)
        for o in self.eng.values():
            if o.count and o is not e:
                self._wait(e, o.sem, o.count)


def build_program(stop_at=None, dbg=False):
    nc = bass.Bass("TRN2", target_bir_lowering=False)

    def din(name, shape):
        return nc.dram_tensor(name, list(shape), F32, kind="ExternalInput").ap()

    x_d = din("x", [S, D])
    p_d = din("p", [2, S, 256])
    w_in_d = din("w_in", [2, D, 3584])
    w_glu_d = din("w_glu", [2, D, 1024])
    w_out_d = din("w_out", [2, D, D])
    w_pg_d = din("w_pg", [2, D, D])
    w_pp_d = din("w_pp", [2, 256, D])
    gmix_d = din("gmix", [2, 128, 8])
    gple_d = din("gple", [2, 128, 8])
    bglu_d = din("bglu", [2, 128, 8])
    dpad_d = din("dpad", [2, 128, 8])
    gq_d = din("gq", [2, 128, 1])
    gk_d = din("gk", [2, 128, 1])
    are_d = din("a_re", [2, 32, 64])
    aim_d = din("a_im", [2, 32, 64])
    ldt_d = din("logdt", [2, 32, 1])
    bre_d = din("b_re", [2, 32, 64, 16])
    bim_d = din("b_im", [2, 32, 64, 16])
    ccat_d = din("ccat", [2, 4, 128, 128])
    ccsw_d = din("ccatsw", [2, 4, 128, 128])
    out_d = nc.dram_tensor("out", [S, D], F32, kind="ExternalOutput").ap()
    wbf_d = nc.dram_tensor("wbf", [2, NPIECE, 128, 4096], BF16, kind="Internal").ap()
    yss_d = nc.dram_tensor("yss", [2, 4, 128, S], BF16, kind="Internal").ap()
    hmid_d = nc.dram_tensor("hmid", [S, D], F32, kind="Internal").ap()
    dbg_d = {}

    def dbg_out(name, shape):
        dbg_d[name] = nc.dram_tensor(name, list(shape), F32, kind="ExternalOutput").ap()
        return dbg_d[name]

    with ExitStack() as st:
        kb = KB(nc, st)
        op = kb.op

        pb = [kb.ps(f"pb{i}", [128, 512], F32) for i in range(8)]
        pbk = [f"pb{i}" for i in range(8)]
        identb = kb.sb("identb", [128, 128], BF16)
        identf = kb.sb("identf", [128, 128], F32)
        tri = kb.sb("tri", [128, 128], BF16)
        ones = kb.sb("ones", [128, 128], BF16)
        masklt = kb.sb("masklt", [128, 128], BF16)
        blockones = kb.sb("blockones", [128, 128], BF16)
        bdmask = kb.sb("bdmask", [128, 128], F32)
        NRING = 3
        ring = [kb.sb(f"ring{i}", [128, 4096], BF16) for i in range(NRING)]
        stage = [kb.sb(f"stage{i}", [128, 1024], F32) for i in range(2)]

        def mk_affine(t, key, pattern, cm, cmp_):
            op("pool", lambda e: e.memset(t[:], 1.0), writes=[key])
            op("pool", lambda e: e.affine_select(out=t[:], in_=t[:], pattern=pattern, compare_op=cmp_, fill=0.0,
                                                 base=0, channel_multiplier=cm), reads=[key], writes=[key])

        mk_affine(identb, "identb", [[-1, 128]], 1, ALU.is_equal)
        mk_affine(identf, "identf", [[-1, 128]], 1, ALU.is_equal)
        mk_affine(tri, "tri", [[-1, 128]], 1, ALU.is_ge)
        mk_affine(masklt, "masklt", [[1, 128]], -1, ALU.is_gt)
        op("pool", lambda e: e.memset(ones[:], 1.0), writes=["ones"])
        op("pool", lambda e: e.memset(blockones[:], 0.0), writes=["blockones"])
        for hh in range(2):
            op("pool", lambda e, hh=hh: e.memset(blockones[64 * hh:64 * hh + 64, 64 * hh:64 * hh + 64], 1.0),
               writes=["blockones"])
        op("pool", lambda e: e.memset(bdmask[:], 0.0), writes=["bdmask"])
        for gg in range(4):
            op("pool", lambda e, gg=gg: e.memset(bdmask[32 * gg:32 * gg + 32, 32 * gg:32 * gg + 32], 1.0),
               writes=["bdmask"])

        def piece_src(L, pid):
            if pid <= 6:
                return w_in_d[L, :, pid * 512:(pid + 1) * 512].rearrange("(k p) n -> p k n", p=128), 8, 512
            if pid <= 8:
                return w_glu_d[L, :, (pid - 7) * 512:(pid - 6) * 512].rearrange("(k p) n -> p k n", p=128), 8, 512
            if pid <= 10:
                return w_out_d[L, :, (pid - 9) * 512:(pid - 8) * 512].rearrange("(k p) n -> p k n", p=128), 8, 512
            if pid <= 12:
                return w_pg_d[L, :, (pid - 11) * 512:(pid - 10) * 512].rearrange("(k p) n -> p k n", p=128), 8, 512
            return w_pp_d[L].rearrange("(k p) n -> p k n", p=128), 2, 1024

        converted = set()
        ring_ctr = [0]

        class _Shift:
            def __init__(self, pf, off):
                self.pf, self.off = pf, off

            def get(self, idx, ahead=2):
                return self.pf.get(idx + self.off, ahead)

        class Pref:
            def __init__(self, L, pids, ceng="dve"):
                self.L = L
                self.pids = pids
                self.loaded = []
                self.ceng = ceng

            def _load(self, idx):
                L, pid = self.L, self.pids[idx]
                s = ring_ctr[0] % NRING
                ring_ctr[0] += 1
                key = f"ring{s}"
                src, nk, ncol = piece_src(L, pid)
                n = nk * ncol
                v3 = ring[s][:, 0:n].rearrange("p (k n) -> p k n", k=nk)
                if (L, pid) not in converted:
                    ck = max(1, 1024 // ncol)
                    nh = ck * ncol
                    for q_ in range(nk // ck):
                        hf = q_ % 2
                        sv = stage[hf][:, 0:nh].rearrange("p (k n) -> p k n", k=ck)
                        kb.dma("sp", sv, src[:, q_ * ck:(q_ + 1) * ck, :], f"ld_stage{hf}", writes=[f"stage{hf}"])
                        if self.ceng == "act":
                            op("act", lambda e: e.activation(out=ring[s][:, q_ * nh:(q_ + 1) * nh], in_=stage[hf][:, 0:nh],
                                                             func=AF.Copy), reads=[f"stage{hf}"], writes=[key])
                        else:
                            op("dve", lambda e: e.tensor_copy(out=ring[s][:, q_ * nh:(q_ + 1) * nh], in_=stage[hf][:, 0:nh]),
                               reads=[f"stage{hf}"], writes=[key])
                    kb.dma("sp", wbf_d[L, pid, :, 0:n], ring[s][:, 0:n], f"st_ring{s}",
                           reads=[key], writes=[f"wbf{L}_{pid}"])
                    converted.add((L, pid))
                else:
                    kb.dma("sp", ring[s][:, 0:n], wbf_d[L, pid, :, 0:n], f"ld_ring{s}",
                           reads=[f"wbf{L}_{pid}"], writes=[key])
                self.loaded.append((v3, key))

            def get(self, idx, ahead=2):
                while len(self.loaded) <= min(idx + ahead, len(self.pids) - 1):
                    self._load(len(self.loaded))
                return self.loaded[idx]

        def norm_phase(stk_bufs, src_rows, gain, tagp, seng):
            hsub, hnt, ss, lnv, rstd, sqj, hnT = stk_bufs
            for i in range(NSUB):
                hs = hsub[i % 2]
                hk = f"hsub{i % 2}"
                kb.dma("sp", hs[:], src_rows(i), f"ld_hsub{i % 2}", writes=[hk])
                op("act", lambda e: e.activation(out=sqj[:], in_=hs[:], func=AF.Square, accum_out=ss[:, i:i + 1]),
                   reads=[hk], writes=["sqj", f"ss{i}"])
                op("act", lambda e: e.activation(out=lnv[:, i:i + 1], in_=ss[:, i:i + 1], func=AF.Ln,
                                                 scale=1.0 / D, bias=EPS), reads=[f"ss{i}"], writes=[f"lnv{i}"])
                op("act", lambda e: e.activation(out=rstd[:, i:i + 1], in_=lnv[:, i:i + 1], func=AF.Exp, scale=-0.5),
                   reads=[f"lnv{i}"], writes=[f"rstd{i}"])
                ht = hnt[i % 2]
                tk = f"hnt{i % 2}"
                if seng == "act":
                    op("act", lambda e: e.activation(out=ht[:], in_=hs[:], func=AF.Copy, scale=rstd[:, i:i + 1]),
                       reads=[hk, f"rstd{i}"], writes=[tk])
                else:
                    op("dve", lambda e: e.tensor_scalar(out=ht[:], in0=hs[:], scalar1=rstd[:, i:i + 1], scalar2=None,
                                                        op0=ALU.mult), reads=[hk, f"rstd{i}"], writes=[tk])
                bk = i % 2
                pv = pb[bk][:].bitcast(BF16)
                for kc in range(8):
                    op("pe", lambda e, kc=kc: e.transpose(out=pv[:, kc * 128:(kc + 1) * 128],
                                                          in_=ht[:, kc * 128:(kc + 1) * 128], identity=identb[:]),
                       reads=[tk, "identb"], writes=[pbk[bk]], sig=(kc == 7))
                if seng == "act":
                    for kc in range(8):
                        op("act", lambda e: e.activation(out=hnT[:, kc, i * 128:(i + 1) * 128], in_=pv[:, kc * 128:(kc + 1) * 128],
                                                         func=AF.Copy, scale=gain[:, kc:kc + 1]),
                           reads=[pbk[bk], tagp], writes=["hnT"])
                else:
                    op("dve", lambda e: e.tensor_tensor(out=hnT[:, :, i * 128:(i + 1) * 128],
                                                        in0=pv.rearrange("p (k t) -> p k t", k=8),
                                                        in1=gain[:, :].unsqueeze(2).to_broadcast([128, 8, 128]),
                                                        op=ALU.mult),
                       reads=[pbk[bk], tagp], writes=["hnT"])

        def proj_fm(w3, wkey, col0, hnT, bank):
            for kc in range(8):
                op("pe", lambda e, kc=kc: e.matmul(pb[bank][:], lhsT=w3[:, kc, col0:col0 + 128], rhs=hnT[:, kc, :],
                                                   start=(kc == 0), stop=(kc == 7)),
                   reads=[wkey, "hnT"], writes=[pbk[bank]], sig=(kc == 7))

        for L in range(2):
            h_src = x_d if L == 0 else hmid_d
            h_dst = hmid_d if L == 0 else out_d

            with ExitStack() as lst:
                gmix = kb.sb(f"gmix{L}", [128, 8], F32, lst)
                gple = kb.sb(f"gple{L}", [128, 8], F32, lst)
                bglu = kb.sb(f"bglu{L}", [128, 8], F32, lst)
                dpad = kb.sb(f"dpad{L}", [128, 8], F32, lst)
                gq = kb.sb(f"gq{L}", [128, 1], F32, lst)
                gk = kb.sb(f"gk{L}", [128, 1], F32, lst)
                for t, d_, k_ in ((gmix, gmix_d, "gmix"), (gple, gple_d, "gple"), (bglu, bglu_d, "bglu"),
                                  (dpad, dpad_d, "dpad"), (gq, gq_d, "gq"), (gk, gk_d, "gk")):
                    kb.dma("sp", t[:], d_[L], "ld_small_" + k_, writes=[k_])
                op("dve", lambda e: e.tensor_scalar(out=gq[:], in0=gq[:], scalar1=0.125, scalar2=None, op0=ALU.mult),
                   reads=["gq"], writes=["gq"])

                with ExitStack() as s1:
                    T_tab = kb.sb("T_tab", [128, 8, 9, 128], BF16, s1)
                    WS_tab = kb.sb("WS_tab", [128, 8, 8, 128], BF16, s1)
                    WC_tab = kb.sb("WC_tab", [128, 32, 8, 32], BF16, s1)
                    AA1 = kb.sb("AA1", [128, 2, 32], F32, s1)
                    AA2 = kb.sb("AA2", [128, 2, 32], F32, s1)
                    Hh = kb.sb("Hh", [128, 2, 32, 65], F32, s1)
                    with ExitStack() as su:
                        def g32(name):
                            return kb.sb(name, [32, 64], F32, su)
                        are, aim, Lr, Li, mag, cc_, ss_, t1, t2, t3 = [g32(n) for n in
                                                                      ("are", "aim", "Lr", "Li", "mag", "cc_", "ss_", "t1", "t2", "t3")]
                        ldt = kb.sb("ldt", [32, 1], F32, su)
                        dtv = kb.sb("dtv", [32, 1], F32, su)
                        kb.dma("sp", are[:], are_d[L], "ld_are", writes=["are"])
                        kb.dma("sp", aim[:], aim_d[L], "ld_aim", writes=["aim"])
                        kb.dma("sp", ldt[:], ldt_d[L], "ld_ldt", writes=["ldt"])
                        op("act", lambda e: e.activation(out=dtv[:], in_=ldt[:], func=AF.Exp), reads=["ldt"], writes=["dtv"])

                        def dv(fn, reads, writes):
                            op("dve", fn, reads=reads, writes=writes)

                        def tt_(o, a, b, alu, ok, ak, bk_):
                            dv(lambda e: e.tensor_tensor(out=o, in0=a, in1=b, op=alu), [ak, bk_], [ok])

                        dv(lambda e: e.tensor_scalar(out=Lr[:], in0=are[:], scalar1=dtv[:, 0:1], scalar2=None, op0=ALU.mult),
                           ["are", "dtv"], ["Lr"])
                        dv(lambda e: e.tensor_scalar(out=Li[:], in0=aim[:], scalar1=dtv[:, 0:1], scalar2=None, op0=ALU.mult),
                           ["aim", "dtv"], ["Li"])
                        op("act", lambda e: e.activation(out=mag[:], in_=Lr[:], func=AF.Exp), reads=["Lr"], writes=["mag"])
                        op("act", lambda e: e.activation(out=ss_[:], in_=Li[:], func=AF.Sin, scale=1.0 / 32.0),
                           reads=["Li"], writes=["ss_"])
                        op("act", lambda e: e.activation(out=cc_[:], in_=Li[:], func=AF.Sin, scale=1.0 / 32.0,
                                                         bias=math.pi / 2), reads=["Li"], writes=["cc_"])
                        for _ in range(5):
                            tt_(t1[:], cc_[:], cc_[:], ALU.mult, "t1", "cc_", "cc_")
                            tt_(t2[:], ss_[:], ss_[:], ALU.mult, "t2", "ss_", "ss_")
                            tt_(t3[:], cc_[:], ss_[:], ALU.mult, "t3", "cc_", "ss_")
                            tt_(cc_[:], t1[:], t2[:], ALU.subtract, "cc_", "t1", "t2")
                            dv(lambda e: e.tensor_scalar(out=ss_[:], in0=t3[:], scalar1=2.0, scalar2=None, op0=ALU.mult),
                               ["t3"], ["ss_"])
                        Pre = kb.sb("Pre", [32, 9, 64], F32, su)
                        Pim = kb.sb("Pim", [32, 9, 64], F32, su)
                        tt_(Pre[:, 1, :], mag[:], cc_[:], ALU.mult, "P1r", "mag", "cc_")
                        tt_(Pim[:, 1, :], mag[:], ss_[:], ALU.mult, "P1i", "mag", "ss_")
                        tt_(t1[:], are[:], are[:], ALU.mult, "t1", "are", "are")
                        tt_(t2[:], aim[:], aim[:], ALU.mult, "t2", "aim", "aim")
                        tt_(t1[:], t1[:], t2[:], ALU.add, "t1", "t1", "t2")
                        dv(lambda e: e.reciprocal(out=t3[:], in_=t1[:]), ["t1"], ["t3"])
                        dv(lambda e: e.tensor_scalar(out=t1[:], in0=Pre[:, 1, :], scalar1=-1.0, scalar2=None, op0=ALU.add),
                           ["P1r"], ["t1"])
                        tt_(t2[:], t1[:], are[:], ALU.mult, "t2", "t1", "are")
                        tt_(mag[:], Pim[:, 1, :], aim[:], ALU.mult, "mag", "P1i", "aim")
                        tt_(t2[:], t2[:], mag[:], ALU.add, "t2", "t2", "mag")
                        tt_(Pre[:, 0, :], t2[:], t3[:], ALU.mult, "P0r", "t2", "t3")
                        tt_(t2[:], Pim[:, 1, :], are[:], ALU.mult, "t2", "P1i", "are")
                        tt_(mag[:], t1[:], aim[:], ALU.mult, "mag", "t1", "aim")
                        tt_(t2[:], t2[:], mag[:], ALU.subtract, "t2", "t2", "mag")
                        tt_(Pim[:, 0, :], t2[:], t3[:], ALU.mult, "P0i", "t2", "t3")
                        for k in range(1, 8):
                            a, b_ = f"P{k}r", f"P{k}i"
                            tt_(t1[:], Pre[:, k, :], Pre[:, 1, :], ALU.mult, "t1", a, "P1r")
                            tt_(t2[:], Pim[:, k, :], Pim[:, 1, :], ALU.mult, "t2", b_, "P1i")
                            tt_(Pre[:, k + 1, :], t1[:], t2[:], ALU.subtract, f"P{k + 1}r", "t1", "t2")
                            tt_(t1[:], Pre[:, k, :], Pim[:, 1, :], ALU.mult, "t1", a, "P1i")
                            tt_(t2[:], Pim[:, k, :], Pre[:, 1, :], ALU.mult, "t2", b_, "P1r")
                            tt_(Pim[:, k + 1, :], t1[:], t2[:], ALU.add, f"P{k + 1}i", "t1", "t2")
                        A12 = kb.sb("A12", [128, 9, 2, 32], F32, su)
                        cat = kb.sb("cat", [32, 2, 128], F32, su)
                        for k in range(9):
                            rk, ik = f"P{k}r", f"P{k}i"
                            dv(lambda e, k=k: e.tensor_copy(out=cat[:, 0, 0:64], in_=Pre[:, k, :]), [rk], ["cat"])
                            dv(lambda e, k=k: e.tensor_copy(out=cat[:, 0, 64:128], in_=Pre[:, k, :]), [rk], ["cat"])
                            dv(lambda e, k=k: e.tensor_scalar(out=cat[:, 1, 0:64], in0=Pim[:, k, :], scalar1=-1.0,
                                                              scalar2=None, op0=ALU.mult), [ik], ["cat"])
                            dv(lambda e, k=k: e.tensor_copy(out=cat[:, 1, 64:128], in_=Pim[:, k, :]), [ik], ["cat"])
                            for w_ in range(2):
                                op("pe", lambda e, w_=w_: e.transpose(out=pb[0][:, w_ * 32:(w_ + 1) * 32], in_=cat[:, w_, :],
                                                                      identity=identf[0:32, 0:32]),
                                   reads=["cat", "identf"], writes=["pb0"])
                            dv(lambda e, k=k: e.tensor_copy(out=A12[:, k, :, :],
                                                            in_=pb[0][:, 0:64].rearrange("p (a b) -> p a b", a=2)),
                               ["pb0"], [f"A12_{k}"])
                        for w_ in range(2):
                            dv(lambda e, w_=w_: e.tensor_copy(
                                out=AA1[:, w_, :].rearrange("p (a b) -> p a b", a=4),
                                in_=A12[:, 8, 0, :].rearrange("p (b a) -> p a b", a=4)), ["A12_8"], ["AA1"])
                        dv(lambda e: e.tensor_copy(out=AA2[:, 0, :].rearrange("p (a b) -> p a b", a=4),
                                                   in_=A12[:, 8, 1, :].rearrange("p (b a) -> p a b", a=4)), ["A12_8"], ["AA2"])
                        dv(lambda e: e.tensor_scalar(out=AA2[:, 1, :].rearrange("p (a b) -> p a b", a=4),
                                                     in0=A12[:, 8, 1, :].rearrange("p (b a) -> p a b", a=4),
                                                     scalar1=-1.0, scalar2=None, op0=ALU.mult), ["A12_8"], ["AA2"])
                        Braw = kb.sb("Braw", [128, 32, 16], F32, su)
                        Brsw = kb.sb("Brsw", [128, 32, 16], F32, su)
                        Bm = kb.sb("Bm", [128, 32, 16], F32, su)
                        Bsw = kb.sb("Bsw", [128, 32, 16], F32, su)
                        tmpA = kb.sb("tmpA", [128, 32, 16], F32, su)
                        tmpB = kb.sb("tmpB", [128, 32, 16], F32, su)
                        bre_v = bre_d[L].rearrange("g p h -> p g h")
                        bim_v = bim_d[L].rearrange("g p h -> p g h")
                        kb.dma("sp", Braw[0:64], bre_v, "ld_b0", writes=["Braw"])
                        kb.dma("sp", Braw[64:128], bim_v, "ld_b1", writes=["Braw"])
                        kb.dma("sp", Brsw[0:64], bim_v, "ld_b2", writes=["Brsw"])
                        kb.dma("sp", Brsw[64:128], bre_v, "ld_b3", writes=["Brsw"])

                        def bc(k, w_):
                            return A12[:, k, w_, :].unsqueeze(2).to_broadcast([128, 32, 16])

                        tt_(tmpA[:], Braw[:], bc(0, 0), ALU.mult, "tmpA", "Braw", "A12_0")
                        tt_(tmpB[:], Brsw[:], bc(0, 1), ALU.mult, "tmpB", "Brsw", "A12_0")
                        tt_(Bm[:], tmpA[:], tmpB[:], ALU.add, "Bm", "tmpA", "tmpB")
                        tt_(tmpA[:], Brsw[:], bc(0, 0), ALU.mult, "tmpA", "Brsw", "A12_0")
                        tt_(tmpB[:], Braw[:], bc(0, 1), ALU.mult, "tmpB", "Braw", "A12_0")
                        tt_(Bsw[:], tmpA[:], tmpB[:], ALU.subtract, "Bsw", "tmpA", "tmpB")
                        Cm = kb.sb("Cm", [128, 32, 16], F32, su)
                        Cmsw = kb.sb("Cmsw", [128, 32, 16], F32, su)
                        ccs = kb.sb("ccs", [128, 128], F32, su)
                        for (dst, dkey, srcd, neg_lo) in ((Cm, "Cm", ccat_d, False), (Cmsw, "Cmsw", ccsw_d, True)):
                            for c in range(4):
                                kb.dma("sp", ccs[:], srcd[L, c], "ld_ccs", writes=["ccs"])
                                op("pe", lambda e: e.transpose(out=pb[1][:, 0:128], in_=ccs[:], identity=identf[:]),
                                   reads=["ccs", "identf"], writes=["pb1"])
                                lo = pb[1][0:64, 0:128].rearrange("p (a b) -> p a b", a=8)
                                hi = pb[1][64:128, 0:128].rearrange("p (a b) -> p a b", a=8)
                                dlo = dst[0:64, 8 * c:8 * c + 8, :]
                                dhi = dst[64:128, 8 * c:8 * c + 8, :]
                                if neg_lo:
                                    dv(lambda e, lo=lo, dlo=dlo: e.tensor_scalar(out=dlo, in0=lo, scalar1=-1.0, scalar2=None,
                                                                                 op0=ALU.mult), ["pb1"], [dkey])
                                    dv(lambda e, hi=hi, dhi=dhi: e.tensor_copy(out=dhi, in_=hi), ["pb1"], [dkey])
                                else:
                                    dv(lambda e, lo=lo, dlo=dlo: e.tensor_copy(out=dlo, in_=lo), ["pb1"], [dkey])
                                    dv(lambda e, hi=hi, dhi=dhi: e.tensor_scalar(out=dhi, in0=hi, scalar1=-1.0, scalar2=None,
                                                                                 op0=ALU.mult), ["pb1"], [dkey])
                        T0f = kb.sb("T0f", [128, 8, 128], F32, su)
                        Cpad = kb.sb("Cpad", [128, 32, 32], BF16, su)
                        ABpad = kb.sb("ABpad", [128, 32, 32], BF16, su)
                        op("dve", lambda e: e.memset(Cpad[:], 0.0), writes=["Cpad"])
                        op("dve", lambda e: e.memset(ABpad[:], 0.0), writes=["ABpad"])
                        op("dve", lambda e: e.memset(WC_tab[:], 0.0), writes=["WC"])
                        dv(lambda e: e.tensor_copy(out=Cpad[:, :, 0:16], in_=Cm[:]), ["Cm"], ["Cpad"])
                        for tau in range(8):
                            if tau == 0:
                                dv(lambda e: e.tensor_copy(out=ABpad[:, :, 0:16], in_=Bm[:]), ["Bm"], ["ABpad"])
                            else:
                                tt_(tmpA[:], Bm[:], bc(tau, 0), ALU.mult, "tmpA", "Bm", f"A12_{tau}")
                                tt_(tmpB[:], Bsw[:], bc(tau, 1), ALU.mult, "tmpB", "Bsw", f"A12_{tau}")
                                tt_(ABpad[:, :, 0:16], tmpA[:], tmpB[:], ALU.add, "ABpad", "tmpA", "tmpB")
                            pv = pb[2][:].bitcast(BF16)
                            for pc in range(8):
                                op("pe", lambda e, pc=pc: e.transpose(
                                    out=pv[:, pc * 128:(pc + 1) * 128],
                                    in_=ABpad[:, 4 * pc:4 * pc + 4, :].rearrange("p a b -> p (a b)"), identity=identb[:]),
                                   reads=["ABpad", "identb"], writes=["pb2"])
                            dv(lambda e, tau=tau: e.tensor_copy(out=WS_tab[:, :, 7 - tau, :],
                                                                in_=pv.rearrange("p (a b) -> p a b", a=8)), ["pb2"], ["WS"])
                            for pc in range(8):
                                bk = 3 + pc // 4
                                op("pe", lambda e, pc=pc, bk=bk: e.matmul(
                                    pb[bk][:, (pc % 4) * 128:(pc % 4 + 1) * 128],
                                    lhsT=ABpad[:, 4 * pc:4 * pc + 4, :].rearrange("p a b -> p (a b)"),
                                    rhs=Cpad[:, 4 * pc:4 * pc + 4, :].rearrange("p a b -> p (a b)"), start=True, stop=True),
                                   reads=["ABpad", "Cpad"], writes=[pbk[bk]])
                            for hf in range(2):
                                if tau > 0:
                                    dv(lambda e, hf=hf, tau=tau: e.tensor_tensor(
                                        out=T_tab[:, 4 * hf:4 * hf + 4, tau, :],
                                        in0=pb[3 + hf][:].rearrange("p (a b) -> p a b", a=4),
                                        in1=bdmask[:].unsqueeze(1).to_broadcast([128, 4, 128]), op=ALU.mult),
                                       [pbk[3 + hf], "bdmask"], ["T"])
                                else:
                                    dv(lambda e, hf=hf: e.tensor_tensor(
                                        out=T0f[:, 4 * hf:4 * hf + 4, :],
                                        in0=pb[3 + hf][:].rearrange("p (a b) -> p a b", a=4),
                                        in1=bdmask[:].unsqueeze(1).to_broadcast([128, 4, 128]), op=ALU.mult),
                                       [pbk[3 + hf], "bdmask"], ["T0f"])
                            if tau == 0:
                                for pc in range(8):
                                    dv(lambda e, pc=pc: e.scalar_tensor_tensor(
                                        out=T0f[:, pc, :], in0=identf[:], scalar=dpad[:, pc:pc + 1], in1=T0f[:, pc, :],
                                        op0=ALU.mult, op1=ALU.add), ["identf", "dpad", "T0f"], ["T0f"])
                                dv(lambda e: e.tensor_copy(out=T_tab[:, :, 0, :], in_=T0f[:]), ["T0f"], ["T"])
                                dv(lambda e: e.tensor_tensor(out=T0f[:], in0=T0f[:], in1=T_tab[:, :, 0, :], op=ALU.subtract),
                                   ["T0f", "T"], ["T0f"])
                                dv(lambda e: e.tensor_copy(out=T_tab[:, :, 8, :], in_=T0f[:]), ["T0f"], ["T"])
                        for lp in range(8):
                            k = lp + 1
                            tt_(tmpA[:], Cm[:], bc(k, 0), ALU.mult, "tmpA", "Cm", f"A12_{k}")
                            tt_(tmpB[:], Cmsw[:], bc(k, 1), ALU.mult, "tmpB", "Cmsw", f"A12_{k}")
                            tt_(WC_tab[:, :, lp, 0:16], tmpA[:], tmpB[:], ALU.subtract, "WC", "tmpA", "tmpB")
                        op("dve", lambda e: e.memset(Hh[:, :, :, 0:1], 0.0), writes=["Hh"])
                        kb.barrier()
                    hsub = [kb.sb(f"hsub{i}", [128, D], F32, s1) for i in range(2)]
                    hnt = [kb.sb(f"hnt{i}", [128, D], BF16, s1) for i in range(2)]
                    ss = kb.sb("ssq", [128, 4], F32, s1)
                    lnv = kb.sb("lnv", [128, 4], F32, s1)
                    rstd = kb.sb("rstd", [128, 4], F32, s1)
                    sqj = kb.sb("sqj", [128, D], BF16, s1)
                    hnT = kb.sb("hnT", [128, 8, TT], BF16, s1)
                    nb = (hsub, hnt, ss, lnv, rstd, sqj, hnT)
                    uTp2 = [kb.sb(f"uTp{i}", [128, 8, TT], BF16, s1) for i in range(2)]
                    sgs2 = [kb.sb(f"sgs{i}", [128, 4, TT], BF16, s1) for i in range(2)]
                    Hprev2 = [kb.sb(f"Hprev{i}", [128, 32, 64], BF16, s1) for i in range(2)]
                    zTp = kb.sb("zTp", [128, 8, TT], BF16, s1)
                    ysT = kb.sb("ysT", [128, 4, TT], BF16, s1)
                    SS = kb.sb("SS", [128, 2, 32, 64], F32, s1)
                    rt1 = kb.sb("rt1", [128, 2, 32], F32, s1)
                    rt2 = kb.sb("rt2", [128, 2, 32], F32, s1)
                    Ysb = [kb.sb(f"Ysb{i}", [64, 8, 4, 32], BF16, s1) for i in range(2)]
                    sgt = [kb.sb(f"sgt{i}", [128, TT], F32, s1) for i in range(2)]
                    vat = [kb.sb(f"vat{i}", [128, TT], F32, s1) for i in range(2)]
                    hh_ap = Hh[:]
                    pstep = list(hh_ap.ap[0])

                    def stage_X1(j, pf):
                        uTp, sgs, Hprev = uTp2[j % 2], sgs2[j % 2], Hprev2[j % 2]
                        uk, sk, hk_ = f"uTp{j % 2}", f"sgs{j % 2}", f"Hprev{j % 2}"
                        norm_phase(nb, lambda i: h_src[j * TT + i * 128:j * TT + (i + 1) * 128, :], gmix, "gmix", "act")
                        for half in range(2):
                            w3, wk = pf.get(half)
                            for q4 in range(4):
                                pc = half * 4 + q4
                                bank = 2 + pc % 2
                                proj_fm(w3, wk, q4 * 128, hnT, bank)
                                op("act", lambda e: e.activation(out=uTp[:, pc, :].rearrange("p (l s) -> p l s", l=8),
                                                                 in_=pb[bank][:].rearrange("p (s l) -> p l s", l=8), func=AF.Copy),
                                   reads=[pbk[bank]], writes=[uk])
                        w3, wk = pf.get(2)
                        for c in range(4):
                            bank = 2 + c % 2
                            proj_fm(w3, wk, c * 128, hnT, bank)
                            op("act", lambda e: e.activation(out=sgs[:, c, :].rearrange("p (l s) -> p l s", l=8),
                                                             in_=pb[bank][:].rearrange("p (s l) -> p l s", l=8), func=AF.Silu),
                               reads=[pbk[bank]], writes=[sk])
                        for pc in range(8):
                            for l in range(8):
                                for gg in range(4):
                                    op("pe", lambda e: e.matmul(
                                        pb[4 + gg][:, pc * 64:(pc + 1) * 64],
                                        lhsT=WS_tab[32 * gg:32 * gg + 32, pc, l, :],
                                        rhs=uTp[32 * gg:32 * gg + 32, pc, l * 64:(l + 1) * 64],
                                        start=(l == 0), stop=(l == 7), tile_position=(32 * gg, 0)),
                                       reads=["WS", uk], writes=[pbk[4 + gg]], sig=(l == 7))

                    def stage_X2(j):
                        Hprev = Hprev2[j % 2]
                        hk_ = f"Hprev{j % 2}"
                        for gg in range(4):
                            op("dve", lambda e: e.tensor_copy(out=SS[:, 0, 8 * gg:8 * gg + 8, :].rearrange("p a b -> p (a b)"),
                                                              in_=pb[4 + gg][:]), reads=[pbk[4 + gg]], writes=["SS0"])
                        op("act", lambda e: e.activation(out=SS[0:64, 1, :, :], in_=SS[64:128, 0, :, :], func=AF.Copy),
                           reads=["SS0"], writes=["SS1"])
                        op("act", lambda e: e.activation(out=SS[64:128, 1, :, :], in_=SS[0:64, 0, :, :], func=AF.Copy),
                           reads=["SS0"], writes=["SS1"])
                        for sc in range(64):
                            cur = Hh[:, :, :, sc]
                            swp = bass.AP(hh_ap.tensor, hh_ap.offset + 32 * 65 + sc, [pstep, [-32 * 65, 2], [65, 32]])
                            op("dve", lambda e: e.tensor_tensor(out=rt1[:], in0=cur, in1=AA1[:], op=ALU.mult),
                               reads=["Hh", "AA1"], writes=["rt1"])
                            op("dve", lambda e: e.tensor_tensor(out=rt2[:], in0=swp, in1=AA2[:], op=ALU.mult),
                               reads=["Hh", "AA2"], writes=["rt2"])
                            op("dve", lambda e: e.tensor_tensor(out=rt1[:], in0=rt1[:], in1=rt2[:], op=ALU.add),
                               reads=["rt1", "rt2"], writes=["rt1"])
                            op("dve", lambda e: e.tensor_tensor(out=Hh[:, :, :, sc + 1], in0=rt1[:], in1=SS[:, :, :, sc],
                                                                op=ALU.add), reads=["rt1", "SS0", "SS1"], writes=["Hh"])
                        op("dve", lambda e: e.tensor_copy(out=Hprev[:], in_=Hh[:, 0, :, 0:64]), reads=["Hh"], writes=[hk_])
                        op("dve", lambda e: e.tensor_copy(out=Hh[:, :, :, 0:1], in_=Hh[:, :, :, 64:65]), reads=["Hh"], writes=["Hh"])

                    def stage_Y(j, pf, i0):
                        uTp, sgs, Hprev = uTp2[j % 2], sgs2[j % 2], Hprev2[j % 2]
                        uk, sk, hk_ = f"uTp{j % 2}", f"sgs{j % 2}", f"Hprev{j % 2}"

                        def yint(pc):
                            yb, ykey = Ysb[pc % 2], f"Ysb{pc % 2}"
                            for gg in range(4):
                                gi, gn, bank = gg * 8 + pc, 4 * pc + gg, gg // 2
                                op("pe", lambda e: e.matmul(
                                    pb[bank][0:64, (gg % 2) * 256:(gg % 2 + 1) * 256],
                                    lhsT=Hprev[:, gi, :], rhs=WC_tab[:, gn, :, :].rearrange("p a b -> p (a b)"),
                                    start=True, stop=True), reads=[hk_, "WC"], writes=[pbk[bank]])
                            for bank in range(2):
                                op("act", lambda e: e.activation(
                                    out=yb[:, :, 2 * bank:2 * bank + 2, :],
                                    in_=pb[bank][0:64, :].rearrange("p (g l h) -> p l g h", g=2, l=8),
                                    func=AF.Copy), reads=[pbk[bank]], writes=[ykey])

                        yint(0)
                        for pc in range(8):
                            if pc + 1 < 8:
                                yint(pc + 1)
                            yb, ykey = Ysb[pc % 2], f"Ysb{pc % 2}"
                            ybank = 2 + pc % 2
                            op("pe", lambda e: e.matmul(pb[ybank][:], lhsT=T_tab[:, pc, 0, :], rhs=uTp[:, pc, :], start=True, stop=False),
                               reads=["T", uk], writes=[pbk[ybank]], sig=False)
                            op("pe", lambda e: e.matmul(pb[ybank][:], lhsT=T_tab[:, pc, 8, :], rhs=uTp[:, pc, :], start=False, stop=False),
                               reads=["T", uk], writes=[pbk[ybank]], sig=False)
                            for tau in range(1, 8):
                                op("pe", lambda e: e.matmul(pb[ybank][:, tau * 64:TT], lhsT=T_tab[:, pc, tau, :],
                                                            rhs=uTp[:, pc, 0:(8 - tau) * 64], start=False, stop=False),
                                   reads=["T", uk], writes=[pbk[ybank]], sig=False)
                            for lp in range(8):
                                op("pe", lambda e: e.matmul(
                                    pb[ybank][:, lp * 64:(lp + 1) * 64], lhsT=yb[:, lp, :, :].rearrange("p g h -> p (g h)"),
                                    rhs=identb[0:64, 0:64], start=False, stop=(lp == 7)),
                                   reads=[ykey, "identb"], writes=[pbk[ybank]], sig=(lp == 7))
                            op("act", lambda e: e.activation(out=zTp[:, pc, :], in_=pb[ybank][:], func=AF.Gelu_apprx_tanh),
                               reads=[pbk[ybank]], writes=["zTp"])
                        wv3, wvk = pf.get(i0, ahead=1)
                        wg3, wgk = pf.get(i0 + 1, ahead=0)
                        for c in range(4):
                            bv_, bg_ = 4 + 2 * (c % 2), 5 + 2 * (c % 2)
                            for (w3, wk, bank) in ((wv3, wvk, bv_), (wg3, wgk, bg_)):
                                for kc in range(8):
                                    op("pe", lambda e: e.matmul(
                                        pb[bank][:], lhsT=w3[:, kc, c * 128:(c + 1) * 128], rhs=zTp[:, kc, :],
                                        start=(kc == 0), stop=(kc == 7)), reads=[wk, "zTp"], writes=[pbk[bank]], sig=(kc == 7))
                            sg_, va_ = sgt[c % 2], vat[c % 2]
                            op("act", lambda e: e.activation(out=sg_[:], in_=pb[bg_][:], func=AF.Sigmoid, bias=bglu[:, 4 + c:5 + c]),
                               reads=[pbk[bg_], "bglu"], writes=[f"sgt{c % 2}"])
                            op("act", lambda e: e.activation(out=va_[:], in_=pb[bv_][:], func=AF.Identity, bias=bglu[:, c:c + 1]),
                               reads=[pbk[bv_], "bglu"], writes=[f"vat{c % 2}"])
                            op("pool", lambda e: e.tensor_tensor(out=va_[:], in0=va_[:], in1=sg_[:], op=ALU.mult),
                               reads=[f"vat{c % 2}", f"sgt{c % 2}"], writes=[f"vat{c % 2}"])
                            op("pool", lambda e: e.tensor_tensor(out=ysT[:, c, :].rearrange("p (s l) -> p l s", l=8),
                                                                 in0=va_[:].rearrange("p (l s) -> p l s", l=8),
                                                                 in1=sgs[:, c, :].rearrange("p (l s) -> p l s", l=8), op=ALU.mult),
                               reads=[f"vat{c % 2}", sk], writes=["ysT"])
                        kb.dma("sp", yss_d[L, :, :, j * TT:(j + 1) * TT].rearrange("c p t -> p c t"), ysT[:], "st_yss",
                               reads=["ysT"])

                    stage_X1(0, Pref(L, [0, 1, 2], "act"))
                    for j in range(NT + 1):
                        pids = ([7, 8] if j >= 1 else []) + ([0, 1, 2] if j + 1 < NT else [])
                        pf = Pref(L, pids, "act")
                        if j < NT:
                            stage_X2(j)
                        if j >= 1:
                            stage_Y(j - 1, pf, 0)
                        if j + 1 < NT:
                            stage_X1(j + 1, _Shift(pf, 2 if j >= 1 else 0))
                    kb.barrier()
                if stop_at == f"s1_{L}":
                    dd = nc.dram_tensor("dbg_yss", [4, 128, S], BF16, kind="ExternalOutput").ap()
                    kb.dma("sp", dd, yss_d[L], "st_dbgyss")
                    break

                with ExitStack() as s2:
                    kT = kb.sb("kT", [128, 4, S], BF16, s2)
                    Vc = kb.sb("Vc", [128, S // 128, 512], BF16, s2)
                    htile = kb.sb("htile", [128, NSUB, D], F32, s2)
                    hsub = [htile[:, 0, :], htile[:, 1, :]]
                    hnt = [kb.sb(f"hnt{i}", [128, D], BF16, s2) for i in range(2)]
                    ss = kb.sb("ssq", [128, 4], F32, s2)
                    lnv = kb.sb("lnv", [128, 4], F32, s2)
                    rstd = kb.sb("rstd", [128, 4], F32, s2)
                    sqj = kb.sb("sqj", [128, D], BF16, s2)
                    hnT = kb.sb("hnT", [128, 8, TT], BF16, s2)
                    qT = kb.sb("qT", [128, 4, TT], BF16, s2)
                    sga = kb.sb("sga", [128, 4, TT], BF16, s2)
                    yT = kb.sb("yT", [128, 8, TT], BF16, s2)
                    sq = [kb.sb(f"sq{i}", [128, TT], BF16, s2) for i in range(2)]
                    sd = [kb.sb(f"sd{i}", [128, TT], F32, s2) for i in range(2)]
                    e_sb = [kb.sb(f"e_sb{i}", [128, TT], BF16, s2) for i in range(6)]
                    sp_sb = [kb.sb(f"sp_sb{i}", [128, TT], BF16, s2) for i in range(3)]
                    x_sb = [kb.sb(f"x_sb{i}", [128, TT], BF16, s2) for i in range(2)]
                    w_sb = [kb.sb(f"w_sb{i}", [128, TT], BF16, s2) for i in range(3)]
                    S_sb = [kb.sb(f"S_sb{i}", [128, TT], BF16, s2) for i in range(3)]
                    psub = [kb.sb(f"psub{i}", [128, 256], F32, s2) for i in range(2)]
                    pbf = [kb.sb(f"pbf{i}", [128, 256], BF16, s2) for i in range(2)]
                    pT = kb.sb("pT", [128, 2, TT], BF16, s2)
                    gsb = [kb.sb(f"gsb{i}", [128, TT], F32, s2) for i in range(2)]
                    gpp = [kb.sb(f"gpp{i}", [128, TT], F32, s2) for i in range(2)]

                    class _Sub:
                        pass

                    tctr = [0]
                    HT = ["htile", "hsub0", "hsub1"]
                    for j in range(NT):
                        pf = Pref(L, [3, 4, 5, 6, 9, 10, 11, 13, 12], "dve")
                        nb = (hsub, hnt, ss, lnv, rstd, sqj, hnT)
                        norm_phase(nb, lambda i: h_src[j * TT + i * 128:j * TT + (i + 1) * 128, :], gmix, "gmix", "dve")
                        for which in range(2):
                            w3, wk = pf.get(which)
                            gvec = gq if which == 0 else gk
                            gkey = "gq" if which == 0 else "gk"
                            for c in range(4):
                                bank = 2 + c % 2
                                proj_fm(w3, wk, c * 128, hnT, bank)
                                sq_, sd_ = sq[c % 2], sd[c % 2]
                                op("act", lambda e, sq_=sq_, bank=bank: e.activation(out=sq_[:], in_=pb[bank][:], func=AF.Square),
                                   reads=[pbk[bank]], writes=[f"sq{c % 2}"])
                                sbank = 4 + c % 2
                                op("pe", lambda e, sq_=sq_, sbank=sbank: e.matmul(pb[sbank][:], lhsT=blockones[:], rhs=sq_[:],
                                                                                   start=True, stop=True),
                                   reads=[f"sq{c % 2}", "blockones"], writes=[pbk[sbank]])
                                op("act", lambda e, sd_=sd_, sbank=sbank: e.activation(out=sd_[:], in_=pb[sbank][:], func=AF.Ln,
                                                                                      scale=1.0 / 64.0, bias=EPS),
                                   reads=[pbk[sbank]], writes=[f"sd{c % 2}"])
                                op("act", lambda e, sd_=sd_: e.activation(out=sd_[:], in_=sd_[:], func=AF.Exp, scale=-0.5),
                                   reads=[f"sd{c % 2}"], writes=[f"sd{c % 2}"])
                                dst = qT[:, c, :] if which == 0 else kT[:, c, j * TT:(j + 1) * TT]
                                op("dve", lambda e, dst=dst, bank=bank, sd_=sd_, gvec=gvec: e.scalar_tensor_tensor(
                                    out=dst, in0=pb[bank][:], scalar=gvec[:, 0:1], in1=sd_[:], op0=ALU.mult, op1=ALU.mult),
                                   reads=[pbk[bank], gkey, f"sd{c % 2}"], writes=["qT" if which == 0 else "kT"])
                        w3, wk = pf.get(2)
                        for i in range(NSUB):
                            bank = 2 + i % 2
                            for kc in range(8):
                                op("pe", lambda e, kc=kc, i=i, bank=bank: e.matmul(
                                    pb[bank][:], lhsT=hnT[:, kc, i * 128:(i + 1) * 128], rhs=w3[:, kc, :],
                                    start=(kc == 0), stop=(kc == 7)), reads=[wk, "hnT"], writes=[pbk[bank]], sig=(kc == 7))
                            op("dve", lambda e, i=i, bank=bank: e.tensor_copy(out=Vc[:, 4 * j + i, :], in_=pb[bank][:]),
                               reads=[pbk[bank]], writes=["Vc"])
                        w3, wk = pf.get(3)
                        for c in range(4):
                            bank = 2 + c % 2
                            proj_fm(w3, wk, c * 128, hnT, bank)
                            op("act", lambda e, c=c, bank=bank: e.activation(out=sga[:, c, :], in_=pb[bank][:], func=AF.Silu),
                               reads=[pbk[bank]], writes=["sga"])
                        tiles = []
                        for h in range(8):
                            nblk = 4 * j + 4
                            for kb_ in range(nblk - 1, -1, -1):
                                diag = kb_ >= 4 * j
                                tiles.append(dict(h=h, c=h // 2, base=64 * (h % 2), obank=6 + h % 2, kb=kb_, diag=diag,
                                                  qoff=(kb_ - 4 * j) * 128 if diag else 0,
                                                  first=(kb_ == nblk - 1), last=(kb_ == 0)))
                        s_idx = 0
                        for t_ in tiles:
                            if t_["first"]:
                                t_["s_in"], t_["s_off"] = None, None
                            else:
                                t_["s_in"] = s_idx
                                t_["s_off"] = (t_["qoff"] + 128) if t_["diag"] else 0
                            if not t_["last"]:
                                s_idx = (s_idx + 1) % 3
                                t_["s_out"] = s_idx
                            else:
                                t_["s_out"] = None
                        NTL = len(tiles)

                        def st_QK(i):
                            t_ = tiles[i]
                            g_ = tctr[0] + i
                            zb = g_ % 3
                            qo, c, base, kb_ = t_["qoff"], t_["c"], t_["base"], t_["kb"]
                            op("pe", lambda e: e.matmul(pb[zb][:, qo:TT], lhsT=kT[base:base + 64, c, kb_ * 128:(kb_ + 1) * 128],
                                                        rhs=qT[base:base + 64, c, qo:TT], start=True, stop=True),
                               reads=["kT", "qT"], writes=[pbk[zb]])

                        def st_A1(i):
                            t_ = tiles[i]
                            g_ = tctr[0] + i
                            zb, e_, ek = g_ % 3, e_sb[g_ % 6], f"e_sb{g_ % 6}"
                            qo = t_["qoff"]
                            op("act", lambda e: e.activation(out=e_[:, qo:TT], in_=pb[zb][:, qo:TT], func=AF.Exp),
                               reads=[pbk[zb]], writes=[ek])
                            if t_["diag"]:
                                op("pool", lambda e: e.tensor_tensor(out=e_[:, qo:qo + 128], in0=e_[:, qo:qo + 128],
                                                                     in1=masklt[:], op=ALU.mult), reads=[ek, "masklt"], writes=[ek])

                        def st_A2(i):
                            t_ = tiles[i]
                            g_ = tctr[0] + i
                            e_, ek = e_sb[g_ % 6], f"e_sb{g_ % 6}"
                            sp_, spk = sp_sb[g_ % 3], f"sp_sb{g_ % 3}"
                            qo = t_["qoff"]
                            op("act", lambda e: e.activation(out=sp_[:, qo:TT], in_=e_[:, qo:TT], func=AF.Ln, bias=1.0),
                               reads=[ek], writes=[spk])
                            if t_["s_out"] is not None:
                                sn_t, sn_k = S_sb[t_["s_out"]], f"S_sb{t_['s_out']}"
                                if t_["s_in"] is None:
                                    op("dve", lambda e: e.tensor_copy(out=sn_t[:, qo:TT], in_=sp_[:, qo:TT]), reads=[spk], writes=[sn_k])
                                else:
                                    sc_t, sc_k = S_sb[t_["s_in"]], f"S_sb{t_['s_in']}"
                                    a0 = 0
                                    if t_["diag"]:
                                        op("dve", lambda e: e.tensor_copy(out=sn_t[:, qo:qo + 128], in_=sp_[:, qo:qo + 128]),
                                           reads=[spk], writes=[sn_k])
                                        a0 = qo + 128
                                    op("dve", lambda e: e.tensor_tensor(out=sn_t[:, a0:TT], in0=sc_t[:, a0:TT], in1=sp_[:, a0:TT],
                                                                        op=ALU.add), reads=[sc_k, spk], writes=[sn_k])

                        def st_B(i):
                            t_ = tiles[i]
                            g_ = tctr[0] + i
                            cb = 3 + g_ % 2
                            sp_, spk = sp_sb[g_ % 3], f"sp_sb{g_ % 3}"
                            qo = t_["qoff"]
                            has_s = t_["s_in"] is not None and t_["s_off"] < TT
                            op("pe", lambda e: e.matmul(pb[cb][:, qo:TT], lhsT=tri[:], rhs=sp_[:, qo:TT], start=True, stop=not has_s),
                               reads=["tri", spk], writes=[pbk[cb]])
                            if has_s:
                                sc_t, sc_k, so = S_sb[t_["s_in"]], f"S_sb{t_['s_in']}", t_["s_off"]
                                op("pe", lambda e: e.matmul(pb[cb][:, so:TT], lhsT=ones[:], rhs=sc_t[:, so:TT], start=False, stop=True),
                                   reads=["ones", sc_k], writes=[pbk[cb]])

                        def st_C1(i):
                            t_ = tiles[i]
                            g_ = tctr[0] + i
                            cb = 3 + g_ % 2
                            e_, ek = e_sb[g_ % 6], f"e_sb{g_ % 6}"
                            x_, xk = x_sb[g_ % 2], f"x_sb{g_ % 2}"
                            w_, wk_ = w_sb[g_ % 3], f"w_sb{g_ % 3}"
                            qo = t_["qoff"]
                            op("act", lambda e: e.activation(out=x_[:, qo:TT], in_=pb[cb][:, qo:TT], func=AF.Exp, scale=-1.0),
                               reads=[pbk[cb]], writes=[xk])
                            op("dve", lambda e: e.tensor_tensor(out=w_[:, qo:TT], in0=e_[:, qo:TT], in1=x_[:, qo:TT], op=ALU.mult),
                               reads=[ek, xk], writes=[wk_])

                        def st_C2(i):
                            t_ = tiles[i]
                            g_ = tctr[0] + i
                            w_, wk_ = w_sb[g_ % 3], f"w_sb{g_ % 3}"
                            qo, h, c, base, ob = t_["qoff"], t_["h"], t_["c"], t_["base"], t_["obank"]
                            op("pe", lambda e: e.matmul(pb[ob][base:base + 64, qo:TT], lhsT=Vc[:, t_["kb"], h * 64:(h + 1) * 64],
                                                        rhs=w_[:, qo:TT], start=t_["first"], stop=t_["last"], skip_group_check=True),
                               reads=["Vc", wk_], writes=[pbk[ob]])
                            if t_["last"]:
                                op("dve", lambda e: e.tensor_tensor(out=yT[base:base + 64, 4 + c, :], in0=pb[ob][base:base + 64, :],
                                                                    in1=sga[base:base + 64, c, :], op=ALU.mult),
                                   reads=[pbk[ob], "sga"], writes=["yT"])

                        for i in range(-1, NTL + 4):
                            if 0 <= i + 1 < NTL:
                                st_QK(i + 1)
                            if 0 <= i < NTL:
                                st_A1(i)
                            if 0 <= i - 1 < NTL:
                                st_A2(i - 1)
                            if 0 <= i - 2 < NTL:
                                st_B(i - 2)
                            if 0 <= i - 4 < NTL:
                                st_C2(i - 4)
                            if 0 <= i - 3 < NTL:
                                st_C1(i - 3)
                        tctr[0] += NTL
                        kb.dma("sp", yT[:, 0:4, :], yss_d[L, :, :, j * TT:(j + 1) * TT].rearrange("c p t -> p c t"), "ld_yss",
                               writes=["yT"])
                        kb.dma("sp", htile[:], h_src[j * TT:(j + 1) * TT, :].rearrange("(i p) d -> p i d", p=128), "ld_htile", writes=HT)
                        for half in range(2):
                            w3, wk = pf.get(4 + half)
                            for i in range(NSUB):
                                bank = 4 + i % 2
                                for kc in range(8):
                                    op("pe", lambda e, kc=kc, i=i, bank=bank, w3=w3: e.matmul(
                                        pb[bank][:], lhsT=yT[:, kc, i * 128:(i + 1) * 128], rhs=w3[:, kc, :],
                                        start=(kc == 0), stop=(kc == 7)), reads=[wk, "yT"], writes=[pbk[bank]], sig=(kc == 7))
                                op("dve", lambda e, i=i, bank=bank, half=half: e.tensor_tensor(
                                    out=htile[:, i, half * 512:(half + 1) * 512], in0=htile[:, i, half * 512:(half + 1) * 512],
                                    in1=pb[bank][:], op=ALU.add), reads=[pbk[bank]] + HT, writes=HT)
                        if dbg and j == 0 and L == 0:
                            kb.dma("sp", dbg_out("dbg_h1", [128, NSUB, D]), htile[:], "st_dbg1", reads=HT)
                        for i in range(NSUB):
                            hs = htile[:, i, :]
                            op("act", lambda e, hs=hs, i=i: e.activation(out=sqj[:], in_=hs, func=AF.Square, accum_out=ss[:, i:i + 1]),
                               reads=HT, writes=["sqj", f"ss{i}"])
                            op("act", lambda e, i=i: e.activation(out=lnv[:, i:i + 1], in_=ss[:, i:i + 1], func=AF.Ln,
                                                                  scale=1.0 / D, bias=EPS), reads=[f"ss{i}"], writes=[f"lnv{i}"])
                            op("act", lambda e, i=i: e.activation(out=rstd[:, i:i + 1], in_=lnv[:, i:i + 1], func=AF.Exp, scale=-0.5),
                               reads=[f"lnv{i}"], writes=[f"rstd{i}"])
                            ht = hnt[i % 2]
                            tk = f"hnt{i % 2}"
                            op("dve", lambda e, hs=hs, ht=ht, i=i: e.tensor_scalar(out=ht[:], in0=hs, scalar1=rstd[:, i:i + 1],
                                                                                    scalar2=None, op0=ALU.mult),
                               reads=HT + [f"rstd{i}"], writes=[tk])
                            bk = i % 2
                            pv = pb[bk][:].bitcast(BF16)
                            for kc in range(8):
                                op("pe", lambda e, kc=kc, ht=ht, pv=pv: e.transpose(out=pv[:, kc * 128:(kc + 1) * 128],
                                                                                   in_=ht[:, kc * 128:(kc + 1) * 128], identity=identb[:]),
                                   reads=[tk, "identb"], writes=[pbk[bk]], sig=(kc == 7))
                            op("dve", lambda e, i=i, pv=pv: e.tensor_tensor(out=hnT[:, :, i * 128:(i + 1) * 128],
                                                                            in0=pv.rearrange("p (k t) -> p k t", k=8),
                                                                            in1=gple[:, :].unsqueeze(2).to_broadcast([128, 8, 128]),
                                                                            op=ALU.mult), reads=[pbk[bk], "gple"], writes=["hnT"])
                            ps_, pb_ = psub[i % 2], pbf[i % 2]
                            kb.dma("sp", ps_[:], p_d[L, j * TT + i * 128:j * TT + (i + 1) * 128, :], f"ld_psub{i % 2}",
                                   writes=[f"psub{i % 2}"])
                            op("dve", lambda e, ps_=ps_, pb_=pb_: e.tensor_copy(out=pb_[:], in_=ps_[:]),
                               reads=[f"psub{i % 2}"], writes=[f"pbf{i % 2}"])
                            pv2 = pb[2 + i % 2][:].bitcast(BF16)
                            for k2 in range(2):
                                op("pe", lambda e, k2=k2, pb_=pb_, pv2=pv2: e.transpose(out=pv2[:, k2 * 128:(k2 + 1) * 128],
                                                                                       in_=pb_[:, k2 * 128:(k2 + 1) * 128],
                                                                                       identity=identb[:]),
                                   reads=[f"pbf{i % 2}", "identb"], writes=[pbk[2 + i % 2]], sig=(k2 == 1))
                            op("dve", lambda e, i=i, pv2=pv2: e.tensor_copy(out=pT[:, :, i * 128:(i + 1) * 128],
                                                                            in_=pv2[:, 0:256].rearrange("p (k t) -> p k t", k=2)),
                               reads=[pbk[2 + i % 2]], writes=["pT"])
                        wpp3, wppk = None, None
                        for half in range(2):
                            w3, wk = pf.get(6 if half == 0 else 8)
                            if wpp3 is None:
                                wpp3, wppk = pf.get(7)
                            for i in range(NSUB):
                                gb, pbk_ = 4 + i % 2, 6 + i % 2
                                for kc in range(8):
                                    op("pe", lambda e, kc=kc, i=i, gb=gb, w3=w3: e.matmul(
                                        pb[gb][:], lhsT=hnT[:, kc, i * 128:(i + 1) * 128], rhs=w3[:, kc, :],
                                        start=(kc == 0), stop=(kc == 7)), reads=[wk, "hnT"], writes=[pbk[gb]], sig=(kc == 7))
                                for k2 in range(2):
                                    op("pe", lambda e, k2=k2, i=i, pbk_=pbk_, half=half: e.matmul(
                                        pb[pbk_][:], lhsT=pT[:, k2, i * 128:(i + 1) * 128],
                                        rhs=wpp3[:, k2, half * 512:(half + 1) * 512], start=(k2 == 0), stop=(k2 == 1)),
                                       reads=[wppk, "pT"], writes=[pbk[pbk_]], sig=(k2 == 1))
                                g_, gp_ = gsb[i % 2], gpp[i % 2]
                                op("act", lambda e, g_=g_, gb=gb: e.activation(out=g_[:], in_=pb[gb][:], func=AF.Sigmoid),
                                   reads=[pbk[gb]], writes=[f"gsb{i % 2}"])
                                op("dve", lambda e, g_=g_, gp_=gp_, pbk_=pbk_: e.tensor_tensor(out=gp_[:], in0=g_[:], in1=pb[pbk_][:],
                                                                                               op=ALU.mult),
                                   reads=[f"gsb{i % 2}", pbk[pbk_]], writes=[f"gpp{i % 2}"])
                                op("dve", lambda e, i=i, half=half, gp_=gp_: e.tensor_tensor(
                                    out=htile[:, i, half * 512:(half + 1) * 512], in0=htile[:, i, half * 512:(half + 1) * 512],
                                    in1=gp_[:], op=ALU.add), reads=[f"gpp{i % 2}"] + HT, writes=HT)
                        kb.dma("sp", h_dst[j * TT:(j + 1) * TT, :].rearrange("(i p) d -> p i d", p=128), htile[:], "st_h", reads=HT)
                    kb.barrier()
                if stop_at == f"s2_{L}":
                    dd = nc.dram_tensor("dbg_h", [S, D], F32, kind="ExternalOutput").ap()
                    kb.dma("sp", dd, h_dst, "st_dbgh")
                    break
        kb.finish("sp")
        build_program.stats = (kb.ninst, kb.nwaits)
    return nc


def _prep_shared(inp):
    f = np.float32
    w_in = np.asarray(inp["w_in"], f)
    u_cols = w_in[:, :, 0:512].reshape(2, D, 32, 16)
    u_pad = np.zeros((2, D, 32, 32), f)
    u_pad[:, :, :, 0:16] = u_cols
    w_in_p = np.concatenate([u_pad.reshape(2, D, 1024), w_in[:, :, 512:]], axis=2)
    w_glu = np.asarray(inp["ssm_w_glu"], f).reshape(2, 32, 16, 1024)
    w_glu_p = np.zeros((2, 32, 32, 1024), f)
    w_glu_p[:, :, 0:16, :] = w_glu
    w_glu_p = w_glu_p.reshape(2, 1024, 1024)

    def colT(v):
        return np.ascontiguousarray(np.asarray(v, f).reshape(2, 8, 128).transpose(0, 2, 1))

    d = np.asarray(inp["ssm_d"], f)
    d_pad = np.zeros((2, 32, 32), f)
    d_pad[:, :, 0:16] = d
    dpad = np.ascontiguousarray(d_pad.reshape(2, 8, 128).transpose(0, 2, 1))
    gq = np.tile(np.asarray(inp["q_norm_g"], f), (1, 2)).reshape(2, 128, 1)
    gk = np.tile(np.asarray(inp["k_norm_g"], f), (1, 2)).reshape(2, 128, 1)
    cre = np.asarray(inp["ssm_c_re"], f).reshape(2, 4, 128, 64)
    cim = np.asarray(inp["ssm_c_im"], f).reshape(2, 4, 128, 64)
    return {
        "w_in": np.ascontiguousarray(w_in_p), "w_glu": w_glu_p,
        "w_out": np.asarray(inp["w_out"], f), "w_pg": np.asarray(inp["w_ple_gate"], f),
        "w_pp": np.asarray(inp["w_ple_proj"], f),
        "gmix": colT(inp["mix_norm_g"]), "gple": colT(inp["ple_norm_g"]), "bglu": colT(inp["ssm_b_glu"]),
        "dpad": dpad, "gq": np.ascontiguousarray(gq), "gk": np.ascontiguousarray(gk),
        "a_re": np.asarray(inp["ssm_a_re"], f), "a_im": np.asarray(inp["ssm_a_im"], f),
        "logdt": np.asarray(inp["ssm_log_dt"], f).reshape(2, 32, 1),
        "b_re": np.asarray(inp["ssm_b_re"], f), "b_im": np.asarray(inp["ssm_b_im"], f),
        "ccat": np.ascontiguousarray(np.concatenate([cre, cim], axis=3)),
        "ccatsw": np.ascontiguousarray(np.concatenate([cim, cre], axis=3)),
    }


def kernel(**inputs):
    shared = _prep_shared(inputs)
    x = np.asarray(inputs["x"], np.float32)
    p = np.asarray(inputs["p"], np.float32)
    nc = build_program()
    in_maps = []
    for b in range(NCORES):
        m = dict(shared)
        m["x"] = np.ascontiguousarray(x[b])
        m["p"] = np.ascontiguousarray(p[:, b])
        in_maps.append(m)
    res = run_bass_kernel_spmd(nc, in_maps, core_ids=list(range(NCORES)))
    return np.stack([r["out"] for r in res.results], axis=0).astype(np.float32)
```

```python
import math
from contextlib import ExitStack
import numpy as np
import concourse.bass as bass
import concourse.mybir as mybir
from concourse.bass_utils import run_bass_kernel_spmd

F32 = mybir.dt.float32
BF16 = mybir.dt.bfloat16
AF = mybir.ActivationFunctionType
ALU = mybir.AluOpType

S = 4096
D = 1024
TT = 512
NT = S // TT
NSUB = TT // 128
EPS = 1e-6
NPIECE = 14
NCORES = 8


class _Eng:
    def __init__(self, name, handle, sem):
        self.name = name
        self.h = handle
        self.sem = sem
        self.count = 0
        self.waited = {}


class KB:
    def __init__(self, nc, stack):
        self.nc = nc
        self.stack = stack
        self.eng = {}
        for name, h in (("pe", nc.tensor), ("act", nc.scalar), ("dve", nc.vector),
                        ("pool", nc.gpsimd), ("sp", nc.sync)):
            sem = stack.enter_context(nc.semaphore("s_" + name))
            self.eng[name] = _Eng(name, h, sem)
        self.state = {}
        self.dsem = {}
        self.nwaits = 0
        self.ninst = 0

    def sb(self, name, shape, dt, stack=None):
        self._uid = getattr(self, "_uid", 0) + 1
        return (stack or self.stack).enter_context(self.nc.sbuf_tensor(f"{name}_{self._uid}", list(shape), dt))

    def ps(self, name, shape, dt=F32):
        return self.stack.enter_context(self.nc.psum_tensor(name, list(shape), dt))

    def dma_sem(self, name):
        if name not in self.dsem:
            sem = self.stack.enter_context(self.nc.semaphore("d_" + name))
            self.dsem[name] = [sem, 0]
        return self.dsem[name]

    def _st(self, k):
        s = self.state.get(k)
        if s is None:
            s = [None, []]
            self.state[k] = s
        return s

    def _wait(self, e, sem, val):
        key = id(sem)
        if e.waited.get(key, 0) >= val:
            return
        e.h.wait_ge(sem, val)
        e.waited[key] = val
        self.nwaits += 1

    def _deps(self, e, reads, writes):
        for k in reads:
            w = self._st(k)[0]
            if w is not None:
                self._wait(e, w[0], w[1])
        pe = e.name == "pe"
        for k in writes:
            s = self._st(k)
            if s[0] is not None and not (pe and s[0][0] is e.sem):
                self._wait(e, s[0][0], s[0][1])
            for (sem, val) in s[1]:
                if not (pe and sem is e.sem):
                    self._wait(e, sem, val)

    def _commit(self, tag, reads, writes):
        for k in reads:
            r = self._st(k)[1]
            for idx, (sem, val) in enumerate(r):
                if sem is tag[0]:
                    r[idx] = tag if tag[1] > val else (sem, val)
                    break
            else:
                r.append(tag)
        for k in writes:
            s = self._st(k)
            s[0] = tag
            s[1] = []

    def op(self, en, fn, reads=(), writes=(), sig=True):
        e = self.eng[en]
        self._deps(e, reads, writes)
        ins = fn(e.h)
        self.ninst += 1
        if sig:
            e.count += 1
            ins.then_inc(e.sem, 1)
            tag = (e.sem, e.count)
        else:
            tag = (e.sem, e.count + 1)
        self._commit(tag, reads, writes)
        return ins

    def dma(self, qn, out, in_, semname, reads=(), writes=(), **kw):
        e = self.eng[qn]
        ds = self.dma_sem(semname)
        if ds[1] > 0:
            self._wait(e, ds[0], ds[1])
        self._deps(e, reads, writes)
        ins = e.h.dma_start(out=out, in_=in_, **kw)
        ds[1] += 16
        ins.then_inc(ds[0], 16)
        self.ninst += 1
        self._commit((ds[0], ds[1]), reads, writes)
        return ins

    def barrier(self):
        snap_e = [(o.sem, o.count) for o in self.eng.values() if o.count]
        snap_d = [(sem, cnt) for (sem, cnt) in self.dsem.values() if cnt]
        for e in self.eng.values():
            for sem, cnt in snap_e:
                if sem is not e.sem:
                    self._wait(e, sem, cnt)
            for sem, cnt in snap_d:
                self._wait(e, sem, cnt)
        self.state = {}

    def finish(self, en="sp"):
        e = self.eng[en]
        for name, (sem, cnt) in self.dsem.items():
            if cnt:
                self._wait(e, sem, cnt)
        for o in self.eng.values():
            if o.count and o is not e:
                self._wait(e, o.sem, o.count)


def build_program(stop_at=None, dbg=False):
    nc = bass.Bass("TRN2", target_bir_lowering=False)

    def din(name, shape):
        return nc.dram_tensor(name, list(shape), F32, kind="ExternalInput").ap()

    x_d = din("x", [S, D])
    p_d = din("p", [2, S, 256])
    w_in_d = din("w_in", [2, D, 3584])
    w_glu_d = din("w_glu", [2, D, 1024])
    w_out_d = din("w_out", [2, D, D])
    w_pg_d = din("w_pg", [2, D, D])
    w_pp_d = din("w_pp", [2, 256, D])
    gmix_d = din("gmix", [2, 128, 8])
    gple_d = din("gple", [2, 128, 8])
    bglu_d = din("bglu", [2, 128, 8])
    dpad_d = din("dpad", [2, 128, 8])
    gq_d = din("gq", [2, 128, 1])
    gk_d = din("gk", [2, 128, 1])
    are_d = din("a_re", [2, 32, 64])
    aim_d = din("a_im", [2, 32, 64])
    ldt_d = din("logdt", [2, 32, 1])
    bre_d = din("b_re", [2, 32, 64, 16])
    bim_d = din("b_im", [2, 32, 64, 16])
    ccat_d = din("ccat", [2, 4, 128, 128])
    ccsw_d = din("ccatsw", [2, 4, 128, 128])
    out_d = nc.dram_tensor("out", [S, D], F32, kind="ExternalOutput").ap()
    wbf_d = nc.dram_tensor("wbf", [2, NPIECE, 128, 4096], BF16, kind="Internal").ap()
    yss_d = nc.dram_tensor("yss", [2, 4, 128, S], BF16, kind="Internal").ap()
    hmid_d = nc.dram_tensor("hmid", [S, D], F32, kind="Internal").ap()
    dbg_d = {}

    def dbg_out(name, shape):
        dbg_d[name] = nc.dram_tensor(name, list(shape), F32, kind="ExternalOutput").ap()
        return dbg_d[name]

    with ExitStack() as st:
        kb = KB(nc, st)
        op = kb.op

        pb = [kb.ps(f"pb{i}", [128, 512], F32) for i in range(8)]
        pbk = [f"pb{i}" for i in range(8)]
        identb = kb.sb("identb", [128, 128], BF16)
        identf = kb.sb("identf", [128, 128], F32)
        tri = kb.sb("tri", [128, 128], BF16)
        ones = kb.sb("ones", [128, 128], BF16)
        masklt = kb.sb("masklt", [128, 128], BF16)
        blockones = kb.sb("blockones", [128, 128], BF16)
        bdmask = kb.sb("bdmask", [128, 128], F32)
        NRING = 3
        ring = [kb.sb(f"ring{i}", [128, 4096], BF16) for i in range(NRING)]
        stage = [kb.sb(f"stage{i}", [128, 1024], F32) for i in range(2)]

        def mk_affine(t, key, pattern, cm, cmp_):
            op("pool", lambda e: e.memset(t[:], 1.0), writes=[key])
            op("pool", lambda e: e.affine_select(out=t[:], in_=t[:], pattern=pattern, compare_op=cmp_, fill=0.0,
                                                 base=0, channel_multiplier=cm), reads=[key], writes=[key])

        mk_affine(identb, "identb", [[-1, 128]], 1, ALU.is_equal)
        mk_affine(identf, "identf", [[-1, 128]], 1, ALU.is_equal)
        mk_affine(tri, "tri", [[-1, 128]], 1, ALU.is_ge)
        mk_affine(masklt, "masklt", [[1, 128]], -1, ALU.is_gt)
        op("pool", lambda e: e.memset(ones[:], 1.0), writes=["ones"])
        op("pool", lambda e: e.memset(blockones[:], 0.0), writes=["blockones"])
        for hh in range(2):
            op("pool", lambda e, hh=hh: e.memset(blockones[64 * hh:64 * hh + 64, 64 * hh:64 * hh + 64], 1.0),
               writes=["blockones"])
        op("pool", lambda e: e.memset(bdmask[:], 0.0), writes=["bdmask"])
        for gg in range(4):
            op("pool", lambda e, gg=gg: e.memset(bdmask[32 * gg:32 * gg + 32, 32 * gg:32 * gg + 32], 1.0),
               writes=["bdmask"])

        def piece_src(L, pid):
            if pid <= 6:
                return w_in_d[L, :, pid * 512:(pid + 1) * 512].rearrange("(k p) n -> p k n", p=128), 8, 512
            if pid <= 8:
                return w_glu_d[L, :, (pid - 7) * 512:(pid - 6) * 512].rearrange("(k p) n -> p k n", p=128), 8, 512
            if pid <= 10:
                return w_out_d[L, :, (pid - 9) * 512:(pid - 8) * 512].rearrange("(k p) n -> p k n", p=128), 8, 512
            if pid <= 12:
                return w_pg_d[L, :, (pid - 11) * 512:(pid - 10) * 512].rearrange("(k p) n -> p k n", p=128), 8, 512
            return w_pp_d[L].rearrange("(k p) n -> p k n", p=128), 2, 1024

        converted = set()
        ring_ctr = [0]

        class _Shift:
            def __init__(self, pf, off):
                self.pf, self.off = pf, off

            def get(self, idx, ahead=2):
                return self.pf.get(idx + self.off, ahead)

        class Pref:
            def __init__(self, L, pids, ceng="dve"):
                self.L = L
                self.pids = pids
                self.loaded = []
                self.ceng = ceng

            def _load(self, idx):
                L, pid = self.L, self.pids[idx]
                s = ring_ctr[0] % NRING
                ring_ctr[0] += 1
                key = f"ring{s}"
                src, nk, ncol = piece_src(L, pid)
                n = nk * ncol
                v3 = ring[s][:, 0:n].rearrange("p (k n) -> p k n", k=nk)
                if (L, pid) not in converted:
                    ck = max(1, 1024 // ncol)
                    nh = ck * ncol
                    for q_ in range(nk // ck):
                        hf = q_ % 2
                        sv = stage[hf][:, 0:nh].rearrange("p (k n) -> p k n", k=ck)
                        kb.dma("sp", sv, src[:, q_ * ck:(q_ + 1) * ck, :], f"ld_stage{hf}", writes=[f"stage{hf}"])
                        if self.ceng == "act":
                            op("act", lambda e: e.activation(out=ring[s][:, q_ * nh:(q_ + 1) * nh], in_=stage[hf][:, 0:nh],
                                                             func=AF.Copy), reads=[f"stage{hf}"], writes=[key])
                        else:
                            op("dve", lambda e: e.tensor_copy(out=ring[s][:, q_ * nh:(q_ + 1) * nh], in_=stage[hf][:, 0:nh]),
                               reads=[f"stage{hf}"], writes=[key])
                    kb.dma("sp", wbf_d[L, pid, :, 0:n], ring[s][:, 0:n], f"st_ring{s}",
                           reads=[key], writes=[f"wbf{L}_{pid}"])
                    converted.add((L, pid))
                else:
                    kb.dma("sp", ring[s][:, 0:n], wbf_d[L, pid, :, 0:n], f"ld_ring{s}",
                           reads=[f"wbf{L}_{pid}"], writes=[key])
                self.loaded.append((v3, key))

            def get(self, idx, ahead=2):
                while len(self.loaded) <= min(idx + ahead, len(self.pids) - 1):
                    self._load(len(self.loaded))
                return self.loaded[idx]

        def norm_phase(stk_bufs, src_rows, gain, tagp, seng):
            hsub, hnt, ss, lnv, rstd, sqj, hnT = stk_bufs
            for i in range(NSUB):
                hs = hsub[i % 2]
                hk = f"hsub{i % 2}"
                kb.dma("sp", hs[:], src_rows(i), f"ld_hsub{i % 2}", writes=[hk])
                op("act", lambda e: e.activation(out=sqj[:], in_=hs[:], func=AF.Square, accum_out=ss[:, i:i + 1]),
                   reads=[hk], writes=["sqj", f"ss{i}"])
                op("act", lambda e: e.activation(out=lnv[:, i:i + 1], in_=ss[:, i:i + 1], func=AF.Ln,
                                                 scale=1.0 / D, bias=EPS), reads=[f"ss{i}"], writes=[f"lnv{i}"])
                op("act", lambda e: e.activation(out=rstd[:, i:i + 1], in_=lnv[:, i:i + 1], func=AF.Exp, scale=-0.5),
                   reads=[f"lnv{i}"], writes=[f"rstd{i}"])
                ht = hnt[i % 2]
                tk = f"hnt{i % 2}"
                if seng == "act":
                    op("act", lambda e: e.activation(out=ht[:], in_=hs[:], func=AF.Copy, scale=rstd[:, i:i + 1]),
                       reads=[hk, f"rstd{i}"], writes=[tk])
                else:
                    op("dve", lambda e: e.tensor_scalar(out=ht[:], in0=hs[:], scalar1=rstd[:, i:i + 1], scalar2=None,
                                                        op0=ALU.mult), reads=[hk, f"rstd{i}"], writes=[tk])
                bk = i % 2
                pv = pb[bk][:].bitcast(BF16)
                for kc in range(8):
                    op("pe", lambda e, kc=kc: e.transpose(out=pv[:, kc * 128:(kc + 1) * 128],
                                                          in_=ht[:, kc * 128:(kc + 1) * 128], identity=identb[:]),
                       reads=[tk, "identb"], writes=[pbk[bk]], sig=(kc == 7))
                if seng == "act":
                    for kc in range(8):
                        op("act", lambda e: e.activation(out=hnT[:, kc, i * 128:(i + 1) * 128], in_=pv[:, kc * 128:(kc + 1) * 128],
                                                         func=AF.Copy, scale=gain[:, kc:kc + 1]),
                           reads=[pbk[bk], tagp], writes=["hnT"])
                else:
                    op("dve", lambda e: e.tensor_tensor(out=hnT[:, :, i * 128:(i + 1) * 128],
                                                        in0=pv.rearrange("p (k t) -> p k t", k=8),
                                                        in1=gain[:, :].unsqueeze(2).to_broadcast([128, 8, 128]),
                                                        op=ALU.mult),
                       reads=[pbk[bk], tagp], writes=["hnT"])

        def proj_fm(w3, wkey, col0, hnT, bank):
            for kc in range(8):
                op("pe", lambda e, kc=kc: e.matmul(pb[bank][:], lhsT=w3[:, kc, col0:col0 + 128], rhs=hnT[:, kc, :],
                                                   start=(kc == 0), stop=(kc == 7)),
                   reads=[wkey, "hnT"], writes=[pbk[bank]], sig=(kc == 7))

        for L in range(2):
            h_src = x_d if L == 0 else hmid_d
            h_dst = hmid_d if L == 0 else out_d

            with ExitStack() as lst:
                gmix = kb.sb(f"gmix{L}", [128, 8], F32, lst)
                gple = kb.sb(f"gple{L}", [128, 8], F32, lst)
                bglu = kb.sb(f"bglu{L}", [128, 8], F32, lst)
                dpad = kb.sb(f"dpad{L}", [128, 8], F32, lst)
                gq = kb.sb(f"gq{L}", [128, 1], F32, lst)
                gk = kb.sb(f"gk{L}", [128, 1], F32, lst)
                for t, d_, k_ in ((gmix, gmix_d, "gmix"), (gple, gple_d, "gple"), (bglu, bglu_d, "bglu"),
                                  (dpad, dpad_d, "dpad"), (gq, gq_d, "gq"), (gk, gk_d, "gk")):
                    kb.dma("sp", t[:], d_[L], "ld_small_" + k_, writes=[k_])
                op("dve", lambda e: e.tensor_scalar(out=gq[:], in0=gq[:], scalar1=0.125, scalar2=None, op0=ALU.mult),
                   reads=["gq"], writes=["gq"])

                with ExitStack() as s1:
                    T_tab = kb.sb("T_tab", [128, 8, 9, 128], BF16, s1)
                    WS_tab = kb.sb("WS_tab", [128, 8, 8, 128], BF16, s1)
                    WC_tab = kb.sb("WC_tab", [128, 32, 8, 32], BF16, s1)
                    AA1 = kb.sb("AA1", [128, 2, 32], F32, s1)
                    AA2 = kb.sb("AA2", [128, 2, 32], F32, s1)
                    Hh = kb.sb("Hh", [128, 2, 32, 65], F32, s1)
                    with ExitStack() as su:
                        def g32(name):
                            return kb.sb(name, [32, 64], F32, su)
                        are, aim, Lr, Li, mag, cc_, ss_, t1, t2, t3 = [g32(n) for n in
                                                                      ("are", "aim", "Lr", "Li", "mag", "cc_", "ss_", "t1", "t2", "t3")]
                        ldt = kb.sb("ldt", [32, 1], F32, su)
                        dtv = kb.sb("dtv", [32, 1], F32, su)
                        kb.dma("sp", are[:], are_d[L], "ld_are", writes=["are"])
                        kb.dma("sp", aim[:], aim_d[L], "ld_aim", writes=["aim"])
                        kb.dma("sp", ldt[:], ldt_d[L], "ld_ldt", writes=["ldt"])
                        op("act", lambda e: e.activation(out=dtv[:], in_=ldt[:], func=AF.Exp), reads=["ldt"], writes=["dtv"])

                        def dv(fn, reads, writes):
                            op("dve", fn, reads=reads, writes=writes)

                        def tt_(o, a, b, alu, ok, ak, bk_):
                            dv(lambda e: e.tensor_tensor(out=o, in0=a, in1=b, op=alu), [ak, bk_], [ok])

                        dv(lambda e: e.tensor_scalar(out=Lr[:], in0=are[:], scalar1=dtv[:, 0:1], scalar2=None, op0=ALU.mult),
                           ["are", "dtv"], ["Lr"])
                        dv(lambda e: e.tensor_scalar(out=Li[:], in0=aim[:], scalar1=dtv[:, 0:1], scalar2=None, op0=ALU.mult),
                           ["aim", "dtv"], ["Li"])
                        op("act", lambda e: e.activation(out=mag[:], in_=Lr[:], func=AF.Exp), reads=["Lr"], writes=["mag"])
                        op("act", lambda e: e.activation(out=ss_[:], in_=Li[:], func=AF.Sin, scale=1.0 / 32.0),
                           reads=["Li"], writes=["ss_"])
                        op("act", lambda e: e.activation(out=cc_[:], in_=Li[:], func=AF.Sin, scale=1.0 / 32.0,
                                                         bias=math.pi / 2), reads=["Li"], writes=["cc_"])
                        for _ in range(5):
                            tt_(t1[:], cc_[:], cc_[:], ALU.mult, "t1", "cc_", "cc_")
                            tt_(t2[:], ss_[:], ss_[:], ALU.mult, "t2", "ss_", "ss_")
                            tt_(t3[:], cc_[:], ss_[:], ALU.mult, "t3", "cc_", "ss_")
                            tt_(cc_[:], t1[:], t2[:], ALU.subtract, "cc_", "t1", "t2")
                            dv(lambda e: e.tensor_scalar(out=ss_[:], in0=t3[:], scalar1=2.0, scalar2=None, op0=ALU.mult),
                               ["t3"], ["ss_"])
                        Pre = kb.sb("Pre", [32, 9, 64], F32, su)
                        Pim = kb.sb("Pim", [32, 9, 64], F32, su)
                        tt_(Pre[:, 1, :], mag[:], cc_[:], ALU.mult, "P1r", "mag", "cc_")
                        tt_(Pim[:, 1, :], mag[:], ss_[:], ALU.mult, "P1i", "mag", "ss_")
                        tt_(t1[:], are[:], are[:], ALU.mult, "t1", "are", "are")
                        tt_(t2[:], aim[:], aim[:], ALU.mult, "t2", "aim", "aim")
                        tt_(t1[:], t1[:], t2[:], ALU.add, "t1", "t1", "t2")
                        dv(lambda e: e.reciprocal(out=t3[:], in_=t1[:]), ["t1"], ["t3"])
                        dv(lambda e: e.tensor_scalar(out=t1[:], in0=Pre[:, 1, :], scalar1=-1.0, scalar2=None, op0=ALU.add),
                           ["P1r"], ["t1"])
                        tt_(t2[:], t1[:], are[:], ALU.mult, "t2", "t1", "are")
                        tt_(mag[:], Pim[:, 1, :], aim[:], ALU.mult, "mag", "P1i", "aim")
                        tt_(t2[:], t2[:], mag[:], ALU.add, "t2", "t2", "mag")
                        tt_(Pre[:, 0, :], t2[:], t3[:], ALU.mult, "P0r", "t2", "t3")
                        tt_(t2[:], Pim[:, 1, :], are[:], ALU.mult, "t2", "P1i", "are")
                        tt_(mag[:], t1[:], aim[:], ALU.mult, "mag", "t1", "aim")
                        tt_(t2[:], t2[:], mag[:], ALU.subtract, "t2", "t2", "mag")
                        tt_(Pim[:, 0, :], t2[:], t3[:], ALU.mult, "P0i", "t2", "t3")
                        for k in range(1, 8):
                            a, b_ = f"P{k}r", f"P{k}i"
                            tt_(t1[:], Pre[:, k, :], Pre[:, 1, :], ALU.mult, "t1", a, "P1r")
                            tt_(t2[:], Pim[:, k, :], Pim[:, 1, :], ALU.mult, "t2", b_, "P1i")
                            tt_(Pre[:, k + 1, :], t1[:], t2[:], ALU.subtract, f"P{k + 1}r", "t1", "t2")
                            tt_(t1[:], Pre[:, k, :], Pim[:, 1, :], ALU.mult, "t1", a, "P1i")
                            tt_(t2[:], Pim[:, k, :], Pre[:, 1, :], ALU.mult, "t2", b_, "P1r")
                            tt_(Pim[:, k + 1, :], t1[:], t2[:], ALU.add, f"P{k + 1}i", "t1", "t2")
                        A12 = kb.sb("A12", [128, 9, 2, 32], F32, su)
                        cat = kb.sb("cat", [32, 2, 128], F32, su)
                        for k in range(9):
                            rk, ik = f"P{k}r", f"P{k}i"
                            dv(lambda e, k=k: e.tensor_copy(out=cat[:, 0, 0:64], in_=Pre[:, k, :]), [rk], ["cat"])
                            dv(lambda e, k=k: e.tensor_copy(out=cat[:, 0, 64:128], in_=Pre[:, k, :]), [rk], ["cat"])
                            dv(lambda e, k=k: e.tensor_scalar(out=cat[:, 1, 0:64], in0=Pim[:, k, :], scalar1=-1.0,
                                                              scalar2=None, op0=ALU.mult), [ik], ["cat"])
                            dv(lambda e, k=k: e.tensor_copy(out=cat[:, 1, 64:128], in_=Pim[:, k, :]), [ik], ["cat"])
                            for w_ in range(2):
                                op("pe", lambda e, w_=w_: e.transpose(out=pb[0][:, w_ * 32:(w_ + 1) * 32], in_=cat[:, w_, :],
                                                                      identity=identf[0:32, 0:32]),
                                   reads=["cat", "identf"], writes=["pb0"])
                            dv(lambda e, k=k: e.tensor_copy(out=A12[:, k, :, :],
                                                            in_=pb[0][:, 0:64].rearrange("p (a b) -> p a b", a=2)),
                               ["pb0"], [f"A12_{k}"])
                        for w_ in range(2):
                            dv(lambda e, w_=w_: e.tensor_copy(
                                out=AA1[:, w_, :].rearrange("p (a b) -> p a b", a=4),
                                in_=A12[:, 8, 0, :].rearrange("p (b a) -> p a b", a=4)), ["A12_8"], ["AA1"])
                        dv(lambda e: e.tensor_copy(out=AA2[:, 0, :].rearrange("p (a b) -> p a b", a=4),
                                                   in_=A12[:, 8, 1, :].rearrange("p (b a) -> p a b", a=4)), ["A12_8"], ["AA2"])
                        dv(lambda e: e.tensor_scalar(out=AA2[:, 1, :].rearrange("p (a b) -> p a b", a=4),
                                                     in0=A12[:, 8, 1, :].rearrange("p (b a) -> p a b", a=4),
                                                     scalar1=-1.0, scalar2=None, op0=ALU.mult), ["A12_8"], ["AA2"])
                        Braw = kb.sb("Braw", [128, 32, 16], F32, su)
                        Brsw = kb.sb("Brsw", [128, 32, 16], F32, su)
                        Bm = kb.sb("Bm", [128, 32, 16], F32, su)
                        Bsw = kb.sb("Bsw", [128, 32, 16], F32, su)
                        tmpA = kb.sb("tmpA", [128, 32, 16], F32, su)
                        tmpB = kb.sb("tmpB", [128, 32, 16], F32, su)
                        bre_v = bre_d[L].rearrange("g p h -> p g h")
                        bim_v = bim_d[L].rearrange("g p h -> p g h")
                        kb.dma("sp", Braw[0:64], bre_v, "ld_b0", writes=["Braw"])
                        kb.dma("sp", Braw[64:128], bim_v, "ld_b1", writes=["Braw"])
                        kb.dma("sp", Brsw[0:64], bim_v, "ld_b2", writes=["Brsw"])
                        kb.dma("sp", Brsw[64:128], bre_v, "ld_b3", writes=["Brsw"])

                        def bc(k, w_):
                            return A12[:, k, w_, :].unsqueeze(2).to_broadcast([128, 32, 16])

                        tt_(tmpA[:], Braw[:], bc(0, 0), ALU.mult, "tmpA", "Braw", "A12_0")
                        tt_(tmpB[:], Brsw[:], bc(0, 1), ALU.mult, "tmpB", "Brsw", "A12_0")
                        tt_(Bm[:], tmpA[:], tmpB[:], ALU.add, "Bm", "tmpA", "tmpB")
                        tt_(tmpA[:], Brsw[:], bc(0, 0), ALU.mult, "tmpA", "Brsw", "A12_0")
                        tt_(tmpB[:], Braw[:], bc(0, 1), ALU.mult, "tmpB", "Braw", "A12_0")
                        tt_(Bsw[:], tmpA[:], tmpB[:], ALU.subtract, "Bsw", "tmpA", "tmpB")
                        Cm = kb.sb("Cm", [128, 32, 16], F32, su)
                        Cmsw = kb.sb("Cmsw", [128, 32, 16], F32, su)
                        ccs = kb.sb("ccs", [128, 128], F32, su)
                        for (dst, dkey, srcd, neg_lo) in ((Cm, "Cm", ccat_d, False), (Cmsw, "Cmsw", ccsw_d, True)):
                            for c in range(4):
                                kb.dma("sp", ccs[:], srcd[L, c], "ld_ccs", writes=["ccs"])
                                op("pe", lambda e: e.transpose(out=pb[1][:, 0:128], in_=ccs[:], identity=identf[:]),
                                   reads=["ccs", "identf"], writes=["pb1"])
                                lo = pb[1][0:64, 0:128].rearrange("p (a b) -> p a b", a=8)
                                hi = pb[1][64:128, 0:128].rearrange("p (a b) -> p a b", a=8)
                                dlo = dst[0:64, 8 * c:8 * c + 8, :]
                                dhi = dst[64:128, 8 * c:8 * c + 8, :]
                                if neg_lo:
                                    dv(lambda e, lo=lo, dlo=dlo: e.tensor_scalar(out=dlo, in0=lo, scalar1=-1.0, scalar2=None,
                                                                                 op0=ALU.mult), ["pb1"], [dkey])
                                    dv(lambda e, hi=hi, dhi=dhi: e.tensor_copy(out=dhi, in_=hi), ["pb1"], [dkey])
                                else:
                                    dv(lambda e, lo=lo, dlo=dlo: e.tensor_copy(out=dlo, in_=lo), ["pb1"], [dkey])
                                    dv(lambda e, hi=hi, dhi=dhi: e.tensor_scalar(out=dhi, in0=hi, scalar1=-1.0, scalar2=None,
                                                                                 op0=ALU.mult), ["pb1"], [dkey])
                        T0f = kb.sb("T0f", [128, 8, 128], F32, su)
                        Cpad = kb.sb("Cpad", [128, 32, 32], BF16, su)
                        ABpad = kb.sb("ABpad", [128, 32, 32], BF16, su)
                        op("dve", lambda e: e.memset(Cpad[:], 0.0), writes=["Cpad"])
                        op("dve", lambda e: e.memset(ABpad[:], 0.0), writes=["ABpad"])
                        op("dve", lambda e: e.memset(WC_tab[:], 0.0), writes=["WC"])
                        dv(lambda e: e.tensor_copy(out=Cpad[:, :, 0:16], in_=Cm[:]), ["Cm"], ["Cpad"])
                        for tau in range(8):
                            if tau == 0:
                                dv(lambda e: e.tensor_copy(out=ABpad[:, :, 0:16], in_=Bm[:]), ["Bm"], ["ABpad"])
                            else:
                                tt_(tmpA[:], Bm[:], bc(tau, 0), ALU.mult, "tmpA", "Bm", f"A12_{tau}")
                                tt_(tmpB[:], Bsw[:], bc(tau, 1), ALU.mult, "tmpB", "Bsw", f"A12_{tau}")
                                tt_(ABpad[:, :, 0:16], tmpA[:], tmpB[:], ALU.add, "ABpad", "tmpA", "tmpB")
                            pv = pb[2][:].bitcast(BF16)
                            for pc in range(8):
                                op("pe", lambda e, pc=pc: e.transpose(
                                    out=pv[:, pc * 128:(pc + 1) * 128],
                                    in_=ABpad[:, 4 * pc:4 * pc + 4, :].rearrange("p a b -> p (a b)"), identity=identb[:]),
                                   reads=["ABpad", "identb"], writes=["pb2"])
                            dv(lambda e, tau=tau: e.tensor_copy(out=WS_tab[:, :, 7 - tau, :],
                                                                in_=pv.rearrange("p (a b) -> p a b", a=8)), ["pb2"], ["WS"])
                            for pc in range(8):
                                bk = 3 + pc // 4
                                op("pe", lambda e, pc=pc, bk=bk: e.matmul(
                                    pb[bk][:, (pc % 4) * 128:(pc % 4 + 1) * 128],
                                    lhsT=ABpad[:, 4 * pc:4 * pc + 4, :].rearrange("p a b -> p (a b)"),
                                    rhs=Cpad[:, 4 * pc:4 * pc + 4, :].rearrange("p a b -> p (a b)"), start=True, stop=True),
                                   reads=["ABpad", "Cpad"], writes=[pbk[bk]])
                            for hf in range(2):
                                if tau > 0:
                                    dv(lambda e, hf=hf, tau=tau: e.tensor_tensor(
                                        out=T_tab[:, 4 * hf:4 * hf + 4, tau, :],
                                        in0=pb[3 + hf][:].rearrange("p (a b) -> p a b", a=4),
                                        in1=bdmask[:].unsqueeze(1).to_broadcast([128, 4, 128]), op=ALU.mult),
                                       [pbk[3 + hf], "bdmask"], ["T"])
                                else:
                                    dv(lambda e, hf=hf: e.tensor_tensor(
                                        out=T0f[:, 4 * hf:4 * hf + 4, :],
                                        in0=pb[3 + hf][:].rearrange("p (a b) -> p a b", a=4),
                                        in1=bdmask[:].unsqueeze(1).to_broadcast([128, 4, 128]), op=ALU.mult),
                                       [pbk[3 + hf], "bdmask"], ["T0f"])
                            if tau == 0:
                                for pc in range(8):
                                    dv(lambda e, pc=pc: e.scalar_tensor_tensor(
                                        out=T0f[:, pc, :], in0=identf[:], scalar=dpad[:, pc:pc + 1], in1=T0f[:, pc, :],
                                        op0=ALU.mult, op1=ALU.add), ["identf", "dpad", "T0f"], ["T0f"])
                                dv(lambda e: e.tensor_copy(out=T_tab[:, :, 0, :], in_=T0f[:]), ["T0f"], ["T"])
                                dv(lambda e: e.tensor_tensor(out=T0f[:], in0=T0f[:], in1=T_tab[:, :, 0, :], op=ALU.subtract),
                                   ["T0f", "T"], ["T0f"])
                                dv(lambda e: e.tensor_copy(out=T_tab[:, :, 8, :], in_=T0f[:]), ["T0f"], ["T"])
                        for lp in range(8):
                            k = lp + 1
                            tt_(tmpA[:], Cm[:], bc(k, 0), ALU.mult, "tmpA", "Cm", f"A12_{k}")
                            tt_(tmpB[:], Cmsw[:], bc(k, 1), ALU.mult, "tmpB", "Cmsw", f"A12_{k}")
                            tt_(WC_tab[:, :, lp, 0:16], tmpA[:], tmpB[:], ALU.subtract, "WC", "tmpA", "tmpB")
                        op("dve", lambda e: e.memset(Hh[:, :, :, 0:1], 0.0), writes=["Hh"])
                        kb.barrier()
                    hsub = [kb.sb(f"hsub{i}", [128, D], F32, s1) for i in range(2)]
                    hnt = [kb.sb(f"hnt{i}", [128, D], BF16, s1) for i in range(2)]
                    ss = kb.sb("ssq", [128, 4], F32, s1)
                    lnv = kb.sb("lnv", [128, 4], F32, s1)
                    rstd = kb.sb("rstd", [128, 4], F32, s1)
                    sqj = kb.sb("sqj", [128, D], BF16, s1)
                    hnT = kb.sb("hnT", [128, 8, TT], BF16, s1)
                    nb = (hsub, hnt, ss, lnv, rstd, sqj, hnT)
                    uTp2 = [kb.sb(f"uTp{i}", [128, 8, TT], BF16, s1) for i in range(2)]
                    sgs2 = [kb.sb(f"sgs{i}", [128, 4, TT], BF16, s1) for i in range(2)]
                    Hprev2 = [kb.sb(f"Hprev{i}", [128, 32, 64], BF16, s1) for i in range(2)]
                    zTp = kb.sb("zTp", [128, 8, TT], BF16, s1)
                    ysT = kb.sb("ysT", [128, 4, TT], BF16, s1)
                    SS = kb.sb("SS", [128, 2, 32, 64], F32, s1)
                    rt1 = kb.sb("rt1", [128, 2, 32], F32, s1)
                    rt2 = kb.sb("rt2", [128, 2, 32], F32, s1)
                    Ysb = [kb.sb(f"Ysb{i}", [64, 8, 4, 32], BF16, s1) for i in range(2)]
                    sgt = [kb.sb(f"sgt{i}", [128, TT], F32, s1) for i in range(2)]
                    vat = [kb.sb(f"vat{i}", [128, TT], F32, s1) for i in range(2)]
                    hh_ap = Hh[:]
                    pstep = list(hh_ap.ap[0])

                    def stage_X1(j, pf):
                        uTp, sgs, Hprev = uTp2[j % 2], sgs2[j % 2], Hprev2[j % 2]
                        uk, sk, hk_ = f"uTp{j % 2}", f"sgs{j % 2}", f"Hprev{j % 2}"
                        norm_phase(nb, lambda i: h_src[j * TT + i * 128:j * TT + (i + 1) * 128, :], gmix, "gmix", "act")
                        for half in range(2):
                            w3, wk = pf.get(half)
                            for q4 in range(4):
                                pc = half * 4 + q4
                                bank = 2 + pc % 2
                                proj_fm(w3, wk, q4 * 128, hnT, bank)
                                op("act", lambda e: e.activation(out=uTp[:, pc, :].rearrange("p (l s) -> p l s", l=8),
                                                                 in_=pb[bank][:].rearrange("p (s l) -> p l s", l=8), func=AF.Copy),
                                   reads=[pbk[bank]], writes=[uk])
                        w3, wk = pf.get(2)
                        for c in range(4):
                            bank = 2 + c % 2
                            proj_fm(w3, wk, c * 128, hnT, bank)
                            op("act", lambda e: e.activation(out=sgs[:, c, :].rearrange("p (l s) -> p l s", l=8),
                                                             in_=pb[bank][:].rearrange("p (s l) -> p l s", l=8), func=AF.Silu),
                               reads=[pbk[bank]], writes=[sk])
                        for pc in range(8):
                            for l in range(8):
                                for gg in range(4):
                                    op("pe", lambda e: e.matmul(
                                        pb[4 + gg][:, pc * 64:(pc + 1) * 64],
                                        lhsT=WS_tab[32 * gg:32 * gg + 32, pc, l, :],
                                        rhs=uTp[32 * gg:32 * gg + 32, pc, l * 64:(l + 1) * 64],
                                        start=(l == 0), stop=(l == 7), tile_position=(32 * gg, 0)),
                                       reads=["WS", uk], writes=[pbk[4 + gg]], sig=(l == 7))

                    def stage_X2(j):
                        Hprev = Hprev2[j % 2]
                        hk_ = f"Hprev{j % 2}"
                        for gg in range(4):
                            op("dve", lambda e: e.tensor_copy(out=SS[:, 0, 8 * gg:8 * gg + 8, :].rearrange("p a b -> p (a b)"),
                                                              in_=pb[4 + gg][:]), reads=[pbk[4 + gg]], writes=["SS0"])
                        op("act", lambda e: e.activation(out=SS[0:64, 1, :, :], in_=SS[64:128, 0, :, :], func=AF.Copy),
                           reads=["SS0"], writes=["SS1"])
                        op("act", lambda e: e.activation(out=SS[64:128, 1, :, :], in_=SS[0:64, 0, :, :], func=AF.Copy),
                           reads=["SS0"], writes=["SS1"])
                        for sc in range(64):
                            cur = Hh[:, :, :, sc]
                            swp = bass.AP(hh_ap.tensor, hh_ap.offset + 32 * 65 + sc, [pstep, [-32 * 65, 2], [65, 32]])
                            op("dve", lambda e: e.tensor_tensor(out=rt1[:], in0=cur, in1=AA1[:], op=ALU.mult),
                               reads=["Hh", "AA1"], writes=["rt1"])
                            op("dve", lambda e: e.tensor_tensor(out=rt2[:], in0=swp, in1=AA2[:], op=ALU.mult),
                               reads=["Hh", "AA2"], writes=["rt2"])
                            op("dve", lambda e: e.tensor_tensor(out=rt1[:], in0=rt1[:], in1=rt2[:], op=ALU.add),
                               reads=["rt1", "rt2"], writes=["rt1"])
                            op("dve", lambda e: e.tensor_tensor(out=Hh[:, :, :, sc + 1], in0=rt1[:], in1=SS[:, :, :, sc],
                                                                op=ALU.add), reads=["rt1", "SS0", "SS1"], writes=["Hh"])
                        op("dve", lambda e: e.tensor_copy(out=Hprev[:], in_=Hh[:, 0, :, 0:64]), reads=["Hh"], writes=[hk_])
                        op("dve", lambda e: e.tensor_copy(out=Hh[:, :, :, 0:1], in_=Hh[:, :, :, 64:65]), reads=["Hh"], writes=["Hh"])

                    def stage_Y(j, pf, i0):
                        uTp, sgs, Hprev = uTp2[j % 2], sgs2[j % 2], Hprev2[j % 2]
                        uk, sk, hk_ = f"uTp{j % 2}", f"sgs{j % 2}", f"Hprev{j % 2}"

                        def yint(pc):
                            yb, ykey = Ysb[pc % 2], f"Ysb{pc % 2}"
                            for gg in range(4):
                                gi, gn, bank = gg * 8 + pc, 4 * pc + gg, gg // 2
                                op("pe", lambda e: e.matmul(
                                    pb[bank][0:64, (gg % 2) * 256:(gg % 2 + 1) * 256],
                                    lhsT=Hprev[:, gi, :], rhs=WC_tab[:, gn, :, :].rearrange("p a b -> p (a b)"),
                                    start=True, stop=True), reads=[hk_, "WC"], writes=[pbk[bank]])
                            for bank in range(2):
                                op("act", lambda e: e.activation(
                                    out=yb[:, :, 2 * bank:2 * bank + 2, :],
                                    in_=pb[bank][0:64, :].rearrange("p (g l h) -> p l g h", g=2, l=8),
                                    func=AF.Copy), reads=[pbk[bank]], writes=[ykey])

                        yint(0)
                        for pc in range(8):
                            if pc + 1 < 8:
                                yint(pc + 1)
                            yb, ykey = Ysb[pc % 2], f"Ysb{pc % 2}"
                            ybank = 2 + pc % 2
                            op("pe", lambda e: e.matmul(pb[ybank][:], lhsT=T_tab[:, pc, 0, :], rhs=uTp[:, pc, :], start=True, stop=False),
                               reads=["T", uk], writes=[pbk[ybank]], sig=False)
                            op("pe", lambda e: e.matmul(pb[ybank][:], lhsT=T_tab[:, pc, 8, :], rhs=uTp[:, pc, :], start=False, stop=False),
                               reads=["T", uk], writes=[pbk[ybank]], sig=False)
                            for tau in range(1, 8):
                                op("pe", lambda e: e.matmul(pb[ybank][:, tau * 64:TT], lhsT=T_tab[:, pc, tau, :],
                                                            rhs=uTp[:, pc, 0:(8 - tau) * 64], start=False, stop=False),
                                   reads=["T", uk], writes=[pbk[ybank]], sig=False)
                            for lp in range(8):
                                op("pe", lambda e: e.matmul(
                                    pb[ybank][:, lp * 64:(lp + 1) * 64], lhsT=yb[:, lp, :, :].rearrange("p g h -> p (g h)"),
                                    rhs=identb[0:64, 0:64], start=False, stop=(lp == 7)),
                                   reads=[ykey, "identb"], writes=[pbk[ybank]], sig=(lp == 7))
                            op("act", lambda e: e.activation(out=zTp[:, pc, :], in_=pb[ybank][:], func=AF.Gelu_apprx_tanh),
                               reads=[pbk[ybank]], writes=["zTp"])
                        wv3, wvk = pf.get(i0, ahead=1)
                        wg3, wgk = pf.get(i0 + 1, ahead=0)
                        for c in range(4):
                            bv_, bg_ = 4 + 2 * (c % 2), 5 + 2 * (c % 2)
                            for (w3, wk, bank) in ((wv3, wvk, bv_), (wg3, wgk, bg_)):
                                for kc in range(8):
                                    op("pe", lambda e: e.matmul(
                                        pb[bank][:], lhsT=w3[:, kc, c * 128:(c + 1) * 128], rhs=zTp[:, kc, :],
                                        start=(kc == 0), stop=(kc == 7)), reads=[wk, "zTp"], writes=[pbk[bank]], sig=(kc == 7))
                            sg_, va_ = sgt[c % 2], vat[c % 2]
                            op("act", lambda e: e.activation(out=sg_[:], in_=pb[bg_][:], func=AF.Sigmoid, bias=bglu[:, 4 + c:5 + c]),
                               reads=[pbk[bg_], "bglu"], writes=[f"sgt{c % 2}"])
                            op("act", lambda e: e.activation(out=va_[:], in_=pb[bv_][:], func=AF.Identity, bias=bglu[:, c:c + 1]),
                               reads=[pbk[bv_], "bglu"], writes=[f"vat{c % 2}"])
                            op("pool", lambda e: e.tensor_tensor(out=va_[:], in0=va_[:], in1=sg_[:], op=ALU.mult),
                               reads=[f"vat{c % 2}", f"sgt{c % 2}"], writes=[f"vat{c % 2}"])
                            op("pool", lambda e: e.tensor_tensor(out=ysT[:, c, :].rearrange("p (s l) -> p l s", l=8),
                                                                 in0=va_[:].rearrange("p (l s) -> p l s", l=8),
                                                                 in1=sgs[:, c, :].rearrange("p (l s) -> p l s", l=8), op=ALU.mult),
                               reads=[f"vat{c % 2}", sk], writes=["ysT"])
                        kb.dma("sp", yss_d[L, :, :, j * TT:(j + 1) * TT].rearrange("c p t -> p c t"), ysT[:], "st_yss",
                               reads=["ysT"])

                    stage_X1(0, Pref(L, [0, 1, 2], "act"))
                    for j in range(NT + 1):
                        pids = ([7, 8] if j >= 1 else []) + ([0, 1, 2] if j + 1 < NT else [])
                        pf = Pref(L, pids, "act")
                        if j < NT:
                            stage_X2(j)
                        if j >= 1:
                            stage_Y(j - 1, pf, 0)
                        if j + 1 < NT:
                            stage_X1(j + 1, _Shift(pf, 2 if j >= 1 else 0))
                    kb.barrier()
                if stop_at == f"s1_{L}":
                    dd = nc.dram_tensor("dbg_yss", [4, 128, S], BF16, kind="ExternalOutput").ap()
                    kb.dma("sp", dd, yss_d[L], "st_dbgyss")
                    break

                with ExitStack() as s2:
                    kT = kb.sb("kT", [128, 4, S], BF16, s2)
                    Vc = kb.sb("Vc", [128, S // 128, 512], BF16, s2)
                    htile = kb.sb("htile", [128, NSUB, D], F32, s2)
                    hsub = [htile[:, 0, :], htile[:, 1, :]]
                    hnt = [kb.sb(f"hnt{i}", [128, D], BF16, s2) for i in range(2)]
                    ss = kb.sb("ssq", [128, 4], F32, s2)
                    lnv = kb.sb("lnv", [128, 4], F32, s2)
                    rstd = kb.sb("rstd", [128, 4], F32, s2)
                    sqj = kb.sb("sqj", [128, D], BF16, s2)
                    hnT = kb.sb("hnT", [128, 8, TT], BF16, s2)
                    qT = kb.sb("qT", [128, 4, TT], BF16, s2)
                    sga = kb.sb("sga", [128, 4, TT], BF16, s2)
                    yT = kb.sb("yT", [128, 8, TT], BF16, s2)
                    sq = [kb.sb(f"sq{i}", [128, TT], BF16, s2) for i in range(2)]
                    sd = [kb.sb(f"sd{i}", [128, TT], F32, s2) for i in range(2)]
                    e_sb = [kb.sb(f"e_sb{i}", [128, TT], BF16, s2) for i in range(6)]
                    sp_sb = [kb.sb(f"sp_sb{i}", [128, TT], BF16, s2) for i in range(3)]
                    x_sb = [kb.sb(f"x_sb{i}", [128, TT], BF16, s2) for i in range(2)]
                    w_sb = [kb.sb(f"w_sb{i}", [128, TT], BF16, s2) for i in range(3)]
                    S_sb = [kb.sb(f"S_sb{i}", [128, TT], BF16, s2) for i in range(3)]
                    psub = [kb.sb(f"psub{i}", [128, 256], F32, s2) for i in range(2)]
                    pbf = [kb.sb(f"pbf{i}", [128, 256], BF16, s2) for i in range(2)]
                    pT = kb.sb("pT", [128, 2, TT], BF16, s2)
                    gsb = [kb.sb(f"gsb{i}", [128, TT], F32, s2) for i in range(2)]
                    gpp = [kb.sb(f"gpp{i}", [128, TT], F32, s2) for i in range(2)]

                    class _Sub:
                        pass

                    tctr = [0]
                    HT = ["htile", "hsub0", "hsub1"]
                    for j in range(NT):
                        pf = Pref(L, [3, 4, 5, 6, 9, 10, 11, 13, 12], "dve")
                        nb = (hsub, hnt, ss, lnv, rstd, sqj, hnT)
                        norm_phase(nb, lambda i: h_src[j * TT + i * 128:j * TT + (i + 1) * 128, :], gmix, "gmix", "dve")
                        for which in range(2):
                            w3, wk = pf.get(which)
                            gvec = gq if which == 0 else gk
                            gkey = "gq" if which == 0 else "gk"
                            for c in range(4):
                                bank = 2 + c % 2
                                proj_fm(w3, wk, c * 128, hnT, bank)
                                sq_, sd_ = sq[c % 2], sd[c % 2]
                                op("act", lambda e, sq_=sq_, bank=bank: e.activation(out=sq_[:], in_=pb[bank][:], func=AF.Square),
                                   reads=[pbk[bank]], writes=[f"sq{c % 2}"])
                                sbank = 4 + c % 2
                                op("pe", lambda e, sq_=sq_, sbank=sbank: e.matmul(pb[sbank][:], lhsT=blockones[:], rhs=sq_[:],
                                                                                   start=True, stop=True),
                                   reads=[f"sq{c % 2}", "blockones"], writes=[pbk[sbank]])
                                op("act", lambda e, sd_=sd_, sbank=sbank: e.activation(out=sd_[:], in_=pb[sbank][:], func=AF.Ln,
                                                                                      scale=1.0 / 64.0, bias=EPS),
                                   reads=[pbk[sbank]], writes=[f"sd{c % 2}"])
                                op("act", lambda e, sd_=sd_: e.activation(out=sd_[:], in_=sd_[:], func=AF.Exp, scale=-0.5),
                                   reads=[f"sd{c % 2}"], writes=[f"sd{c % 2}"])
                                dst = qT[:, c, :] if which == 0 else kT[:, c, j * TT:(j + 1) * TT]
                                op("dve", lambda e, dst=dst, bank=bank, sd_=sd_, gvec=gvec: e.scalar_tensor_tensor(
                                    out=dst, in0=pb[bank][:], scalar=gvec[:, 0:1], in1=sd_[:], op0=ALU.mult, op1=ALU.mult),
                                   reads=[pbk[bank], gkey, f"sd{c % 2}"], writes=["qT" if which == 0 else "kT"])
                        w3, wk = pf.get(2)
                        for i in range(NSUB):
                            bank = 2 + i % 2
                            for kc in range(8):
                                op("pe", lambda e, kc=kc, i=i, bank=bank: e.matmul(
                                    pb[bank][:], lhsT=hnT[:, kc, i * 128:(i + 1) * 128], rhs=w3[:, kc, :],
                                    start=(kc == 0), stop=(kc == 7)), reads=[wk, "hnT"], writes=[pbk[bank]], sig=(kc == 7))
                            op("dve", lambda e, i=i, bank=bank: e.tensor_copy(out=Vc[:, 4 * j + i, :], in_=pb[bank][:]),
                               reads=[pbk[bank]], writes=["Vc"])
                        w3, wk = pf.get(3)
                        for c in range(4):
                            bank = 2 + c % 2
                            proj_fm(w3, wk, c * 128, hnT, bank)
                            op("act", lambda e, c=c, bank=bank: e.activation(out=sga[:, c, :], in_=pb[bank][:], func=AF.Silu),
                               reads=[pbk[bank]], writes=["sga"])
                        kb.dma("sp", yT[:, 0:4, :], yss_d[L, :, :, j * TT:(j + 1) * TT].rearrange("c p t -> p c t"), "ld_yss",
                               writes=["yT"])
                        kb.dma("sp", htile[:], h_src[j * TT:(j + 1) * TT, :].rearrange("(i p) d -> p i d", p=128), "ld_htile", writes=HT)
                        tiles = []
                        for h in range(8):
                            nblk = 4 * j + 4
                            for kb_ in range(nblk - 1, -1, -1):
                                diag = kb_ >= 4 * j
                                tiles.append(dict(h=h, c=h // 2, base=64 * (h % 2), obank=6 + h % 2, kb=kb_, diag=diag,
                                                  qoff=(kb_ - 4 * j) * 128 if diag else 0,
                                                  first=(kb_ == nblk - 1), last=(kb_ == 0)))
                        s_idx = 0
                        for t_ in tiles:
                            if t_["first"]:
                                t_["s_in"], t_["s_off"] = None, None
                            else:
                                t_["s_in"] = s_idx
                                t_["s_off"] = (t_["qoff"] + 128) if t_["diag"] else 0
                            if not t_["last"]:
                                s_idx = (s_idx + 1) % 3
                                t_["s_out"] = s_idx
                            else:
                                t_["s_out"] = None
                        NTL = len(tiles)

                        def st_QK(i):
                            t_ = tiles[i]
                            g_ = tctr[0] + i
                            zb = g_ % 3
                            qo, c, base, kb_ = t_["qoff"], t_["c"], t_["base"], t_["kb"]
                            op("pe", lambda e: e.matmul(pb[zb][:, qo:TT], lhsT=kT[base:base + 64, c, kb_ * 128:(kb_ + 1) * 128],
                                                        rhs=qT[base:base + 64, c, qo:TT], start=True, stop=True),
                               reads=["kT", "qT"], writes=[pbk[zb]])

                        def st_A1(i):
                            t_ = tiles[i]
                            g_ = tctr[0] + i
                            zb, e_, ek = g_ % 3, e_sb[g_ % 6], f"e_sb{g_ % 6}"
                            qo = t_["qoff"]
                            op("act", lambda e: e.activation(out=e_[:, qo:TT], in_=pb[zb][:, qo:TT], func=AF.Exp),
                               reads=[pbk[zb]], writes=[ek])
                            if t_["diag"]:
                                op("pool", lambda e: e.tensor_tensor(out=e_[:, qo:qo + 128], in0=e_[:, qo:qo + 128],
                                                                     in1=masklt[:], op=ALU.mult), reads=[ek, "masklt"], writes=[ek])

                        def st_A2(i):
                            t_ = tiles[i]
                            g_ = tctr[0] + i
                            e_, ek = e_sb[g_ % 6], f"e_sb{g_ % 6}"
                            sp_, spk = sp_sb[g_ % 3], f"sp_sb{g_ % 3}"
                            qo = t_["qoff"]
                            op("act", lambda e: e.activation(out=sp_[:, qo:TT], in_=e_[:, qo:TT], func=AF.Ln, bias=1.0),
                               reads=[ek], writes=[spk])
                            if t_["s_out"] is not None:
                                sn_t, sn_k = S_sb[t_["s_out"]], f"S_sb{t_['s_out']}"
                                if t_["s_in"] is None:
                                    op("dve", lambda e: e.tensor_copy(out=sn_t[:, qo:TT], in_=sp_[:, qo:TT]), reads=[spk], writes=[sn_k])
                                else:
                                    sc_t, sc_k = S_sb[t_["s_in"]], f"S_sb{t_['s_in']}"
                                    a0 = 0
                                    if t_["diag"]:
                                        op("dve", lambda e: e.tensor_copy(out=sn_t[:, qo:qo + 128], in_=sp_[:, qo:qo + 128]),
                                           reads=[spk], writes=[sn_k])
                                        a0 = qo + 128
                                    op("dve", lambda e: e.tensor_tensor(out=sn_t[:, a0:TT], in0=sc_t[:, a0:TT], in1=sp_[:, a0:TT],
                                                                        op=ALU.add), reads=[sc_k, spk], writes=[sn_k])

                        def st_B(i):
                            t_ = tiles[i]
                            g_ = tctr[0] + i
                            cb = 3 + g_ % 2
                            sp_, spk = sp_sb[g_ % 3], f"sp_sb{g_ % 3}"
                            qo = t_["qoff"]
                            has_s = t_["s_in"] is not None and t_["s_off"] < TT
                            op("pe", lambda e: e.matmul(pb[cb][:, qo:TT], lhsT=tri[:], rhs=sp_[:, qo:TT], start=True, stop=not has_s),
                               reads=["tri", spk], writes=[pbk[cb]])
                            if has_s:
                                sc_t, sc_k, so = S_sb[t_["s_in"]], f"S_sb{t_['s_in']}", t_["s_off"]
                                op("pe", lambda e: e.matmul(pb[cb][:, so:TT], lhsT=ones[:], rhs=sc_t[:, so:TT], start=False, stop=True),
                                   reads=["ones", sc_k], writes=[pbk[cb]])

                        def st_C1(i):
                            t_ = tiles[i]
                            g_ = tctr[0] + i
                            cb = 3 + g_ % 2
                            e_, ek = e_sb[g_ % 6], f"e_sb{g_ % 6}"
                            x_, xk = x_sb[g_ % 2], f"x_sb{g_ % 2}"
                            w_, wk_ = w_sb[g_ % 3], f"w_sb{g_ % 3}"
                            qo = t_["qoff"]
                            op("act", lambda e: e.activation(out=x_[:, qo:TT], in_=pb[cb][:, qo:TT], func=AF.Exp, scale=-1.0),
                               reads=[pbk[cb]], writes=[xk])
                            op("dve", lambda e: e.tensor_tensor(out=w_[:, qo:TT], in0=e_[:, qo:TT], in1=x_[:, qo:TT], op=ALU.mult),
                               reads=[ek, xk], writes=[wk_])

                        def st_C2(i):
                            t_ = tiles[i]
                            g_ = tctr[0] + i
                            w_, wk_ = w_sb[g_ % 3], f"w_sb{g_ % 3}"
                            qo, h, c, base, ob = t_["qoff"], t_["h"], t_["c"], t_["base"], t_["obank"]
                            op("pe", lambda e: e.matmul(pb[ob][base:base + 64, qo:TT], lhsT=Vc[:, t_["kb"], h * 64:(h + 1) * 64],
                                                        rhs=w_[:, qo:TT], start=t_["first"], stop=t_["last"], skip_group_check=True),
                               reads=["Vc", wk_], writes=[pbk[ob]])
                            if t_["last"]:
                                op("dve", lambda e: e.tensor_tensor(out=yT[base:base + 64, 4 + c, :], in0=pb[ob][base:base + 64, :],
                                                                    in1=sga[base:base + 64, c, :], op=ALU.mult),
                                   reads=[pbk[ob], "sga"], writes=["yT"])

                        for i in range(-1, NTL + 4):
                            if 0 <= i + 1 < NTL:
                                st_QK(i + 1)
                            if 0 <= i < NTL:
                                st_A1(i)
                            if 0 <= i - 1 < NTL:
                                st_A2(i - 1)
                            if 0 <= i - 2 < NTL:
                                st_B(i - 2)
                            if 0 <= i - 4 < NTL:
                                st_C2(i - 4)
                            if 0 <= i - 3 < NTL:
                                st_C1(i - 3)
                        tctr[0] += NTL
                        for half in range(2):
                            w3, wk = pf.get(4 + half)
                            for i in range(NSUB):
                                bank = 4 + i % 2
                                for kc in range(8):
                                    op("pe", lambda e, kc=kc, i=i, bank=bank, w3=w3: e.matmul(
                                        pb[bank][:], lhsT=yT[:, kc, i * 128:(i + 1) * 128], rhs=w3[:, kc, :],
                                        start=(kc == 0), stop=(kc == 7)), reads=[wk, "yT"], writes=[pbk[bank]], sig=(kc == 7))
                                op("dve", lambda e, i=i, bank=bank, half=half: e.tensor_tensor(
                                    out=htile[:, i, half * 512:(half + 1) * 512], in0=htile[:, i, half * 512:(half + 1) * 512],
                                    in1=pb[bank][:], op=ALU.add), reads=[pbk[bank]] + HT, writes=HT)
                        if dbg and j == 0 and L == 0:
                            kb.dma("sp", dbg_out("dbg_h1", [128, NSUB, D]), htile[:], "st_dbg1", reads=HT)
                        for i in range(NSUB):
                            hs = htile[:, i, :]
                            op("act", lambda e, hs=hs, i=i: e.activation(out=sqj[:], in_=hs, func=AF.Square, accum_out=ss[:, i:i + 1]),
                               reads=HT, writes=["sqj", f"ss{i}"])
                            op("act", lambda e, i=i: e.activation(out=lnv[:, i:i + 1], in_=ss[:, i:i + 1], func=AF.Ln,
                                                                  scale=1.0 / D, bias=EPS), reads=[f"ss{i}"], writes=[f"lnv{i}"])
                            op("act", lambda e, i=i: e.activation(out=rstd[:, i:i + 1], in_=lnv[:, i:i + 1], func=AF.Exp, scale=-0.5),
                               reads=[f"lnv{i}"], writes=[f"rstd{i}"])
                            ht = hnt[i % 2]
                            tk = f"hnt{i % 2}"
                            op("dve", lambda e, hs=hs, ht=ht, i=i: e.tensor_scalar(out=ht[:], in0=hs, scalar1=rstd[:, i:i + 1],
                                                                                    scalar2=None, op0=ALU.mult),
                               reads=HT + [f"rstd{i}"], writes=[tk])
                            bk = i % 2
                            pv = pb[bk][:].bitcast(BF16)
                            for kc in range(8):
                                op("pe", lambda e, kc=kc, ht=ht, pv=pv: e.transpose(out=pv[:, kc * 128:(kc + 1) * 128],
                                                                                   in_=ht[:, kc * 128:(kc + 1) * 128], identity=identb[:]),
                                   reads=[tk, "identb"], writes=[pbk[bk]], sig=(kc == 7))
                            op("dve", lambda e, i=i, pv=pv: e.tensor_tensor(out=hnT[:, :, i * 128:(i + 1) * 128],
                                                                            in0=pv.rearrange("p (k t) -> p k t", k=8),
                                                                            in1=gple[:, :].unsqueeze(2).to_broadcast([128, 8, 128]),
                                                                            op=ALU.mult), reads=[pbk[bk], "gple"], writes=["hnT"])
                            ps_, pb_ = psub[i % 2], pbf[i % 2]
                            kb.dma("sp", ps_[:], p_d[L, j * TT + i * 128:j * TT + (i + 1) * 128, :], f"ld_psub{i % 2}",
                                   writes=[f"psub{i % 2}"])
                            op("dve", lambda e, ps_=ps_, pb_=pb_: e.tensor_copy(out=pb_[:], in_=ps_[:]),
                               reads=[f"psub{i % 2}"], writes=[f"pbf{i % 2}"])
                            pv2 = pb[2 + i % 2][:].bitcast(BF16)
                            for k2 in range(2):
                                op("pe", lambda e, k2=k2, pb_=pb_, pv2=pv2: e.transpose(out=pv2[:, k2 * 128:(k2 + 1) * 128],
                                                                                       in_=pb_[:, k2 * 128:(k2 + 1) * 128],
                                                                                       identity=identb[:]),
                                   reads=[f"pbf{i % 2}", "identb"], writes=[pbk[2 + i % 2]], sig=(k2 == 1))
                            op("dve", lambda e, i=i, pv2=pv2: e.tensor_copy(out=pT[:, :, i * 128:(i + 1) * 128],
                                                                            in_=pv2[:, 0:256].rearrange("p (k t) -> p k t", k=2)),
                               reads=[pbk[2 + i % 2]], writes=["pT"])
                        wpp3, wppk = None, None
                        for half in range(2):
                            w3, wk = pf.get(6 if half == 0 else 8)
                            if wpp3 is None:
                                wpp3, wppk = pf.get(7)
                            for i in range(NSUB):
                                gb, pbk_ = 4 + i % 2, 6 + i % 2
                                for kc in range(8):
                                    op("pe", lambda e, kc=kc, i=i, gb=gb, w3=w3: e.matmul(
                                        pb[gb][:], lhsT=hnT[:, kc, i * 128:(i + 1) * 128], rhs=w3[:, kc, :],
                                        start=(kc == 0), stop=(kc == 7)), reads=[wk, "hnT"], writes=[pbk[gb]], sig=(kc == 7))
                                for k2 in range(2):
                                    op("pe", lambda e, k2=k2, i=i, pbk_=pbk_, half=half: e.matmul(
                                        pb[pbk_][:], lhsT=pT[:, k2, i * 128:(i + 1) * 128],
                                        rhs=wpp3[:, k2, half * 512:(half + 1) * 512], start=(k2 == 0), stop=(k2 == 1)),
                                       reads=[wppk, "pT"], writes=[pbk[pbk_]], sig=(k2 == 1))
                                g_, gp_ = gsb[i % 2], gpp[i % 2]
                                op("act", lambda e, g_=g_, gb=gb: e.activation(out=g_[:], in_=pb[gb][:], func=AF.Sigmoid),
                                   reads=[pbk[gb]], writes=[f"gsb{i % 2}"])
                                op("dve", lambda e, g_=g_, gp_=gp_, pbk_=pbk_: e.tensor_tensor(out=gp_[:], in0=g_[:], in1=pb[pbk_][:],
                                                                                               op=ALU.mult),
                                   reads=[f"gsb{i % 2}", pbk[pbk_]], writes=[f"gpp{i % 2}"])
                                op("dve", lambda e, i=i, half=half, gp_=gp_: e.tensor_tensor(
                                    out=htile[:, i, half * 512:(half + 1) * 512], in0=htile[:, i, half * 512:(half + 1) * 512],
                                    in1=gp_[:], op=ALU.add), reads=[f"gpp{i % 2}"] + HT, writes=HT)
                        kb.dma("sp", h_dst[j * TT:(j + 1) * TT, :].rearrange("(i p) d -> p i d", p=128), htile[:], "st_h", reads=HT)
                    kb.barrier()
                if stop_at == f"s2_{L}":
                    dd = nc.dram_tensor("dbg_h", [S, D], F32, kind="ExternalOutput").ap()
                    kb.dma("sp", dd, h_dst, "st_dbgh")
                    break
        kb.finish("sp")
        build_program.stats = (kb.ninst, kb.nwaits)
    return nc


def _prep_shared(inp):
    f = np.float32
    w_in = np.asarray(inp["w_in"], f)
    u_cols = w_in[:, :, 0:512].reshape(2, D, 32, 16)
    u_pad = np.zeros((2, D, 32, 32), f)
    u_pad[:, :, :, 0:16] = u_cols
    w_in_p = np.concatenate([u_pad.reshape(2, D, 1024), w_in[:, :, 512:]], axis=2)
    w_glu = np.asarray(inp["ssm_w_glu"], f).reshape(2, 32, 16, 1024)
    w_glu_p = np.zeros((2, 32, 32, 1024), f)
    w_glu_p[:, :, 0:16, :] = w_glu
    w_glu_p = w_glu_p.reshape(2, 1024, 1024)

    def colT(v):
        return np.ascontiguousarray(np.asarray(v, f).reshape(2, 8, 128).transpose(0, 2, 1))

    d = np.asarray(inp["ssm_d"], f)
    d_pad = np.zeros((2, 32, 32), f)
    d_pad[:, :, 0:16] = d
    dpad = np.ascontiguousarray(d_pad.reshape(2, 8, 128).transpose(0, 2, 1))
    gq = np.tile(np.asarray(inp["q_norm_g"], f), (1, 2)).reshape(2, 128, 1)
    gk = np.tile(np.asarray(inp["k_norm_g"], f), (1, 2)).reshape(2, 128, 1)
    cre = np.asarray(inp["ssm_c_re"], f).reshape(2, 4, 128, 64)
    cim = np.asarray(inp["ssm_c_im"], f).reshape(2, 4, 128, 64)
    return {
        "w_in": np.ascontiguousarray(w_in_p), "w_glu": w_glu_p,
        "w_out": np.asarray(inp["w_out"], f), "w_pg": np.asarray(inp["w_ple_gate"], f),
        "w_pp": np.asarray(inp["w_ple_proj"], f),
        "gmix": colT(inp["mix_norm_g"]), "gple": colT(inp["ple_norm_g"]), "bglu": colT(inp["ssm_b_glu"]),
        "dpad": dpad, "gq": np.ascontiguousarray(gq), "gk": np.ascontiguousarray(gk),
        "a_re": np.asarray(inp["ssm_a_re"], f), "a_im": np.asarray(inp["ssm_a_im"], f),
        "logdt": np.asarray(inp["ssm_log_dt"], f).reshape(2, 32, 1),
        "b_re": np.asarray(inp["ssm_b_re"], f), "b_im": np.asarray(inp["ssm_b_im"], f),
        "ccat": np.ascontiguousarray(np.concatenate([cre, cim], axis=3)),
        "ccatsw": np.ascontiguousarray(np.concatenate([cim, cre], axis=3)),
    }


def kernel(**inputs):
    shared = _prep_shared(inputs)
    x = np.asarray(inputs["x"], np.float32)
    p = np.asarray(inputs["p"], np.float32)
    nc = build_program()
    in_maps = []
    for b in range(NCORES):
        m = dict(shared)
        m["x"] = np.ascontiguousarray(x[b])
        m["p"] = np.ascontiguousarray(p[:, b])
        in_maps.append(m)
    res = run_bass_kernel_spmd(nc, in_maps, core_ids=list(range(NCORES)))
    return np.stack([r["out"] for r in res.results], axis=0).astype(np.float32)
```
